# Optimizing a Trainium2 kernel written in Bass

```python
import jax, jax.numpy as jnp
from jax import lax
import numpy as np

D_MODEL = 1024
BATCH = 2
SEQ = 16384
DEPTH = 2
DEC_BATCH = 8
DEC_SEQ = 2048
PAST_LEN = 128

D_PLE = 256
CHUNK = 64
N_DIR = 2
EPS = 1e-6
GLA_HEADS = 4
GLA_KEY = D_MODEL // 2
GLA_VAL = D_MODEL
GLA_DK = GLA_KEY // GLA_HEADS
GLA_DV = GLA_VAL // GLA_HEADS
GLA_RANK = 16
GLA_TAU = 16.0
SSD_WIDTH = D_MODEL
SSD_HEADDIM = 64
SSD_HEADS = SSD_WIDTH // SSD_HEADDIM
SSD_GROUPS = 4
SSD_HPG = SSD_HEADS // SSD_GROUPS
SSD_STATE = 128
SSD_CONV = 5
SSD_XBC = SSD_WIDTH + 2 * SSD_GROUPS * SSD_STATE
MIX_WIDTH = GLA_VAL + SSD_WIDTH
SPLIT_SIZES = (GLA_KEY, GLA_KEY, GLA_VAL, GLA_VAL, N_DIR * GLA_RANK,
               SSD_WIDTH, SSD_XBC, N_DIR * SSD_HEADS)
IN_WIDTH = (2 * GLA_KEY + 2 * GLA_VAL + N_DIR * GLA_RANK
            + SSD_WIDTH + SSD_XBC + N_DIR * SSD_HEADS)

kernel_name = "hybrid_gla_ssd_parallel_encoder"


def _rmsnorm(x, g):
    xf = x.astype(jnp.float32)
    r = lax.rsqrt(jnp.mean(xf * xf, axis=-1, keepdims=True) + EPS)
    return (xf * r * g.astype(jnp.float32)).astype(x.dtype)


def _flip(a):
    return jnp.flip(a, axis=1)


def _chunks(a, nc):
    return jnp.moveaxis(a.reshape((a.shape[0], nc, CHUNK) + a.shape[2:]), 1, 0)


def _unchunks(a):
    a = jnp.moveaxis(a, 0, 1)
    return a.reshape((a.shape[0], a.shape[1] * a.shape[2]) + a.shape[3:])


def _chunk_mask(strict):
    return jnp.tril(jnp.ones((CHUNK, CHUNK), dtype=bool), k=-1 if strict else 0)


def _gla_direction(q, k, v, la, strict):
    bt, L = q.shape[:2]
    nc = L // CHUNK
    mask = _chunk_mask(strict)[None, :, :, None, None]

    def step(S, inp):
        qc, kc, vc, lc = inp
        cum = jnp.cumsum(lc, axis=1)
        seg = jnp.where(mask, cum[:, :, None] - cum[:, None], -jnp.inf)
        att = jnp.einsum('bihk,bjhk,bijhk->bhij', qc, kc, jnp.exp(seg))
        y = jnp.einsum('bhij,bjhv->bihv', att, vc)
        y = y + jnp.einsum('bihk,bhkv->bihv', qc * jnp.exp(cum), S)
        tail = jnp.exp(cum[:, -1:] - cum)
        S = S * jnp.exp(cum[:, -1])[..., None] + jnp.einsum('bjhk,bjhv->bhkv', kc * tail, vc)
        return S, y

    S0 = jnp.zeros((bt, GLA_HEADS, GLA_DK, GLA_DV), jnp.float32)
    _, ys = lax.scan(step, S0, (_chunks(q, nc), _chunks(k, nc), _chunks(v, nc), _chunks(la, nc)))
    return _unchunks(ys)


def _ssd_direction(x, dt, A, bm, cm, strict):
    bt, L = x.shape[:2]
    nc = L // CHUNK
    mask = _chunk_mask(strict)[None, :, :, None, None]

    def step(hs, inp):
        xc, dtc, bc, cc = inp
        cum = jnp.cumsum(dtc * A, axis=1)
        seg = jnp.where(mask, cum[:, :, None] - cum[:, None], -jnp.inf)
        cb = jnp.einsum('bign,bjgn->bijg', cc, bc)
        w = cb[..., None] * jnp.exp(seg) * dtc[:, None]
        y = jnp.einsum('bijgr,bjgrp->bigrp', w, xc)
        y = y + jnp.einsum('bign,bgrpn->bigrp', cc, hs) * jnp.exp(cum)[..., None]
        tail = jnp.exp(cum[:, -1:] - cum) * dtc
        hs = hs * jnp.exp(cum[:, -1])[..., None, None] + jnp.einsum('bjgr,bjgn,bjgrp->bgrpn', tail, bc, xc)
        return hs, y

    h0 = jnp.zeros((bt, SSD_GROUPS, SSD_HPG, SSD_HEADDIM, SSD_STATE), jnp.float32)
    _, ys = lax.scan(step, h0, (_chunks(x, nc), _chunks(dt, nc), _chunks(bm, nc), _chunks(cm, nc)))
    return _unchunks(ys)


def _centred_dwconv(x, w):
    pad = (SSD_CONV - 1) // 2
    return lax.conv_general_dilated(
        x, w[:, None, :].astype(x.dtype), window_strides=(1,), padding=[(pad, pad)],
        dimension_numbers=('NWC', 'WIO', 'NWC'), feature_group_count=x.shape[-1])


def _layer(h, p, norm_g, w_in, w_gla_gate, b_gla_gate, gla_onorm_g, conv_w, conv_b,
           dt_bias, a_log, d_skip, ssd_norm_g, w_out, w_ple_gate, w_ple_proj, ple_norm_g):
    f32 = jnp.float32
    bt, L, _ = h.shape
    u = _rmsnorm(h, norm_g)
    proj = u @ w_in
    pieces = []
    off = 0
    for size in SPLIT_SIZES:
        pieces.append(proj[..., off:off + size])
        off += size
    q, k, v, z_gla, lr, z_ssd, xbc, dt_raw = pieces

    q = (q.astype(f32) * GLA_DK ** -0.5).reshape(bt, L, GLA_HEADS, GLA_DK)
    k = k.astype(f32).reshape(bt, L, GLA_HEADS, GLA_DK)
    v = v.astype(f32).reshape(bt, L, GLA_HEADS, GLA_DV)
    lr = lr.astype(f32).reshape(bt, L, N_DIR, GLA_RANK)
    gate_logits = jnp.einsum('bldr,drk->bldk', lr, w_gla_gate.astype(f32)) + b_gla_gate.astype(f32)
    la = (jax.nn.log_sigmoid(gate_logits) / GLA_TAU).reshape(bt, L, N_DIR, GLA_HEADS, GLA_DK)
    o = (_gla_direction(q, k, v, la[:, :, 0], False)
         + _flip(_gla_direction(_flip(q), _flip(k), _flip(v), _flip(la[:, :, 1]), True)))
    o = _rmsnorm(o, gla_onorm_g).reshape(bt, L, GLA_VAL)
    o = o * jax.nn.silu(z_gla.astype(f32))

    xbc = jax.nn.silu(_centred_dwconv(xbc, conv_w) + conv_b).astype(f32)
    gn = SSD_GROUPS * SSD_STATE
    xs = xbc[..., :SSD_WIDTH].reshape(bt, L, SSD_GROUPS, SSD_HPG, SSD_HEADDIM)
    bm = xbc[..., SSD_WIDTH:SSD_WIDTH + gn].reshape(bt, L, SSD_GROUPS, SSD_STATE)
    cm = xbc[..., SSD_WIDTH + gn:].reshape(bt, L, SSD_GROUPS, SSD_STATE)
    dt = jax.nn.softplus(dt_raw.astype(f32).reshape(bt, L, N_DIR, SSD_HEADS) + dt_bias.astype(f32))
    dt = dt.reshape(bt, L, N_DIR, SSD_GROUPS, SSD_HPG)
    A = -jnp.exp(a_log.astype(f32)).reshape(N_DIR, SSD_GROUPS, SSD_HPG)
    y = (_ssd_direction(xs, dt[:, :, 0], A[0], bm, cm, False)
         + _flip(_ssd_direction(_flip(xs), _flip(dt[:, :, 1]), A[1], _flip(bm), _flip(cm), True)))
    y = y + d_skip.astype(f32).reshape(SSD_GROUPS, SSD_HPG)[..., None] * xs
    y = y.reshape(bt, L, SSD_WIDTH) * jax.nn.silu(z_ssd.astype(f32))
    y = _rmsnorm(y, ssd_norm_g)

    mix = jnp.concatenate([o, y], axis=-1).astype(h.dtype)
    h = h + mix @ w_out

    e = _rmsnorm(p @ w_ple_proj, ple_norm_g)
    h = h + jax.nn.sigmoid(h @ w_ple_gate) * e
    return h


def setup_inputs(seed: int = 0) -> dict:
    key = jax.random.key(seed)
    ks = jax.random.split(key, 24)
    nrm = jax.random.normal
    f32 = jnp.float32
    dt0 = jnp.exp(jax.random.uniform(ks[12], (DEPTH, N_DIR, SSD_HEADS), f32,
                                     float(np.log(1e-3)), float(np.log(1e-1))))
    return {
        "x_prompt": nrm(ks[0], (BATCH, SEQ, D_MODEL), f32),
        "x_sample": nrm(ks[1], (DEC_BATCH, DEC_SEQ, D_MODEL), f32),
        "p_prompt": nrm(ks[2], (DEPTH, BATCH, SEQ, D_PLE), f32),
        "p_sample": nrm(ks[3], (DEPTH, DEC_BATCH, DEC_SEQ, D_PLE), f32),
        "norm_g": 1.0 + 0.02 * nrm(ks[4], (DEPTH, D_MODEL), f32),
        "w_in": nrm(ks[5], (DEPTH, D_MODEL, IN_WIDTH), f32) * D_MODEL ** -0.5,
        "w_gla_gate": nrm(ks[6], (DEPTH, N_DIR, GLA_RANK, GLA_KEY), f32) * GLA_RANK ** -0.5,
        "b_gla_gate": 0.1 * nrm(ks[7], (DEPTH, N_DIR, GLA_KEY), f32),
        "gla_onorm_g": 1.0 + 0.02 * nrm(ks[8], (DEPTH, GLA_DV), f32),
        "conv_w": nrm(ks[9], (DEPTH, SSD_CONV, SSD_XBC), f32) * SSD_CONV ** -0.5,
        "conv_b": 0.02 * nrm(ks[10], (DEPTH, SSD_XBC), f32),
        "dt_bias": dt0 + jnp.log(-jnp.expm1(-dt0)),
        "a_log": jnp.log(jax.random.uniform(ks[11], (DEPTH, N_DIR, SSD_HEADS), f32, 1.0, 16.0)),
        "d_skip": 1.0 + 0.1 * nrm(ks[13], (DEPTH, SSD_HEADS), f32),
        "ssd_norm_g": 1.0 + 0.02 * nrm(ks[14], (DEPTH, SSD_WIDTH), f32),
        "w_out": nrm(ks[15], (DEPTH, MIX_WIDTH, D_MODEL), f32) * MIX_WIDTH ** -0.5,
        "w_ple_gate": nrm(ks[16], (DEPTH, D_MODEL, D_MODEL), f32) * D_MODEL ** -0.5,
        "w_ple_proj": nrm(ks[17], (DEPTH, D_PLE, D_MODEL), f32) * D_PLE ** -0.5,
        "ple_norm_g": 1.0 + 0.02 * nrm(ks[18], (DEPTH, D_MODEL), f32),
        "final_norm_g": 1.0 + 0.02 * nrm(ks[19], (D_MODEL,), f32),
    }


def reference(x_prompt, x_sample, p_prompt, p_sample, norm_g, w_in, w_gla_gate, b_gla_gate,
              gla_onorm_g, conv_w, conv_b, dt_bias, a_log, d_skip, ssd_norm_g, w_out,
              w_ple_gate, w_ple_proj, ple_norm_g, final_norm_g):
    def trunk(h, p):
        for i in range(DEPTH):
            h = _layer(h, p[i], norm_g[i], w_in[i], w_gla_gate[i], b_gla_gate[i], gla_onorm_g[i],
                       conv_w[i], conv_b[i], dt_bias[i], a_log[i], d_skip[i], ssd_norm_g[i],
                       w_out[i], w_ple_gate[i], w_ple_proj[i], ple_norm_g[i])
        return _rmsnorm(h, final_norm_g)

    y_prompt = trunk(x_prompt, p_prompt)
    y_sample = trunk(x_sample, p_sample)
    return (y_prompt, y_sample)
```

```python
import numpy as np
from contextlib import ExitStack
import concourse.bass as bass
import concourse.mybir as mybir
from concourse.bass_utils import run_bass_kernel_spmd

F32 = mybir.dt.float32
BF16 = mybir.dt.bfloat16
ALU = mybir.AluOpType
AF = mybir.ActivationFunctionType

D = 1024
NIN = 6208
EPS = 1e-6
TAU = 16.0
QSCALE = 128 ** -0.5
DEBUG = False
PHASES = "12b"
BLK = 16
STOP = 0
VAR = 0


class Buf:
    __slots__ = ("name", "lw", "rd", "dsem", "excl")

    def __init__(self, name):
        self.name = name
        self.excl = False
        self.lw = None
        self.rd = {}
        self.dsem = None


class DSem:
    def __init__(self, name, h):
        self.name, self.h, self.cnt = name, h, 0


class V:
    def __init__(self, ap, bufs):
        self.ap, self.bufs = ap, tuple(bufs)

    def r(self, pat, **kw):
        return V(self.ap.rearrange(pat, **kw), self.bufs)

    def __getitem__(self, key):
        return V(self.ap[key], self.bufs)

    def bc(self, axis, shape):
        return V(self.ap.unsqueeze(axis).broadcast_to(list(shape)), self.bufs)


class Tile:
    def __init__(self, t, name, ncols, seg):
        self.t, self.name = t, name
        self.ncols = ncols
        self.seg = seg or ncols
        n = (ncols + self.seg - 1) // self.seg
        self.bufs = [Buf(f"{name}.{i}") for i in range(n)]

    def __call__(self, c0=None, c1=None, p0=None, p1=None):
        ncols = self.ncols
        a = 0 if c0 is None else c0
        b = ncols if c1 is None else c1
        bufs = self.bufs[a // self.seg:(b - 1) // self.seg + 1]
        if p0 is None:
            ap = self.t[:, a:b]
        else:
            ap = self.t[p0:p1, a:b]
        return V(ap, bufs)


class K:
    def __init__(self, nc, st):
        self.nc, self.st = nc, st
        self.E = {"pe": nc.tensor, "act": nc.scalar, "dve": nc.vector, "pool": nc.gpsimd, "sp": nc.sync}
        self.sem = {k: st.enter_context(nc.semaphore("s_" + k)) for k in self.E}
        self.cnt = {k: 0 for k in self.E}
        self.waited = {k: {} for k in self.E}
        self.dpool = [DSem(f"d{i}", st.enter_context(nc.semaphore(f"d{i}"))) for i in range(56)]
        self.dfree = list(self.dpool)
        self.swpool = [DSem(f"w{i}", st.enter_context(nc.semaphore(f"w{i}"))) for i in range(6)]
        self.swfree = list(self.swpool)
        self.ring = [DSem(f"r{i}", st.enter_context(nc.semaphore(f"r{i}"))) for i in range(4)]
        self.uid = 0
        self.dummy = self.sb(st, "dummy", [128, 8], F32)

    def sb(self, ph, name, shape, dt, seg=None):
        self.uid += 1
        t = ph.enter_context(self.nc.sbuf_tensor(f"{name}_{self.uid}", list(shape), dt))
        ncols = int(np.prod(shape[1:]))
        return Tile(t, name, ncols, seg)

    def ps(self, ph, name, shape, dt, seg=None):
        self.uid += 1
        t = ph.enter_context(self.nc.psum_tensor(f"{name}_{self.uid}", list(shape), dt))
        tl = Tile(t, name, shape[1], seg)
        for b in tl.bufs:
            b.excl = True
        return tl

    def _sync(self, e, reads, writes):
        deps = {}

        def add(ev):
            if ev is not None:
                (sn, h), v = ev
                if deps.get(sn, (None, 0))[1] < v:
                    deps[sn] = (h, v)

        for b in reads:
            add(b.lw)
        for b in writes:
            add(b.lw)
            for ev in b.rd.values():
                add(ev)
        w = self.waited[e]
        for sn, (h, v) in deps.items():
            if e == "pe" and sn == "s_pe":
                continue
            if w.get(sn, 0) < v:
                self.E[e].wait_ge(h, v)
                w[sn] = v

    def _mark(self, ev, reads, writes):
        for b in writes:
            b.lw = ev
            b.rd = {}
        for b in reads:
            b.rd[ev[0][0]] = ev

    def op(self, e, fn, outs, ins):
        writes = [b for v in outs for b in v.bufs]
        reads = [b for v in ins for b in v.bufs]
        writes += [b for b in reads if b.excl]
        reads = [b for b in reads if not b.excl]
        self._sync(e, reads, writes)
        r = fn(self.E[e])
        last = r[-1] if isinstance(r, (list, tuple)) else r
        self.cnt[e] += 1
        last.then_inc(self.sem[e], 1)
        ev = (("s_" + e, self.sem[e]), self.cnt[e])
        self._mark(ev, reads, writes)

    def dma(self, q, out, in_, sembuf=None):
        writes = list(out.bufs)
        reads = list(in_.bufs)
        self._sync(q, reads, writes)
        sb_ = sembuf if sembuf is not None else out.bufs[0]
        if sb_.dsem is None:
            sb_.dsem = self.swfree.pop() if q == "pool" else self.dfree.pop()
            if DEBUG:
                print("DSEM", sb_.name, sb_.dsem.name, q)
        ds = sb_.dsem
        ds.cnt += 16
        self.E[q].dma_start(out=out.ap, in_=in_.ap).then_inc(ds.h, 16)
        ev = ((ds.name, ds.h), ds.cnt)
        self._mark(ev, reads, writes)

    def dma_multi(self, q, items, outv):
        writes = list(outv.bufs)
        self._sync(q, [], writes)
        w = self.waited[q]
        for idx, (o, i) in enumerate(items):
            ds = self.ring[idx % len(self.ring)]
            if w.get(ds.name, 0) < ds.cnt:
                self.E[q].wait_ge(ds.h, ds.cnt)
                w[ds.name] = ds.cnt
            ds.cnt += 16
            self.E[q].dma_start(out=o, in_=i).then_inc(ds.h, 16)
        for ds in self.ring:
            if w.get(ds.name, 0) < ds.cnt:
                self.E[q].wait_ge(ds.h, ds.cnt)
                w[ds.name] = ds.cnt
        self.cnt[q] += 1
        self.E[q].memset(self.dummy.t[:, :], 0.0).then_inc(self.sem[q], 1)
        self._mark((("s_" + q, self.sem[q]), self.cnt[q]), [], writes)

    def barrier(self, release=()):
        for e in self.E:
            w = self.waited[e]
            for e2 in self.E:
                if e2 != e and w.get("s_" + e2, 0) < self.cnt[e2]:
                    self.E[e].wait_ge(self.sem[e2], self.cnt[e2])
                    w["s_" + e2] = self.cnt[e2]
            for ds in self.dpool + self.ring + self.swpool:
                if ds.cnt and w.get(ds.name, 0) < ds.cnt:
                    self.E[e].wait_ge(ds.h, ds.cnt)
                    w[ds.name] = ds.cnt
        self.dfree = list(self.dpool)
        self.swfree = list(self.swpool)


def tt(k, e, out, a, b, op):
    k.op(e, lambda g: g.tensor_tensor(out=out.ap, in0=a.ap, in1=b.ap, op=op), [out], [a, b])


def ts(k, e, out, a, s1, s2, op0, op1=None):
    ins = [a] + [s for s in (s1, s2) if isinstance(s, V)]
    a1 = s1.ap if isinstance(s1, V) else s1
    a2 = s2.ap if isinstance(s2, V) else s2
    if op1 is None:
        k.op(e, lambda g: g.tensor_scalar(out=out.ap, in0=a.ap, scalar1=a1, scalar2=None, op0=op0), [out], ins)
    else:
        k.op(e, lambda g: g.tensor_scalar(out=out.ap, in0=a.ap, scalar1=a1, scalar2=a2, op0=op0, op1=op1), [out], ins)


def stt(k, out, a, s, b, op0, op1):
    ins = [a, b] + ([s] if isinstance(s, V) else [])
    sa = s.ap if isinstance(s, V) else s
    k.op("dve", lambda g: g.scalar_tensor_tensor(out=out.ap, in0=a.ap, scalar=sa, in1=b.ap, op0=op0, op1=op1),
         [out], ins)


def act(k, out, a, func, scale=1.0, bias=0.0, accum=None):
    ins = [a] + ([scale] if isinstance(scale, V) else []) + ([bias] if isinstance(bias, V) else [])
    sc = scale.ap if isinstance(scale, V) else scale
    bi = bias.ap if isinstance(bias, V) else bias
    outs = [out] + ([accum] if accum is not None else [])
    if accum is None:
        k.op("act", lambda g: g.activation(out=out.ap, in_=a.ap, func=func, bias=bi, scale=sc), outs, ins)
    else:
        k.op("act", lambda g: g.activation(out=out.ap, in_=a.ap, func=func, bias=bi, scale=sc,
                                           accum_out=accum.ap), outs, ins)


def recip(k, out, a):
    k.op("dve", lambda g: g.reciprocal(out=out.ap, in_=a.ap), [out], [a])


def cp(k, e, out, a):
    if e == "act":
        k.op("act", lambda g: g.copy(out=out.ap, in_=a.ap), [out], [a])
    else:
        k.op(e, lambda g: g.tensor_copy(out=out.ap, in_=a.ap), [out], [a])


def mset(k, e, out, val):
    k.op(e, lambda g: g.memset(out.ap, val), [out], [])


def mms(k, items, outs, ins):
    def fn(pe):
        r = None
        for (o, l, rr, s0, s1) in items:
            r = pe.matmul(o, l, rr, start=s0, stop=s1)
        return r
    k.op("pe", fn, outs, ins)


def trs(k, items, outs, ins):
    def fn(pe):
        r = None
        for (o, i, idn) in items:
            r = pe.transpose(o, i, idn)
        return r
    k.op("pe", fn, outs, ins)


class Prog:
    def __init__(self, seqs):
        self.seqs = seqs
        self.nc = nc = bass.Bass("TRN2", target_bir_lowering=False)
        dt_in = lambda n, s: nc.dram_tensor(n, list(s), F32, kind="ExternalInput").ap()
        self.xin, self.pin, self.yout = [], [], []
        for i, L in enumerate(seqs):
            self.xin.append(dt_in(f"x{i}", [L, D]))
            self.pin.append(dt_in(f"p{i}", [2, L, 256]))
            self.yout.append(nc.dram_tensor(f"y{i}", [L, D], F32, kind="ExternalOutput").ap())
        self.flg_in = dt_in("flg", [1])
        self.w = {
            "norm_g": dt_in("norm_g", [2, D]), "w_in": dt_in("w_in", [2, D, NIN]),
            "w_gla_gate": dt_in("w_gla_gate", [2, 2, 16, 512]), "b_gla_gate": dt_in("b_gla_gate", [2, 2, 512]),
            "gla_onorm_g": dt_in("gla_onorm_g", [2, 256]), "conv_w": dt_in("conv_w", [2, 5, 2048]),
            "conv_b": dt_in("conv_b", [2, 2048]), "dt_bias": dt_in("dt_bias", [2, 2, 16]),
            "a_log": dt_in("a_log", [2, 2, 16]), "d_skip": dt_in("d_skip", [2, 16]),
            "ssd_norm_g": dt_in("ssd_norm_g", [2, D]), "w_out": dt_in("w_out", [2, 2048, D]),
            "w_ple_gate": dt_in("w_ple_gate", [2, D, D]), "w_ple_proj": dt_in("w_ple_proj", [2, 256, D]),
            "ple_norm_g": dt_in("ple_norm_g", [2, D]), "final_norm_g": dt_in("final_norm_g", [D]),
        }
        T = max(seqs) // 128
        self.Tmax = T
        scr = lambda n, s, d: nc.dram_tensor(n, list(s), d, kind="Internal").ap()
        self.s = {
            "hs": scr("s_hs", [T, 128, D], F32),
            "qeb": scr("s_qeb", [T, 128, 512], BF16), "ktb": scr("s_ktb", [T, 128, 512], BF16),
            "v": scr("s_v", [T, 128, 1024], BF16), "sg": scr("s_sg", [T, 128, 1024], BF16),
            "op": scr("s_op", [T, 128, 1024], F32), "elb": scr("s_elb", [T, 128, 4], F32),
            "c": scr("s_c", [T, 128, 512], BF16), "btm": scr("s_btm", [T, 128, 512], BF16),
            "xtb": scr("s_xtb", [T, 128, 1024], BF16), "sz": scr("s_sz", [T, 128, 1024], BF16),
            "yp": scr("s_yp", [T, 128, 1024], F32), "sm": scr("s_sm", [T, 128, 32], F32),
        }
        self.sbuf_ = {n: [Buf(f"dram_{n}_{t}") for t in range(T)] for n in self.s}
        self.wbuf = Buf("weights")

        with ExitStack() as st:
            self.k = K(nc, st)
            for si, L in enumerate(seqs):
                for layer in range(2):
                    if "1" in PHASES:
                        self.phase_f1(si, L, layer)
                    if "2" in PHASES:
                        self.phase_f2(si, L, layer)
                    if "b" in PHASES:
                        self.phase_b(si, L, layer)
            self.k.barrier()

    def scr(self, name, t):
        return V(self.s[name][t], [self.sbuf_[name][t]])

    def wv(self, ap):
        return V(ap, [self.wbuf])

    def hsrc(self, si, layer, t):
        if layer == 0:
            return V(self.xin[si][t * 128:(t + 1) * 128, :], [self.wbuf])
        return self.scr("hs", t)

    def consts(self, ph, need):
        k = self.k
        c = {}
        ones = k.sb(ph, "ones", [128, 128], F32)
        mset(k, "pool", ones(), 1.0)
        c["ones"] = ones

        def affine(name, pattern, cm, cmp):
            t = k.sb(ph, name, [128, 128], F32)
            k.op("pool", lambda g: g.affine_select(out=t.t[:, :], in_=ones.t[:, :], pattern=pattern, compare_op=cmp,
                                                   fill=0.0, base=0, channel_multiplier=cm), [t()], [ones()])
            return t

        identf = affine("identf", [[-1, 128]], 1, ALU.is_equal)
        c["identf"] = identf
        ident = k.sb(ph, "ident", [128, 128], BF16)
        cp(k, "pool", ident(), identf())
        c["ident"] = ident
        if "masks" in need:
            c["LE"] = affine("mLE", [[1, 128]], -1, ALU.is_ge)
            c["GT"] = affine("mGT", [[-1, 128]], 1, ALU.is_gt)
            c["GE"] = affine("mGE", [[-1, 128]], 1, ALU.is_ge)
        onesr = k.sb(ph, "onesr", [1, 128], BF16)
        mset(k, "pool", onesr(), 1.0)
        c["onesr"] = onesr
        return c

    def load_bcast(self, ph, name, ap1d, n):
        t = self.k.sb(ph, name, [128, n], F32)
        self.k.dma("sp", t(), self.wv(ap1d.partition_broadcast(128)))
        return t

    def load_w(self, dst, src_ap, rows, ncols):
        k = self.k
        items = []
        for kc in range(rows // 128):
            for c0 in range(0, ncols, 1024):
                c1 = min(ncols, c0 + 1024)
                items.append((dst.t[:, kc * ncols + c0:kc * ncols + c1], src_ap[kc * 128:(kc + 1) * 128, c0:c1]))
        k.dma_multi("pool", items, dst())

    def front_load(self, fe, si, layer, t):
        self.k.dma("sp", fe["x"][t % 2](), self.hsrc(si, layer, t))

    def front(self, fe, si, layer, t, uT, off):
        k = self.k
        x = fe["x"][t % 2]
        ss = fe["ss"][t % 2]
        act(k, fe["junk"](), x(), AF.Square, accum=ss(0, 1))
        act(k, ss(1, 2), ss(0, 1), AF.Ln, scale=1.0 / D, bias=fe["eps"](0, 1))
        act(k, ss(2, 3), ss(1, 2), AF.Exp, scale=-0.5)
        u = fe["u"]
        stt(k, u(), x(), ss(2, 3), fe["g"](), ALU.mult, ALU.mult)
        pT = fe["pT"]
        idn = fe["ident"]
        trs(k, [(pT(kc * 128, (kc + 1) * 128).ap, u(kc * 128, (kc + 1) * 128).ap, idn().ap) for kc in range(8)],
            [pT(0, 1024)], [u(), idn()])
        cp(k, "act", uT().r("p (a b) -> p a b", a=8)[:, :, off:off + 128],
           pT(0, 1024).r("p (a b) -> p a b", a=8))

    def front_bufs(self, ph, c, layer, pT):
        k = self.k
        fe = {"x": [k.sb(ph, f"x{i}", [128, D], F32) for i in range(2)],
              "ss": [k.sb(ph, f"ss{i}", [128, 4], F32) for i in range(2)],
              "junk": k.sb(ph, "junk", [128, D], BF16), "u": k.sb(ph, "u", [128, D], BF16),
              "g": self.load_bcast(ph, "normg", self.w["norm_g"][layer], D),
              "pT": pT, "ident": c["ident"], "uTw": None}
        eps = k.sb(ph, "eps", [128, 1], F32)
        mset(k, "pool", eps(), EPS)
        fe["eps"] = eps
        return fe

    def phase_f1(self, si, L, layer):
        k = self.k
        NT = L // 128
        w = self.w
        with ExitStack() as ph:
            c = self.consts(ph, {"masks"})
            W = k.sb(ph, "W1", [128, 8 * 3104], BF16)
            self.load_w(W, w["w_in"][layer][:, 0:3104], D, 3104)
            Wv = lambda kc, c0, c1: W(kc * 3104 + c0, kc * 3104 + c1)
            wg = k.sb(ph, "wg", [32, 1024], BF16)
            mset(k, "pool", wg(), 0.0)
            k.dma("pool", wg(0, 512, 0, 16), self.wv(w["w_gla_gate"][layer, 0]))
            k.dma("pool", wg(512, 1024, 16, 32), self.wv(w["w_gla_gate"][layer, 1]))
            bgr = k.sb(ph, "bgr", [1, 1024], BF16)
            k.dma("pool", bgr(), self.wv(w["b_gla_gate"][layer].rearrange("(o a) b -> o (a b)", o=1)))
            gon = self.load_bcast(ph, "gon", w["gla_onorm_g"][layer], 256)
            flg = self.load_bcast(ph, "flg", self.flg_in, 1)
            PT = k.ps(ph, "PT", [128, 2048], BF16, seg=1024)
            PA = k.ps(ph, "PA", [128, 1024], F32, seg=512)
            PB = k.ps(ph, "PB", [128, 1024], F32, seg=512)
            PC = k.ps(ph, "PC", [128, 1024], F32, seg=512)
            fe = self.front_bufs(ph, c, layer, PT)
            uT = [k.sb(ph, f"uT{i}", [128, 8 * 128], BF16) for i in range(2)]
            qk = [k.sb(ph, f"qk{i}", [128, 1024], F32, seg=512) for i in range(2)]
            vtm_ = [k.sb(ph, f"vtm{i}", [128, 1024], BF16, seg=512) for i in range(2)]
            lrT_ = [k.sb(ph, f"lrT{i}", [32, 128], BF16) for i in range(2)]
            sg = k.sb(ph, "sg", [128, 1024], BF16)
            tmpA = k.sb(ph, "tmpA", [128, 1024], F32, seg=512)
            tmp = k.sb(ph, "tmpf", [128, 1024], F32)
            Lsp = k.sb(ph, "Lsp", [128, 1024], F32)
            cum = k.sb(ph, "cum", [128, 1024], F32, seg=512)
            rmask = k.sb(ph, "rmask", [128, 512], F32)
            mset(k, "pool", rmask(), 1.0)
            for hd in range(4):
                mset(k, "pool", rmask(hd * 128, hd * 128 + 1), 0.0)
            Ep = [k.sb(ph, f"Ep{d}", [128, 512], F32) for d in range(2)]
            Em = [k.sb(ph, f"Em{d}", [128, 512], F32) for d in range(2)]
            qe = [k.sb(ph, f"qe{d}", [128, 512], BF16) for d in range(2)]
            ke = [k.sb(ph, f"ke{d}", [128, 512], BF16) for d in range(2)]
            kt = [k.sb(ph, f"kt{d}", [128, 512], BF16) for d in range(2)]
            kttm = [k.sb(ph, f"kttm{d}", [128, 512], BF16) for d in range(2)]
            elb = k.sb(ph, "elb", [128, 4], F32)
            att = k.sb(ph, "att", [128, 1024], BF16, seg=512)
            opart = k.sb(ph, "opart", [128, 1024], F32)
            S32 = k.sb(ph, "S32", [128, 1024], F32)
            Sbf = k.sb(ph, "Sbf", [128, 1024], BF16)
            mset(k, "pool", S32(), 0.0)
            mset(k, "pool", Sbf(), 0.0)

            def stageA(t):
                u = uT[t % 2]
                if t + 1 < NT:
                    self.front_load(fe, si, layer, t + 1)
                self.front(fe, si, layer, t, u, 0)
                yield
                u3 = lambda kc: u(kc * 128, (kc + 1) * 128)
                for qi, P in ((0, PA(0, 512)), (1, PA(512, 1024))):
                    items = []
                    for hd in range(4):
                        for kc in range(8):
                            items.append((P.ap[:, hd * 128:(hd + 1) * 128],
                                          Wv(kc, qi * 512 + hd * 128, qi * 512 + (hd + 1) * 128).ap,
                                          u3(kc).ap, kc == 0, kc == 7))
                    mms(k, items, [P], [W(), u()])
                    cp(k, "act", qk[t % 2](qi * 512, (qi + 1) * 512), P)
                    yield
                mms(k, [(PA(0, 128, 0, 32).ap, Wv(kc, 3072, 3104).ap, u3(kc).ap, kc == 0, kc == 7) for kc in range(8)],
                    [PA(0, 512)], [W(), u()])
                cp(k, "act", lrT_[t % 2](), PA(0, 128, 0, 32))
                vtm = vtm_[t % 2]
                for half in range(2):
                    P = PA((1 - half) * 512, (2 - half) * 512)
                    mms(k, [(P.ap, u3(kc).ap, Wv(kc, 1024 + half * 512, 1024 + (half + 1) * 512).ap, kc == 0, kc == 7)
                            for kc in range(8)], [P], [W(), u()])
                    cp(k, "act", vtm(half * 512, (half + 1) * 512), P)
                k.dma("sp", self.scr("v", t), vtm(), sembuf=vtm.bufs[0])
                yield
                for half in range(2):
                    P = PA((1 - half) * 512, (2 - half) * 512)
                    mms(k, [(P.ap, u3(kc).ap, Wv(kc, 2048 + half * 512, 2048 + (half + 1) * 512).ap, kc == 0, kc == 7)
                            for kc in range(8)], [P], [W(), u()])
                    th = tmpA(half * 512, (half + 1) * 512)
                    act(k, th, P, AF.Exp, scale=-1.0)
                    act(k, th, th, AF.Ln, bias=1.0)
                    act(k, th, th, AF.Exp, scale=-1.0)
                    tt(k, "dve", th, P, th, ALU.mult)
                yield
                tt(k, "dve", sg().r("p (h d) -> p h d", h=4), tmpA().r("p (h d) -> p h d", h=4),
                   gon().bc(1, [128, 4, 256]), ALU.mult)
                k.dma("sp", self.scr("sg", t), sg(), sembuf=sg.bufs[0])

            def stageB(t):
                vtm = vtm_[t % 2]
                lrT = lrT_[t % 2]
                if t % BLK == 0 and t > 0:
                    ts(k, "dve", S32(), S32(), flg(0, 1), None, ALU.mult)
                    cp(k, "act", Sbf(), S32())
                qs = qk[t % 2](0, 512)
                ks_ = qk[t % 2](512, 1024)
                items = []
                for d in range(2):
                    for hd in range(4):
                        o = PB.t[:, (d * 4 + hd) * 128:(d * 4 + hd + 1) * 128]
                        items.append((o, wg.t[0:32, d * 512 + hd * 128:d * 512 + (hd + 1) * 128], lrT.t[0:32, :], True, False))
                        items.append((o, bgr.t[0:1, d * 512 + hd * 128:d * 512 + (hd + 1) * 128], c["onesr"].t[0:1, :], False, True))
                mms(k, items, [PB()], [wg(), lrT(), bgr(), c["onesr"]()])
                act(k, Lsp(), PB(), AF.Exp, scale=-1.0)
                act(k, Lsp(), Lsp(), AF.Ln, bias=1.0)
                yield
                for d in range(2):
                    k.op("dve", lambda g, d=d: g.tensor_tensor_scan(
                        out=cum.t[:, d * 512:(d + 1) * 512], data0=rmask.t[:, :], data1=Lsp.t[:, d * 512:(d + 1) * 512],
                        initial=0.0, op0=ALU.mult, op1=ALU.add), [cum(d * 512, (d + 1) * 512)], [rmask(), Lsp()])
                tt(k, "dve", tmp(0, 512), cum(512, 1024), Lsp(512, 1024), ALU.subtract)
                for hd in range(4):
                    ts(k, "dve", tmp(512 + hd * 128, 512 + (hd + 1) * 128), tmp(hd * 128, (hd + 1) * 128), -1.0,
                       cum(512 + hd * 128 + 127, 512 + hd * 128 + 128), ALU.mult, ALU.add)
                yield
                srcs = [cum(0, 512), tmp(512, 1024)]
                for d in range(2):
                    act(k, Ep[d](), srcs[d], AF.Exp, scale=-1.0 / TAU)
                    act(k, Em[d](), srcs[d], AF.Exp, scale=1.0 / TAU)
                for d in range(2):
                    stt(k, qe[d](), qs, QSCALE, Ep[d](), ALU.mult, ALU.mult)
                    tt(k, "dve", ke[d](), ks_, Em[d](), ALU.mult)
                    for hd in range(4):
                        col = hd * 128 + (127 if d == 0 else 0)
                        ts(k, "dve", kt[d](hd * 128, (hd + 1) * 128), ke[d](hd * 128, (hd + 1) * 128),
                           Ep[d](col, col + 1), None, ALU.mult)
                    yield
                cp(k, "pool", elb().r("p (h o) -> p h o", h=4), Ep[1]().r("p (h t) -> p h t", h=4)[:, :, 0:1])
                k.dma("sp", self.scr("elb", t), elb(), sembuf=elb.bufs[0])
                k.dma("sp", self.scr("qeb", t), qe[1](), sembuf=qe[1].bufs[0])
                PT1 = PT(1024, 2048)
                trs(k, [(PT.t[:, 1024 + (d * 4 + hd) * 128:1024 + (d * 4 + hd + 1) * 128],
                         kt[d].t[:, hd * 128:(hd + 1) * 128], c["ident"].t[:, :]) for d in range(2) for hd in range(4)],
                    [PT1], [kt[0](), kt[1](), c["ident"]()])
                cp(k, "act", kttm[0](), PT(1024, 1536))
                cp(k, "act", kttm[1](), PT(1536, 2048))
                k.dma("sp", self.scr("ktb", t), kttm[1](), sembuf=kttm[1].bufs[0])
                yield
                items = []
                for d in range(2):
                    for hd in range(4):
                        items.append((PB.t[:, (d * 4 + hd) * 128:(d * 4 + hd + 1) * 128],
                                      ke[d].t[:, hd * 128:(hd + 1) * 128], qe[d].t[:, hd * 128:(hd + 1) * 128], True, True))
                mms(k, items, [PB()], [ke[0](), ke[1](), qe[0](), qe[1]()])
                for d, mk in ((0, c["LE"]), (1, c["GT"])):
                    tt(k, "dve", att(d * 512, (d + 1) * 512).r("p (h t) -> p h t", h=4),
                       PB(d * 512, (d + 1) * 512).r("p (h t) -> p h t", h=4), mk().bc(1, [128, 4, 128]), ALU.mult)
                yield
                for half in range(2):
                    items = []
                    for hd in (2 * half, 2 * half + 1):
                        o = PC.t[:, hd * 256:(hd + 1) * 256]
                        vv = vtm.t[:, hd * 256:(hd + 1) * 256]
                        items.append((o, att.t[:, hd * 128:(hd + 1) * 128], vv, True, False))
                        items.append((o, att.t[:, 512 + hd * 128:512 + (hd + 1) * 128], vv, False, False))
                        items.append((o, qe[0].t[:, hd * 128:(hd + 1) * 128], Sbf.t[:, hd * 256:(hd + 1) * 256], False, True))
                    mms(k, items, [PC(half * 512, (half + 1) * 512)], [att(), vtm(), qe[0](), Sbf()])
                cp(k, "act", opart(), PC())
                k.dma("sp", self.scr("op", t), opart(), sembuf=opart.bufs[0])
                yield
                for half in range(2):
                    items = []
                    for hd in (2 * half, 2 * half + 1):
                        items.append((PC.t[:, hd * 256:(hd + 1) * 256], kttm[0].t[:, hd * 128:(hd + 1) * 128],
                                      vtm.t[:, hd * 256:(hd + 1) * 256], True, True))
                    mms(k, items, [PC(half * 512, (half + 1) * 512)], [kttm[0](), vtm()])
                for hd in range(4):
                    stt(k, S32(hd * 256, (hd + 1) * 256), S32(hd * 256, (hd + 1) * 256),
                        Ep[0](hd * 128 + 127, hd * 128 + 128), PC(hd * 256, (hd + 1) * 256), ALU.mult, ALU.add)
                cp(k, "act", Sbf(), S32())

            self.front_load(fe, si, layer, 0)
            run_gen(stageA(0))
            for t in range(NT):
                merge(stageB(t), stageA(t + 1) if t + 1 < NT else None)
            k.barrier()

    def phase_f2(self, si, L, layer):
        k = self.k
        NT = L // 128
        w = self.w
        with ExitStack() as ph:
            c = self.consts(ph, {"masks"})
            W = k.sb(ph, "W2", [128, 8 * 3104], BF16)
            self.load_w(W, w["w_in"][layer][:, 3104:6208], D, 3104)
            PT = k.ps(ph, "PT", [128, 2048], BF16, seg=1024)
            PA = k.ps(ph, "PA", [128, 1024], F32, seg=512)
            PB = k.ps(ph, "PB", [128, 1024], F32, seg=512)
            PC = k.ps(ph, "PC", [128, 1024], F32, seg=512)
            cwr = k.sb(ph, "cwr", [80, 128], F32)
            k.dma("sp", cwr(), self.wv(w["conv_w"][layer].rearrange("j (b p) -> (j b) p", p=128)))
            trs(k, [(PC.t[:, 0:80], cwr.t[0:80, :], c["identf"].t[0:80, 0:80])], [PC(0, 512)], [cwr(), c["identf"]()])
            cw = k.sb(ph, "cw", [128, 80], F32)
            cp(k, "act", cw(), PC(0, 80))
            diag = k.sb(ph, "diag", [128, 80 * 128], BF16)
            for i in range(80):
                ts(k, "dve", diag(i * 128, (i + 1) * 128), c["identf"](), cw(i, i + 1), None, ALU.mult)
            cbr = k.sb(ph, "cbr", [1, 2048], BF16)
            k.dma("pool", cbr(0, 1024), self.wv(w["conv_b"][layer:layer + 1, 0:1024]))
            k.dma("pool", cbr(1024, 2048), self.wv(w["conv_b"][layer:layer + 1, 1024:2048]))
            lexp = k.sb(ph, "lexp", [128, 2048], BF16)
            sel = k.sb(ph, "sel", [32, 32 * 128], BF16)
            mset(k, "pool", lexp(), 1.0)
            for hf in range(2):
                k.op("pool", lambda g, hf=hf: g.affine_select(
                    out=sel.t[:, hf * 2048:(hf + 1) * 2048].rearrange("p (a b) -> p a b", a=16),
                    in_=lexp.t[0:32, :].rearrange("p (a b) -> p a b", a=16),
                    pattern=[[-1, 16], [0, 128]], compare_op=ALU.is_equal, fill=0.0, base=-16 * hf,
                    channel_multiplier=1), [sel()], [lexp()])
            dtb = self.load_bcast(ph, "dtb", w["dt_bias"][layer].rearrange("a b -> (a b)"), 32)
            Abc = self.load_bcast(ph, "Abc", w["a_log"][layer].rearrange("a b -> (a b)"), 32)
            act(k, Abc(), Abc(), AF.Exp)
            ts(k, "dve", Abc(), Abc(), -1.0, None, ALU.mult)
            Dbc = self.load_bcast(ph, "Dbc", w["d_skip"][layer], 16)
            flg = self.load_bcast(ph, "flg", self.flg_in, 1)
            fe = self.front_bufs(ph, c, layer, PT)
            uT = [k.sb(ph, f"uT{i}", [128, 8 * 132], BF16) for i in range(2)]
            tmpA = k.sb(ph, "tmpA", [128, 1024], F32, seg=512)
            tmp = k.sb(ph, "tmpf", [128, 1024], F32)
            sz = k.sb(ph, "sz", [128, 1024], BF16)
            raw = k.sb(ph, "raw", [128, 16 * 132], BF16, seg=3 * 132)
            xc_ = [k.sb(ph, f"xc{i}", [128, 1024], BF16) for i in range(2)]
            bcf_ = [k.sb(ph, f"bcf{i}", [128, 1024], BF16) for i in range(2)]
            draw_ = [k.sb(ph, f"draw{i}", [128, 32], F32) for i in range(2)]
            sm = k.sb(ph, "sm", [128, 32], F32)
            dts_ = [k.sb(ph, f"dts{i}", [128, 32 * 8], F32, seg=32) for i in range(2)]
            dtaf_ = [k.sb(ph, f"dtaf{i}", [128, 64], F32) for i in range(2)]
            for i in range(2):
                mset(k, "pool", dtaf_[i](), 0.0)
            etot_ = [k.sb(ph, f"etot{i}", [128, 32], F32) for i in range(2)]
            cumfm = k.sb(ph, "cumfm", [32, 128], F32)
            chi_ = [k.sb(ph, f"chi{i}", [32, 128], BF16) for i in range(2)]
            clo_ = [k.sb(ph, f"clo{i}", [32, 128], BF16) for i in range(2)]
            xdt = [k.sb(ph, f"xdt{d}", [128, 1024], BF16) for d in range(2)]
            xtl = [k.sb(ph, f"xtl{d}", [128, 1024], BF16) for d in range(2)]
            yD = k.sb(ph, "yD", [128, 1024], F32)
            btm = k.sb(ph, "btm", [128, 512], BF16)
            cbm = [k.sb(ph, f"cbm{d}", [128, 512], BF16) for d in range(2)]
            seg = k.sb(ph, "seg", [128, 2048], F32)
            WT = [k.sb(ph, f"WT{d}", [128, 2048], BF16) for d in range(2)]
            h32 = k.sb(ph, "h32", [128, 1024], F32)
            hbf = k.sb(ph, "hbf", [128, 1024], BF16)
            mset(k, "pool", h32(), 0.0)
            mset(k, "pool", hbf(), 0.0)
            for i in range(2):
                mset(k, "pool", uT[i](), 0.0)

            self.front_load(fe, si, layer, 0)
            if NT > 1:
                self.front_load(fe, si, layer, 1)
            self.front(fe, si, layer, 0, uT[0], 2)
            if NT > 1:
                self.front(fe, si, layer, 1, uT[1], 2)
                if NT > 2:
                    self.front_load(fe, si, layer, 2)

            def stageA(t):
                u = uT[t % 2]
                un = uT[(t + 1) % 2]
                u3 = u().r("p (a b) -> p a b", a=8)
                un3 = un().r("p (a b) -> p a b", a=8)
                dts = dts_[t % 2]
                _, d_dt, d_dta, d_cum, d_tot, d_ecum, d_tail, d_dtt = [dts(i * 32, (i + 1) * 32) for i in range(8)]
                dtaf, etot, chi, clo = dtaf_[t % 2], etot_[t % 2], chi_[t % 2], clo_[t % 2]
                if t + 1 < NT:
                    if (t + 1) % BLK == 0:
                        ts(k, "pool", u3[:, :, 130:132], un3[:, :, 2:4], flg(0, 1), None, ALU.mult)
                        ts(k, "pool", un3[:, :, 0:2], u3[:, :, 128:130], flg(0, 1), None, ALU.mult)
                    else:
                        cp(k, "pool", u3[:, :, 130:132], un3[:, :, 2:4])
                        cp(k, "pool", un3[:, :, 0:2], u3[:, :, 128:130])
                else:
                    k.op("pool", lambda g: g.memset(u3.ap[:, :, 130:132], 0.0), [u()], [])
                yield
                ui = lambda kc: u.t[:, kc * 132 + 2:kc * 132 + 130]
                ue = lambda kc: u.t[:, kc * 132:(kc + 1) * 132]
                xc, bcf, d_raw = xc_[t % 2], bcf_[t % 2], draw_[t % 2]
                groups = [(0, 1, 2), (3, 4, 5), (6, 7, 8), (9, 10, 11), (12, 13, 14), (15,)]
                for gi, blks in enumerate(groups):
                    base = (gi % 2) * 512
                    items = []
                    for bi, blk in enumerate(blks):
                        o = PA.t[:, base + bi * 132:base + (bi + 1) * 132]
                        for kc in range(8):
                            items.append((o, W.t[:, kc * 3104 + 1024 + blk * 128:kc * 3104 + 1024 + (blk + 1) * 128],
                                          ue(kc), kc == 0, kc == 7))
                    if gi == 5:
                        for kc in range(8):
                            items.append((PA.t[:, base + 132:base + 164], ui(kc), W.t[:, kc * 3104 + 3072:kc * 3104 + 3104],
                                          kc == 0, kc == 7))
                    mms(k, items, [PA(base, base + 512)], [W(), u()])
                    if gi == 5:
                        tt(k, "dve", d_raw(), PA(base + 132, base + 164), dtb(), ALU.add)
                    cp(k, "act", raw(blks[0] * 132, (blks[-1] + 1) * 132), PA(base, base + len(blks) * 132))
                    yield
                for half in range(2):
                    P = PA(half * 512, (half + 1) * 512)
                    mms(k, [(P.ap, ui(kc), W.t[:, kc * 3104 + half * 512:kc * 3104 + (half + 1) * 512], kc == 0, kc == 7)
                            for kc in range(8)], [P], [W(), u()])
                    th = tmpA(half * 512, (half + 1) * 512)
                    act(k, th, P, AF.Exp, scale=-1.0)
                    act(k, th, th, AF.Ln, bias=1.0)
                    act(k, th, th, AF.Exp, scale=-1.0)
                    tt(k, "dve", sz(half * 512, (half + 1) * 512), P, th, ALU.mult)
                    yield
                k.dma("sp", self.scr("sz", t), sz(), sembuf=sz.bufs[0])
                for r in range(4):
                    P = PA((r % 2) * 512, (r % 2 + 1) * 512)
                    dst = xc if r < 2 else bcf
                    items = []
                    for b in range(4):
                        blk = r * 4 + b
                        o = PA.t[:, (r % 2) * 512 + b * 128:(r % 2) * 512 + (b + 1) * 128]
                        for j in range(5):
                            items.append((o, diag.t[:, (j * 16 + blk) * 128:(j * 16 + blk + 1) * 128],
                                          raw.t[:, blk * 132 + j:blk * 132 + j + 128], j == 0, False))
                        items.append((o, cbr.t[0:1, blk * 128:(blk + 1) * 128], c["onesr"].t[0:1, :], False, True))
                    mms(k, items, [P], [diag(), raw(), cbr(), c["onesr"]()])
                    th = tmpA((r % 2) * 512, (r % 2 + 1) * 512)
                    act(k, th, P, AF.Exp, scale=-1.0)
                    act(k, th, th, AF.Ln, bias=1.0)
                    act(k, th, th, AF.Exp, scale=-1.0)
                    tt(k, "dve", dst((r % 2) * 512, (r % 2 + 1) * 512), P, th, ALU.mult)
                    yield
                k.dma("sp", self.scr("c", t), bcf(512, 1024), sembuf=bcf.bufs[0])
                act(k, d_dt, d_raw(), AF.Exp)
                act(k, d_dt, d_dt, AF.Ln, bias=1.0)
                tt(k, "dve", d_dta, d_dt, Abc(), ALU.mult)
                cp(k, "dve", dtaf(0, 16), d_dta[:, 0:16])
                cp(k, "dve", dtaf(48, 64), d_dta[:, 16:32])
                mms(k, [(PA.t[:, 0:16], c["LE"].t[:, :], dts.t[:, 64:80], True, True),
                        (PA.t[:, 16:32], c["GE"].t[:, :], dts.t[:, 80:96], True, True),
                        (PA.t[:, 32:64], c["ones"].t[:, :], dts.t[:, 64:96], True, True),
                        (PA.t[0:32, 64:192], dtaf.t[:, 0:32], c["LE"].t[:, :], True, False),
                        (PA.t[0:32, 64:192], dtaf.t[:, 32:64], c["GE"].t[:, :], False, True)],
                    [PA(0, 512)], [c["LE"](), c["GE"](), c["ones"](), d_dta, dtaf()])
                cp(k, "act", d_cum, PA(0, 32))
                cp(k, "act", d_tot, PA(32, 64))
                cp(k, "act", cumfm(), PA(64, 192, 0, 32))
                cp(k, "act", chi(), cumfm())
                tt(k, "dve", clo(), cumfm(), chi(), ALU.subtract)
                act(k, d_ecum, d_cum, AF.Exp)
                tt(k, "dve", d_tail, d_tot, d_cum, ALU.subtract)
                act(k, d_tail, d_tail, AF.Exp)
                tt(k, "dve", d_dtt, d_dt, d_tail, ALU.mult)
                act(k, etot(), d_tot, AF.Exp)
                cp(k, "pool", sm(0, 16), d_ecum[:, 16:32])
                cp(k, "pool", sm(16, 32), etot(16, 32))
                k.dma("sp", self.scr("sm", t), sm(), sembuf=sm.bufs[0])
                yield
                if t + 2 < NT:
                    self.front(fe, si, layer, t + 2, u, 2)
                    if t + 3 < NT:
                        self.front_load(fe, si, layer, t + 3)

            def stageB(t):
                xc, bcf, d_raw = xc_[t % 2], bcf_[t % 2], draw_[t % 2]
                dts = dts_[t % 2]
                _, d_dt, d_dta, d_cum, d_tot, d_ecum, d_tail, d_dtt = [dts(i * 32, (i + 1) * 32) for i in range(8)]
                dtaf, etot, chi, clo = dtaf_[t % 2], etot_[t % 2], chi_[t % 2], clo_[t % 2]
                if t % BLK == 0 and t > 0:
                    ts(k, "dve", h32(), h32(), flg(0, 1), None, ALU.mult)
                    cp(k, "act", hbf(), h32())
                PT1 = PT(1024, 2048)
                trs(k, [(PT.t[:, 1024 + b * 128:1024 + (b + 1) * 128], xc.t[:, b * 128:(b + 1) * 128], c["ident"].t[:, :])
                        for b in range(8)], [PT1], [xc(), c["ident"]()])
                x3 = PT1.r("p (h d) -> p h d", h=16)
                for d in range(2):
                    tt(k, "dve", xdt[d]().r("p (h d) -> p h d", h=16), x3,
                       d_dt[:, d * 16:(d + 1) * 16].bc(2, [128, 16, 64]), ALU.mult)
                    tt(k, "dve", xtl[d]().r("p (h d) -> p h d", h=16), x3,
                       d_dtt[:, d * 16:(d + 1) * 16].bc(2, [128, 16, 64]), ALU.mult)
                    yield
                tt(k, "dve", yD().r("p (h d) -> p h d", h=16), x3, Dbc().bc(2, [128, 16, 64]), ALU.mult)
                k.dma("sp", self.scr("xtb", t), xtl[1](), sembuf=xtl[1].bufs[0])
                trs(k, [(PT.t[:, 1024 + g * 128:1024 + (g + 1) * 128], bcf.t[:, g * 128:(g + 1) * 128], c["ident"].t[:, :])
                        for g in range(4)], [PT1], [bcf(), c["ident"]()])
                cp(k, "act", btm(), PT(1024, 1536))
                k.dma("sp", self.scr("btm", t), btm(), sembuf=btm.bufs[0])
                mms(k, [(PC.t[:, 512 + g * 128:512 + (g + 1) * 128], bcf.t[:, g * 128:(g + 1) * 128],
                         bcf.t[:, (4 + g) * 128:(5 + g) * 128], True, True) for g in range(4)],
                    [PC(512, 1024)], [bcf()])
                for d, mk in ((0, c["LE"]), (1, c["GT"])):
                    tt(k, "dve", cbm[d]().r("p (g t) -> p g t", g=4), PC(512, 1024).r("p (g t) -> p g t", g=4),
                       mk().bc(1, [128, 4, 128]), ALU.mult)
                yield
                for d in range(2):
                    for r in range(4):
                        base = (r % 2) * 512
                        items = []
                        for hh in range(4):
                            h = r * 4 + hh
                            o = PB.t[:, base + hh * 128:base + (hh + 1) * 128]
                            s_ = sel.t[0:32, (d * 16 + h) * 128:(d * 16 + h + 1) * 128]
                            items.append((o, s_, chi.t[0:32, :], True, False))
                            items.append((o, s_, clo.t[0:32, :], False, True))
                        mms(k, items, [PB(base, base + 512)], [sel(), chi(), clo()])
                        for hh in range(4):
                            h = r * 4 + hh
                            ts(k, "dve", seg(h * 128, (h + 1) * 128), PB(base + hh * 128, base + (hh + 1) * 128),
                               d_cum[:, d * 16 + h:d * 16 + h + 1], 0.0, ALU.subtract, ALU.min)
                        if r % 2 == 1:
                            yield
                    act(k, lexp(), seg(), AF.Exp)
                    for g in range(4):
                        tt(k, "dve", WT[d](g * 512, (g + 1) * 512).r("p (h t) -> p h t", h=4),
                           lexp(g * 512, (g + 1) * 512).r("p (h t) -> p h t", h=4),
                           cbm[d](g * 128, (g + 1) * 128).bc(1, [128, 4, 128]), ALU.mult)
                    yield
                for half in range(2):
                    items = []
                    for h in range(half * 8, half * 8 + 8):
                        o = PC.t[:, h * 64:(h + 1) * 64]
                        items.append((o, WT[0].t[:, h * 128:(h + 1) * 128], xdt[0].t[:, h * 64:(h + 1) * 64], True, False))
                        items.append((o, WT[1].t[:, h * 128:(h + 1) * 128], xdt[1].t[:, h * 64:(h + 1) * 64], False, True))
                    mms(k, items, [PC(half * 512, (half + 1) * 512)], [WT[0](), WT[1](), xdt[0](), xdt[1]()])
                for half in range(2):
                    items = []
                    for g in (2 * half, 2 * half + 1):
                        items.append((PB.t[:, g * 256:(g + 1) * 256], bcf.t[:, (4 + g) * 128:(5 + g) * 128],
                                      hbf.t[:, g * 256:(g + 1) * 256], True, True))
                    mms(k, items, [PB(half * 512, (half + 1) * 512)], [bcf(), hbf()])
                tt(k, "dve", yD(), PC(), yD(), ALU.add)
                tt(k, "dve", tmp().r("p (h d) -> p h d", h=16), PB().r("p (h d) -> p h d", h=16),
                   d_ecum[:, 0:16].bc(2, [128, 16, 64]), ALU.mult)
                tt(k, "dve", yD(), yD(), tmp(), ALU.add)
                k.dma("sp", self.scr("yp", t), yD(), sembuf=yD.bufs[0])
                yield
                for half in range(2):
                    items = []
                    for g in (2 * half, 2 * half + 1):
                        items.append((PC.t[:, g * 256:(g + 1) * 256], btm.t[:, g * 128:(g + 1) * 128],
                                      xtl[0].t[:, g * 256:(g + 1) * 256], True, True))
                    mms(k, items, [PC(half * 512, (half + 1) * 512)], [btm(), xtl[0]()])
                tt(k, "dve", h32().r("p (h d) -> p h d", h=16), h32().r("p (h d) -> p h d", h=16),
                   etot(0, 16).bc(2, [128, 16, 64]), ALU.mult)
                tt(k, "dve", h32(), h32(), PC(), ALU.add)
                cp(k, "act", hbf(), h32())

            run_gen(stageA(0))
            for t in range(NT):
                merge(stageB(t), stageA(t + 1) if t + 1 < NT else None)
            k.barrier()

    def phase_b(self, si, L, layer):
        k = self.k
        NT = L // 128
        w = self.w
        with ExitStack() as ph:
            c = self.consts(ph, set())
            Wo = k.sb(ph, "Wo", [128, 16 * 1024], BF16)
            self.load_w(Wo, w["w_out"][layer], 2048, 1024)
            Wg = k.sb(ph, "Wpg", [128, 8 * 1024], BF16)
            self.load_w(Wg, w["w_ple_gate"][layer], 1024, 1024)
            Wp = k.sb(ph, "Wpp", [128, 2 * 1024], BF16)
            self.load_w(Wp, w["w_ple_proj"][layer], 256, 1024)
            gssd = self.load_bcast(ph, "gssd", w["ssd_norm_g"][layer], D)
            gple = self.load_bcast(ph, "gple", w["ple_norm_g"][layer], D)
            gfin = self.load_bcast(ph, "gfin", w["final_norm_g"], D) if layer == 1 else None
            eps = k.sb(ph, "eps", [128, 1], F32)
            mset(k, "pool", eps(), EPS)
            flg = self.load_bcast(ph, "flg", self.flg_in, 1)
            PT = k.ps(ph, "PT", [128, 2048], BF16, seg=1024)
            PA = k.ps(ph, "PA", [128, 1024], F32, seg=512)
            PB = k.ps(ph, "PB", [128, 1024], F32, seg=512)
            PC = k.ps(ph, "PC", [128, 1024], F32, seg=512)
            namesA = [("qeb", 512, BF16), ("ktb", 512, BF16), ("v", 1024, BF16), ("op", 1024, F32), ("elb", 4, F32),
                      ("c", 512, BF16), ("btm", 512, BF16), ("xtb", 1024, BF16), ("yp", 1024, F32), ("sm", 32, F32)]
            namesB = [("sg", 1024, BF16), ("sz", 1024, BF16)]
            ldA = [{n: k.sb(ph, f"l_{n}{i}", [128, wd], dt) for n, wd, dt in namesA} for i in range(2)]
            ldB = [{n: k.sb(ph, f"l_{n}{i}", [128, wd], dt) for n, wd, dt in namesB} for i in range(2)]
            for i in range(2):
                ldB[i]["h"] = k.sb(ph, f"l_h{i}", [128, D], F32)
                ldB[i]["p"] = k.sb(ph, f"l_p{i}", [128, 256], F32)
            Sb32 = k.sb(ph, "Sb32", [128, 1024], F32)
            Sbbf = k.sb(ph, "Sbbf", [128, 1024], BF16)
            hb32 = k.sb(ph, "hb32", [128, 1024], F32)
            hbbf = k.sb(ph, "hbbf", [128, 1024], BF16)
            for x_ in (Sb32, Sbbf, hb32, hbbf):
                mset(k, "pool", x_(), 0.0)
            o_ = [k.sb(ph, f"o{i}", [128, 1024], F32) for i in range(2)]
            y_ = [k.sb(ph, f"y{i}", [128, 1024], F32) for i in range(2)]
            tmpA = k.sb(ph, "tmpA", [128, 1024], F32)
            junk = k.sb(ph, "junkb", [128, 1024], BF16)
            st = k.sb(ph, "stat", [128, 32], F32, seg=4)
            mix = k.sb(ph, "mix", [128, 2048], BF16, seg=1024)
            mixT = k.sb(ph, "mixT", [128, 2048], BF16, seg=1024)
            hmid = k.sb(ph, "hmid", [128, 1024], F32)
            hmbf = k.sb(ph, "hmbf", [128, 1024], BF16)
            hmT = k.sb(ph, "hmT", [128, 1024], BF16)
            pbf = k.sb(ph, "pbf", [128, 256], BF16)
            pT = k.sb(ph, "pT", [128, 256], BF16)
            eg = k.sb(ph, "eg", [128, 1024], F32, seg=512)
            en = k.sb(ph, "en", [128, 1024], F32)
            hnew = [k.sb(ph, f"hnew{i}", [128, 1024], F32) for i in range(2)]
            yo = [k.sb(ph, f"yo{i}", [128, 1024], F32) for i in range(2)] if layer == 1 else None

            def loadsA(t):
                for n, wd, dt in namesA:
                    k.dma("sp", ldA[t % 2][n](), self.scr(n, t))

            def loadsB(t):
                b = ldB[t % 2]
                for n, wd, dt in namesB:
                    k.dma("sp", b[n](), self.scr(n, t))
                k.dma("sp", b["h"](), self.hsrc(si, layer, t))
                k.dma("sp", b["p"](), self.wv(self.pin[si][layer, t * 128:(t + 1) * 128, :]))

            def rstd(dst, src, n):
                act(k, dst, src, AF.Ln, scale=1.0 / n, bias=eps(0, 1))
                act(k, dst, dst, AF.Exp, scale=-0.5)

            def stageA(t):
                b = ldA[t % 2]
                o, y = o_[t % 2], y_[t % 2]
                if t % BLK == BLK - 1 and t < NT - 1:
                    ts(k, "dve", Sb32(), Sb32(), flg(0, 1), None, ALU.mult)
                    cp(k, "act", Sbbf(), Sb32())
                    ts(k, "dve", hb32(), hb32(), flg(0, 1), None, ALU.mult)
                    cp(k, "act", hbbf(), hb32())
                for half in range(2):
                    mms(k, [(PA.t[:, hd * 256:(hd + 1) * 256], b["qeb"].t[:, hd * 128:(hd + 1) * 128],
                             Sbbf.t[:, hd * 256:(hd + 1) * 256], True, True) for hd in (2 * half, 2 * half + 1)],
                        [PA(half * 512, (half + 1) * 512)], [b["qeb"](), Sbbf()])
                tt(k, "dve", o(), PA(), b["op"](), ALU.add)
                yield
                for half in range(2):
                    mms(k, [(PA.t[:, hd * 256:(hd + 1) * 256], b["ktb"].t[:, hd * 128:(hd + 1) * 128],
                             b["v"].t[:, hd * 256:(hd + 1) * 256], True, True) for hd in (2 * half, 2 * half + 1)],
                        [PA(half * 512, (half + 1) * 512)], [b["ktb"](), b["v"]()])
                for hd in range(4):
                    stt(k, Sb32(hd * 256, (hd + 1) * 256), Sb32(hd * 256, (hd + 1) * 256), b["elb"](hd, hd + 1),
                        PA(hd * 256, (hd + 1) * 256), ALU.mult, ALU.add)
                cp(k, "act", Sbbf(), Sb32())
                yield
                for half in range(2):
                    mms(k, [(PA.t[:, g * 256:(g + 1) * 256], b["c"].t[:, g * 128:(g + 1) * 128],
                             hbbf.t[:, g * 256:(g + 1) * 256], True, True) for g in (2 * half, 2 * half + 1)],
                        [PA(half * 512, (half + 1) * 512)], [b["c"](), hbbf()])
                tt(k, "dve", tmpA().r("p (h d) -> p h d", h=16), PA().r("p (h d) -> p h d", h=16),
                   b["sm"](0, 16).bc(2, [128, 16, 64]), ALU.mult)
                tt(k, "dve", y(), tmpA(), b["yp"](), ALU.add)
                yield
                for half in range(2):
                    mms(k, [(PA.t[:, g * 256:(g + 1) * 256], b["btm"].t[:, g * 128:(g + 1) * 128],
                             b["xtb"].t[:, g * 256:(g + 1) * 256], True, True) for g in (2 * half, 2 * half + 1)],
                        [PA(half * 512, (half + 1) * 512)], [b["btm"](), b["xtb"]()])
                tt(k, "dve", hb32().r("p (h d) -> p h d", h=16), hb32().r("p (h d) -> p h d", h=16),
                   b["sm"](16, 32).bc(2, [128, 16, 64]), ALU.mult)
                tt(k, "dve", hb32(), hb32(), PA(), ALU.add)
                cp(k, "act", hbbf(), hb32())

            def stageB(t, it):
                b = ldB[t % 2]
                o, y = o_[t % 2], y_[t % 2]
                for hd in range(4):
                    act(k, junk(0, 256), o(hd * 256, (hd + 1) * 256), AF.Square, accum=st(hd, hd + 1))
                rstd(st(4, 8), st(0, 4), 256)
                for hd in range(4):
                    stt(k, mix(hd * 256, (hd + 1) * 256), o(hd * 256, (hd + 1) * 256), st(4 + hd, 5 + hd),
                        b["sg"](hd * 256, (hd + 1) * 256), ALU.mult, ALU.mult)
                yield
                tt(k, "dve", y(), y(), b["sz"](), ALU.mult)
                act(k, junk(), y(), AF.Square, accum=st(8, 9))
                rstd(st(12, 13), st(8, 9), 1024)
                stt(k, mix(1024, 2048), y(), st(12, 13), gssd(), ALU.mult, ALU.mult)
                yield
                for half in range(2):
                    trs(k, [(PT.t[:, cc * 128:(cc + 1) * 128], mix.t[:, cc * 128:(cc + 1) * 128], c["ident"].t[:, :])
                            for cc in range(half * 8, half * 8 + 8)],
                        [PT(half * 1024, (half + 1) * 1024)], [mix(half * 1024, (half + 1) * 1024), c["ident"]()])
                    cp(k, "act" if half == 0 else "dve", mixT(half * 1024, (half + 1) * 1024),
                       PT(half * 1024, (half + 1) * 1024))
                yield
                for half in range(2):
                    mms(k, [(PB.t[:, half * 512:(half + 1) * 512], mixT.t[:, cc * 128:(cc + 1) * 128],
                             Wo.t[:, cc * 1024 + half * 512:cc * 1024 + (half + 1) * 512], cc == 0, cc == 15)
                            for cc in range(16)], [PB(half * 512, (half + 1) * 512)], [mixT(), Wo()])
                tt(k, "dve", hmid(), PB(), b["h"](), ALU.add)
                cp(k, "act", hmbf(), hmid())
                cp(k, "pool", pbf(), b["p"]())
                yield
                trs(k, [(PT.t[:, cc * 128:(cc + 1) * 128], hmbf.t[:, cc * 128:(cc + 1) * 128], c["ident"].t[:, :])
                        for cc in range(8)], [PT(0, 1024)], [hmbf(), c["ident"]()])
                cp(k, "act", hmT(), PT(0, 1024))
                trs(k, [(PT.t[:, 1024 + cc * 128:1024 + (cc + 1) * 128], pbf.t[:, cc * 128:(cc + 1) * 128], c["ident"].t[:, :])
                        for cc in range(2)], [PT(1024, 2048)], [pbf(), c["ident"]()])
                cp(k, "dve", pT(), PT(1024, 1280))
                yield
                for half in range(2):
                    mms(k, [(PC.t[:, half * 512:(half + 1) * 512], hmT.t[:, cc * 128:(cc + 1) * 128],
                             Wg.t[:, cc * 1024 + half * 512:cc * 1024 + (half + 1) * 512], cc == 0, cc == 7)
                            for cc in range(8)], [PC(half * 512, (half + 1) * 512)], [hmT(), Wg()])
                    eh = eg(half * 512, (half + 1) * 512)
                    act(k, eh, PC(half * 512, (half + 1) * 512), AF.Exp, scale=-1.0)
                    act(k, eh, eh, AF.Ln, bias=1.0)
                    act(k, eh, eh, AF.Exp, scale=-1.0)
                for half in range(2):
                    mms(k, [(PB.t[:, half * 512:(half + 1) * 512], pT.t[:, cc * 128:(cc + 1) * 128],
                             Wp.t[:, cc * 1024 + half * 512:cc * 1024 + (half + 1) * 512], cc == 0, cc == 1)
                            for cc in range(2)], [PB(half * 512, (half + 1) * 512)], [pT(), Wp()])
                act(k, junk(), PB(), AF.Square, accum=st(16, 17))
                rstd(st(20, 21), st(16, 17), 1024)
                stt(k, en(), PB(), st(20, 21), gple(), ALU.mult, ALU.mult)
                yield
                tt(k, "dve", en(), en(), eg(), ALU.mult)
                hn = hnew[it % 2]
                tt(k, "dve", hn(), hmid(), en(), ALU.add)
                if layer == 0:
                    k.dma("sp", self.scr("hs", t), hn(), sembuf=hn.bufs[0])
                else:
                    act(k, junk(), hn(), AF.Square, accum=st(24, 25))
                    rstd(st(28, 29), st(24, 25), 1024)
                    yy = yo[it % 2]
                    stt(k, yy(), hn(), st(28, 29), gfin(), ALU.mult, ALU.mult)
                    k.dma("sp", V(self.yout[si][t * 128:(t + 1) * 128, :], [Buf("yout")]), yy(), sembuf=yy.bufs[0])

            loadsA(NT - 1)
            loadsB(NT - 1)
            if NT > 1:
                loadsA(NT - 2)
            run_gen(stageA(NT - 1))
            for it, t in enumerate(range(NT - 1, -1, -1)):
                if t - 1 >= 0:
                    loadsB(t - 1)
                if t - 2 >= 0:
                    loadsA(t - 2)
                merge(stageB(t, it), stageA(t - 1) if t - 1 >= 0 else None, pattern="BBABABABBAB")
            k.barrier()


def run_gen(g):
    for _ in g:
        pass


def merge(gb, ga, pattern=None):
    if pattern is not None and ga is not None:
        live = {"B": gb, "A": ga}
        for ch in pattern:
            g = live.get(ch)
            if g is None:
                continue
            try:
                next(g)
            except StopIteration:
                live[ch] = None
        gens = [g for g in (live["B"], live["A"]) if g is not None]
    else:
        gens = [g for g in (gb, ga) if g is not None]
    while gens:
        for g in list(gens):
            try:
                next(g)
            except StopIteration:
                gens.remove(g)


_CACHE = {}


def get_prog(seqs):
    key = tuple(seqs)
    if key not in _CACHE:
        _CACHE[key] = Prog(list(seqs))
    return _CACHE[key]


WNAMES = ["norm_g", "w_in", "w_gla_gate", "b_gla_gate", "gla_onorm_g", "conv_w", "conv_b", "dt_bias", "a_log",
          "d_skip", "ssd_norm_g", "w_out", "w_ple_gate", "w_ple_proj", "ple_norm_g", "final_norm_g"]


def run(seqs, per_core, weights, flags=None):
    prog = get_prog(seqs)
    if flags is None:
        flags = [1.0] * len(per_core)
    in_maps = []
    for core in per_core:
        m = {n: np.ascontiguousarray(weights[n], dtype=np.float32) for n in WNAMES}
        for i, (x, p) in enumerate(core):
            m[f"x{i}"] = np.ascontiguousarray(x, dtype=np.float32)
            m[f"p{i}"] = np.ascontiguousarray(p, dtype=np.float32)
        m["flg"] = np.asarray([flags[len(in_maps)]], dtype=np.float32)
        in_maps.append(m)
    res = run_bass_kernel_spmd(prog.nc, in_maps, core_ids=list(range(len(per_core))))
    return [[r[f"y{i}"] for i in range(len(seqs))] for r in res.results]


def kernel(x_prompt, x_sample, p_prompt, p_sample, **weights):
    x_prompt = np.asarray(x_prompt); x_sample = np.asarray(x_sample)
    p_prompt = np.asarray(p_prompt); p_sample = np.asarray(p_sample)
    Lp, Ls = x_prompt.shape[1], x_sample.shape[1]
    nb = x_sample.shape[0]
    assert nb * Ls == Lp and Ls == BLK * 128
    zx = np.zeros((Lp, D), np.float32)
    zp = np.zeros((2, Lp, 256), np.float32)
    xs = x_sample.reshape(nb * Ls, D)
    ps = p_sample.reshape(2, nb * Ls, 256)
    per_core, flags = [], []
    for cidx in range(8):
        if cidx == 0:
            per_core.append([(x_prompt[0], p_prompt[:, 0])]); flags.append(1.0)
        elif cidx == 4:
            per_core.append([(x_prompt[1], p_prompt[:, 1])]); flags.append(1.0)
        elif cidx == 2:
            per_core.append([(xs, ps)]); flags.append(0.0)
        else:
            per_core.append([(zx, zp)]); flags.append(0.0)
    outs = run([Lp], per_core, weights, flags)
    y_prompt = np.stack([outs[0][0], outs[4][0]], axis=0).astype(np.float32)
    y_sample = outs[2][0].reshape(nb, Ls, D).astype(np.float32)
    return (y_prompt, y_sample)
```

```python
import numpy as np
from contextlib import ExitStack
import concourse.bass as bass
import concourse.mybir as mybir
from concourse.bass_utils import run_bass_kernel_spmd

F32 = mybir.dt.float32
BF16 = mybir.dt.bfloat16
ALU = mybir.AluOpType
AF = mybir.ActivationFunctionType

D = 1024
NIN = 6208
EPS = 1e-6
TAU = 16.0
QSCALE = 128 ** -0.5
DEBUG = False
PHASES = "12b"
BLK = 16
STOP = 0
VAR = 0


class Buf:
    __slots__ = ("name", "lw", "rd", "dsem", "excl")

    def __init__(self, name):
        self.name = name
        self.excl = False
        self.lw = None
        self.rd = {}
        self.dsem = None


class DSem:
    def __init__(self, name, h):
        self.name, self.h, self.cnt = name, h, 0


class V:
    def __init__(self, ap, bufs):
        self.ap, self.bufs = ap, tuple(bufs)

    def r(self, pat, **kw):
        return V(self.ap.rearrange(pat, **kw), self.bufs)

    def __getitem__(self, key):
        return V(self.ap[key], self.bufs)

    def bc(self, axis, shape):
        return V(self.ap.unsqueeze(axis).broadcast_to(list(shape)), self.bufs)


class Tile:
    def __init__(self, t, name, ncols, seg):
        self.t, self.name = t, name
        self.ncols = ncols
        self.seg = seg or ncols
        n = (ncols + self.seg - 1) // self.seg
        self.bufs = [Buf(f"{name}.{i}") for i in range(n)]

    def __call__(self, c0=None, c1=None, p0=None, p1=None):
        ncols = self.ncols
        a = 0 if c0 is None else c0
        b = ncols if c1 is None else c1
        bufs = self.bufs[a // self.seg:(b - 1) // self.seg + 1]
        if p0 is None:
            ap = self.t[:, a:b]
        else:
            ap = self.t[p0:p1, a:b]
        return V(ap, bufs)


class K:
    def __init__(self, nc, st):
        self.nc, self.st = nc, st
        self.E = {"pe": nc.tensor, "act": nc.scalar, "dve": nc.vector, "pool": nc.gpsimd, "sp": nc.sync}
        self.sem = {k: st.enter_context(nc.semaphore("s_" + k)) for k in self.E}
        self.cnt = {k: 0 for k in self.E}
        self.waited = {k: {} for k in self.E}
        self.dpool = [DSem(f"d{i}", st.enter_context(nc.semaphore(f"d{i}"))) for i in range(56)]
        self.dfree = list(self.dpool)
        self.swpool = [DSem(f"w{i}", st.enter_context(nc.semaphore(f"w{i}"))) for i in range(6)]
        self.swfree = list(self.swpool)
        self.ring = [DSem(f"r{i}", st.enter_context(nc.semaphore(f"r{i}"))) for i in range(4)]
        self.uid = 0
        self.dummy = self.sb(st, "dummy", [128, 8], F32)

    def sb(self, ph, name, shape, dt, seg=None):
        self.uid += 1
        t = ph.enter_context(self.nc.sbuf_tensor(f"{name}_{self.uid}", list(shape), dt))
        ncols = int(np.prod(shape[1:]))
        return Tile(t, name, ncols, seg)

    def ps(self, ph, name, shape, dt, seg=None):
        self.uid += 1
        t = ph.enter_context(self.nc.psum_tensor(f"{name}_{self.uid}", list(shape), dt))
        tl = Tile(t, name, shape[1], seg)
        for b in tl.bufs:
            b.excl = True
        return tl

    def _sync(self, e, reads, writes):
        deps = {}

        def add(ev):
            if ev is not None:
                (sn, h), v = ev
                if deps.get(sn, (None, 0))[1] < v:
                    deps[sn] = (h, v)

        for b in reads:
            add(b.lw)
        for b in writes:
            add(b.lw)
            for ev in b.rd.values():
                add(ev)
        w = self.waited[e]
        for sn, (h, v) in deps.items():
            if e == "pe" and sn == "s_pe":
                continue
            if w.get(sn, 0) < v:
                self.E[e].wait_ge(h, v)
                w[sn] = v

    def _mark(self, ev, reads, writes):
        for b in writes:
            b.lw = ev
            b.rd = {}
        for b in reads:
            b.rd[ev[0][0]] = ev

    def op(self, e, fn, outs, ins):
        writes = [b for v in outs for b in v.bufs]
        reads = [b for v in ins for b in v.bufs]
        writes += [b for b in reads if b.excl]
        reads = [b for b in reads if not b.excl]
        self._sync(e, reads, writes)
        r = fn(self.E[e])
        last = r[-1] if isinstance(r, (list, tuple)) else r
        self.cnt[e] += 1
        last.then_inc(self.sem[e], 1)
        ev = (("s_" + e, self.sem[e]), self.cnt[e])
        self._mark(ev, reads, writes)

    def dma(self, q, out, in_, sembuf=None):
        writes = list(out.bufs)
        reads = list(in_.bufs)
        self._sync(q, reads, writes)
        sb_ = sembuf if sembuf is not None else out.bufs[0]
        if sb_.dsem is None:
            sb_.dsem = self.swfree.pop() if q == "pool" else self.dfree.pop()
            if DEBUG:
                print("DSEM", sb_.name, sb_.dsem.name, q)
        ds = sb_.dsem
        ds.cnt += 16
        self.E[q].dma_start(out=out.ap, in_=in_.ap).then_inc(ds.h, 16)
        ev = ((ds.name, ds.h), ds.cnt)
        self._mark(ev, reads, writes)

    def dma_multi(self, q, items, outv):
        writes = list(outv.bufs)
        self._sync(q, [], writes)
        w = self.waited[q]
        for idx, (o, i) in enumerate(items):
            ds = self.ring[idx % len(self.ring)]
            if w.get(ds.name, 0) < ds.cnt:
                self.E[q].wait_ge(ds.h, ds.cnt)
                w[ds.name] = ds.cnt
            ds.cnt += 16
            self.E[q].dma_start(out=o, in_=i).then_inc(ds.h, 16)
        for ds in self.ring:
            if w.get(ds.name, 0) < ds.cnt:
                self.E[q].wait_ge(ds.h, ds.cnt)
                w[ds.name] = ds.cnt
        self.cnt[q] += 1
        self.E[q].memset(self.dummy.t[:, :], 0.0).then_inc(self.sem[q], 1)
        self._mark((("s_" + q, self.sem[q]), self.cnt[q]), [], writes)

    def barrier(self, release=()):
        for e in self.E:
            w = self.waited[e]
            for e2 in self.E:
                if e2 != e and w.get("s_" + e2, 0) < self.cnt[e2]:
                    self.E[e].wait_ge(self.sem[e2], self.cnt[e2])
                    w["s_" + e2] = self.cnt[e2]
            for ds in self.dpool + self.ring + self.swpool:
                if ds.cnt and w.get(ds.name, 0) < ds.cnt:
                    self.E[e].wait_ge(ds.h, ds.cnt)
                    w[ds.name] = ds.cnt
        self.dfree = list(self.dpool)
        self.swfree = list(self.swpool)


def tt(k, e, out, a, b, op):
    k.op(e, lambda g: g.tensor_tensor(out=out.ap, in0=a.ap, in1=b.ap, op=op), [out], [a, b])


def ts(k, e, out, a, s1, s2, op0, op1=None):
    ins = [a] + [s for s in (s1, s2) if isinstance(s, V)]
    a1 = s1.ap if isinstance(s1, V) else s1
    a2 = s2.ap if isinstance(s2, V) else s2
    if op1 is None:
        k.op(e, lambda g: g.tensor_scalar(out=out.ap, in0=a.ap, scalar1=a1, scalar2=None, op0=op0), [out], ins)
    else:
        k.op(e, lambda g: g.tensor_scalar(out=out.ap, in0=a.ap, scalar1=a1, scalar2=a2, op0=op0, op1=op1), [out], ins)


def stt(k, out, a, s, b, op0, op1):
    ins = [a, b] + ([s] if isinstance(s, V) else [])
    sa = s.ap if isinstance(s, V) else s
    k.op("dve", lambda g: g.scalar_tensor_tensor(out=out.ap, in0=a.ap, scalar=sa, in1=b.ap, op0=op0, op1=op1),
         [out], ins)


def act(k, out, a, func, scale=1.0, bias=0.0, accum=None):
    ins = [a] + ([scale] if isinstance(scale, V) else []) + ([bias] if isinstance(bias, V) else [])
    sc = scale.ap if isinstance(scale, V) else scale
    bi = bias.ap if isinstance(bias, V) else bias
    outs = [out] + ([accum] if accum is not None else [])
    if accum is None:
        k.op("act", lambda g: g.activation(out=out.ap, in_=a.ap, func=func, bias=bi, scale=sc), outs, ins)
    else:
        k.op("act", lambda g: g.activation(out=out.ap, in_=a.ap, func=func, bias=bi, scale=sc,
                                           accum_out=accum.ap), outs, ins)


def recip(k, out, a):
    k.op("dve", lambda g: g.reciprocal(out=out.ap, in_=a.ap), [out], [a])


def cp(k, e, out, a):
    if e == "act":
        k.op("act", lambda g: g.copy(out=out.ap, in_=a.ap), [out], [a])
    else:
        k.op(e, lambda g: g.tensor_copy(out=out.ap, in_=a.ap), [out], [a])


def mset(k, e, out, val):
    k.op(e, lambda g: g.memset(out.ap, val), [out], [])


def mms(k, items, outs, ins):
    def fn(pe):
        r = None
        for (o, l, rr, s0, s1) in items:
            r = pe.matmul(o, l, rr, start=s0, stop=s1)
        return r
    k.op("pe", fn, outs, ins)


def trs(k, items, outs, ins):
    def fn(pe):
        r = None
        for (o, i, idn) in items:
            r = pe.transpose(o, i, idn)
        return r
    k.op("pe", fn, outs, ins)


class Prog:
    def __init__(self, seqs):
        self.seqs = seqs
        self.nc = nc = bass.Bass("TRN2", target_bir_lowering=False)
        dt_in = lambda n, s: nc.dram_tensor(n, list(s), F32, kind="ExternalInput").ap()
        self.xin, self.pin, self.yout = [], [], []
        for i, L in enumerate(seqs):
            self.xin.append(dt_in(f"x{i}", [L, D]))
            self.pin.append(dt_in(f"p{i}", [2, L, 256]))
            self.yout.append(nc.dram_tensor(f"y{i}", [L, D], F32, kind="ExternalOutput").ap())
        self.flg_in = dt_in("flg", [1])
        self.w = {
            "norm_g": dt_in("norm_g", [2, D]), "w_in": dt_in("w_in", [2, D, NIN]),
            "w_gla_gate": dt_in("w_gla_gate", [2, 2, 16, 512]), "b_gla_gate": dt_in("b_gla_gate", [2, 2, 512]),
            "gla_onorm_g": dt_in("gla_onorm_g", [2, 256]), "conv_w": dt_in("conv_w", [2, 5, 2048]),
            "conv_b": dt_in("conv_b", [2, 2048]), "dt_bias": dt_in("dt_bias", [2, 2, 16]),
            "a_log": dt_in("a_log", [2, 2, 16]), "d_skip": dt_in("d_skip", [2, 16]),
            "ssd_norm_g": dt_in("ssd_norm_g", [2, D]), "w_out": dt_in("w_out", [2, 2048, D]),
            "w_ple_gate": dt_in("w_ple_gate", [2, D, D]), "w_ple_proj": dt_in("w_ple_proj", [2, 256, D]),
            "ple_norm_g": dt_in("ple_norm_g", [2, D]), "final_norm_g": dt_in("final_norm_g", [D]),
        }
        T = max(seqs) // 128
        self.Tmax = T
        scr = lambda n, s, d: nc.dram_tensor(n, list(s), d, kind="Internal").ap()
        self.s = {
            "hs": scr("s_hs", [T, 128, D], F32),
            "qeb": scr("s_qeb", [T, 128, 512], BF16), "ktb": scr("s_ktb", [T, 128, 512], BF16),
            "v": scr("s_v", [T, 128, 1024], BF16), "sg": scr("s_sg", [T, 128, 1024], BF16),
            "op": scr("s_op", [T, 128, 1024], F32), "elb": scr("s_elb", [T, 128, 4], F32),
            "c": scr("s_c", [T, 128, 512], BF16), "btm": scr("s_btm", [T, 128, 512], BF16),
            "xtb": scr("s_xtb", [T, 128, 1024], BF16), "sz": scr("s_sz", [T, 128, 1024], BF16),
            "yp": scr("s_yp", [T, 128, 1024], F32), "sm": scr("s_sm", [T, 128, 32], F32),
        }
        self.sbuf_ = {n: [Buf(f"dram_{n}_{t}") for t in range(T)] for n in self.s}
        self.wbuf = Buf("weights")

        with ExitStack() as st:
            self.k = K(nc, st)
            for si, L in enumerate(seqs):
                for layer in range(2):
                    if "1" in PHASES:
                        self.phase_f1(si, L, layer)
                    if "2" in PHASES:
                        self.phase_f2(si, L, layer)
                    if "b" in PHASES:
                        self.phase_b(si, L, layer)
            self.k.barrier()

    def scr(self, name, t):
        return V(self.s[name][t], [self.sbuf_[name][t]])

    def wv(self, ap):
        return V(ap, [self.wbuf])

    def hsrc(self, si, layer, t):
        if layer == 0:
            return V(self.xin[si][t * 128:(t + 1) * 128, :], [self.wbuf])
        return self.scr("hs", t)

    def consts(self, ph, need):
        k = self.k
        c = {}
        ones = k.sb(ph, "ones", [128, 128], F32)
        mset(k, "pool", ones(), 1.0)
        c["ones"] = ones

        def affine(name, pattern, cm, cmp):
            t = k.sb(ph, name, [128, 128], F32)
            k.op("pool", lambda g: g.affine_select(out=t.t[:, :], in_=ones.t[:, :], pattern=pattern, compare_op=cmp,
                                                   fill=0.0, base=0, channel_multiplier=cm), [t()], [ones()])
            return t

        identf = affine("identf", [[-1, 128]], 1, ALU.is_equal)
        c["identf"] = identf
        ident = k.sb(ph, "ident", [128, 128], BF16)
        cp(k, "pool", ident(), identf())
        c["ident"] = ident
        if "masks" in need:
            c["LE"] = affine("mLE", [[1, 128]], -1, ALU.is_ge)
            c["GT"] = affine("mGT", [[-1, 128]], 1, ALU.is_gt)
            c["GE"] = affine("mGE", [[-1, 128]], 1, ALU.is_ge)
        onesr = k.sb(ph, "onesr", [1, 128], BF16)
        mset(k, "pool", onesr(), 1.0)
        c["onesr"] = onesr
        return c

    def load_bcast(self, ph, name, ap1d, n):
        t = self.k.sb(ph, name, [128, n], F32)
        self.k.dma("sp", t(), self.wv(ap1d.partition_broadcast(128)))
        return t

    def load_w(self, dst, src_ap, rows, ncols):
        k = self.k
        items = []
        for kc in range(rows // 128):
            for c0 in range(0, ncols, 1024):
                c1 = min(ncols, c0 + 1024)
                items.append((dst.t[:, kc * ncols + c0:kc * ncols + c1], src_ap[kc * 128:(kc + 1) * 128, c0:c1]))
        k.dma_multi("pool", items, dst())

    def front_load(self, fe, si, layer, t):
        self.k.dma("sp", fe["x"][t % 2](), self.hsrc(si, layer, t))

    def front(self, fe, si, layer, t, uT, off):
        k = self.k
        x = fe["x"][t % 2]
        ss = fe["ss"][t % 2]
        act(k, fe["junk"](), x(), AF.Square, accum=ss(0, 1))
        act(k, ss(1, 2), ss(0, 1), AF.Ln, scale=1.0 / D, bias=fe["eps"](0, 1))
        act(k, ss(2, 3), ss(1, 2), AF.Exp, scale=-0.5)
        u = fe["u"]
        stt(k, u(), x(), ss(2, 3), fe["g"](), ALU.mult, ALU.mult)
        pT = fe["pT"]
        idn = fe["ident"]
        trs(k, [(pT(kc * 128, (kc + 1) * 128).ap, u(kc * 128, (kc + 1) * 128).ap, idn().ap) for kc in range(8)],
            [pT(0, 1024)], [u(), idn()])
        cp(k, "act", uT().r("p (a b) -> p a b", a=8)[:, :, off:off + 128],
           pT(0, 1024).r("p (a b) -> p a b", a=8))

    def front_bufs(self, ph, c, layer, pT):
        k = self.k
        fe = {"x": [k.sb(ph, f"x{i}", [128, D], F32) for i in range(2)],
              "ss": [k.sb(ph, f"ss{i}", [128, 4], F32) for i in range(2)],
              "junk": k.sb(ph, "junk", [128, D], BF16), "u": k.sb(ph, "u", [128, D], BF16),
              "g": self.load_bcast(ph, "normg", self.w["norm_g"][layer], D),
              "pT": pT, "ident": c["ident"], "uTw": None}
        eps = k.sb(ph, "eps", [128, 1], F32)
        mset(k, "pool", eps(), EPS)
        fe["eps"] = eps
        return fe

    def phase_f1(self, si, L, layer):
        k = self.k
        NT = L // 128
        w = self.w
        with ExitStack() as ph:
            c = self.consts(ph, {"masks"})
            W = k.sb(ph, "W1", [128, 8 * 3104], BF16)
            self.load_w(W, w["w_in"][layer][:, 0:3104], D, 3104)
            Wv = lambda kc, c0, c1: W(kc * 3104 + c0, kc * 3104 + c1)
            wg = k.sb(ph, "wg", [32, 1024], BF16)
            mset(k, "pool", wg(), 0.0)
            k.dma("pool", wg(0, 512, 0, 16), self.wv(w["w_gla_gate"][layer, 0]))
            k.dma("pool", wg(512, 1024, 16, 32), self.wv(w["w_gla_gate"][layer, 1]))
            bgr = k.sb(ph, "bgr", [1, 1024], BF16)
            k.dma("pool", bgr(), self.wv(w["b_gla_gate"][layer].rearrange("(o a) b -> o (a b)", o=1)))
            gon = self.load_bcast(ph, "gon", w["gla_onorm_g"][layer], 256)
            flg = self.load_bcast(ph, "flg", self.flg_in, 1)
            PT = k.ps(ph, "PT", [128, 2048], BF16, seg=1024)
            PA = k.ps(ph, "PA", [128, 1024], F32, seg=512)
            PB = k.ps(ph, "PB", [128, 1024], F32, seg=512)
            PC = k.ps(ph, "PC", [128, 1024], F32, seg=512)
            fe = self.front_bufs(ph, c, layer, PT)
            uT = [k.sb(ph, f"uT{i}", [128, 8 * 128], BF16) for i in range(2)]
            qk = [k.sb(ph, f"qk{i}", [128, 1024], F32, seg=512) for i in range(2)]
            vtm_ = [k.sb(ph, f"vtm{i}", [128, 1024], BF16, seg=512) for i in range(2)]
            lrT_ = [k.sb(ph, f"lrT{i}", [32, 128], BF16) for i in range(2)]
            sg = k.sb(ph, "sg", [128, 1024], BF16)
            tmpA = k.sb(ph, "tmpA", [128, 1024], F32, seg=512)
            tmp = k.sb(ph, "tmpf", [128, 1024], F32)
            Lsp = k.sb(ph, "Lsp", [128, 1024], F32)
            cum = k.sb(ph, "cum", [128, 1024], F32, seg=512)
            rmask = k.sb(ph, "rmask", [128, 512], F32)
            mset(k, "pool", rmask(), 1.0)
            for hd in range(4):
                mset(k, "pool", rmask(hd * 128, hd * 128 + 1), 0.0)
            Ep = [k.sb(ph, f"Ep{d}", [128, 512], F32) for d in range(2)]
            Em = [k.sb(ph, f"Em{d}", [128, 512], F32) for d in range(2)]
            qe = [k.sb(ph, f"qe{d}", [128, 512], BF16) for d in range(2)]
            ke = [k.sb(ph, f"ke{d}", [128, 512], BF16) for d in range(2)]
            kt = [k.sb(ph, f"kt{d}", [128, 512], BF16) for d in range(2)]
            kttm = [k.sb(ph, f"kttm{d}", [128, 512], BF16) for d in range(2)]
            elb = k.sb(ph, "elb", [128, 4], F32)
            att = k.sb(ph, "att", [128, 1024], BF16, seg=512)
            opart = k.sb(ph, "opart", [128, 1024], F32)
            S32 = k.sb(ph, "S32", [128, 1024], F32)
            Sbf = k.sb(ph, "Sbf", [128, 1024], BF16)
            mset(k, "pool", S32(), 0.0)
            mset(k, "pool", Sbf(), 0.0)

            def stageA(t):
                u = uT[t % 2]
                u3 = lambda kc: u(kc * 128, (kc + 1) * 128)
                for qi, P in ((0, PA(0, 512)), (1, PA(512, 1024))):
                    items = []
                    for hd in range(4):
                        for kc in range(8):
                            items.append((P.ap[:, hd * 128:(hd + 1) * 128],
                                          Wv(kc, qi * 512 + hd * 128, qi * 512 + (hd + 1) * 128).ap,
                                          u3(kc).ap, kc == 0, kc == 7))
                    mms(k, items, [P], [W(), u()])
                    cp(k, "act", qk[t % 2](qi * 512, (qi + 1) * 512), P)
                    yield
                mms(k, [(PA(0, 128, 0, 32).ap, Wv(kc, 3072, 3104).ap, u3(kc).ap, kc == 0, kc == 7) for kc in range(8)],
                    [PA(0, 512)], [W(), u()])
                cp(k, "act", lrT_[t % 2](), PA(0, 128, 0, 32))
                vtm = vtm_[t % 2]
                for half in range(2):
                    P = PA((1 - half) * 512, (2 - half) * 512)
                    mms(k, [(P.ap, u3(kc).ap, Wv(kc, 1024 + half * 512, 1024 + (half + 1) * 512).ap, kc == 0, kc == 7)
                            for kc in range(8)], [P], [W(), u()])
                    cp(k, "act", vtm(half * 512, (half + 1) * 512), P)
                k.dma("sp", self.scr("v", t), vtm(), sembuf=vtm.bufs[0])
                yield
                for half in range(2):
                    P = PA((1 - half) * 512, (2 - half) * 512)
                    mms(k, [(P.ap, u3(kc).ap, Wv(kc, 2048 + half * 512, 2048 + (half + 1) * 512).ap, kc == 0, kc == 7)
                            for kc in range(8)], [P], [W(), u()])
                    th = tmpA(half * 512, (half + 1) * 512)
                    act(k, th, P, AF.Exp, scale=-1.0)
                    act(k, th, th, AF.Ln, bias=1.0)
                    act(k, th, th, AF.Exp, scale=-1.0)
                    tt(k, "dve", th, P, th, ALU.mult)
                yield
                tt(k, "dve", sg().r("p (h d) -> p h d", h=4), tmpA().r("p (h d) -> p h d", h=4),
                   gon().bc(1, [128, 4, 256]), ALU.mult)
                k.dma("sp", self.scr("sg", t), sg(), sembuf=sg.bufs[0])
                yield
                if t + 1 < NT:
                    self.front(fe, si, layer, t + 1, uT[(t + 1) % 2], 0)
                    if t + 2 < NT:
                        self.front_load(fe, si, layer, t + 2)

            def stageB(t):
                vtm = vtm_[t % 2]
                lrT = lrT_[t % 2]
                if t % BLK == 0 and t > 0:
                    ts(k, "dve", S32(), S32(), flg(0, 1), None, ALU.mult)
                    cp(k, "act", Sbf(), S32())
                qs = qk[t % 2](0, 512)
                ks_ = qk[t % 2](512, 1024)
                items = []
                for d in range(2):
                    for hd in range(4):
                        o = PB.t[:, (d * 4 + hd) * 128:(d * 4 + hd + 1) * 128]
                        items.append((o, wg.t[0:32, d * 512 + hd * 128:d * 512 + (hd + 1) * 128], lrT.t[0:32, :], True, False))
                        items.append((o, bgr.t[0:1, d * 512 + hd * 128:d * 512 + (hd + 1) * 128], c["onesr"].t[0:1, :], False, True))
                mms(k, items, [PB()], [wg(), lrT(), bgr(), c["onesr"]()])
                act(k, Lsp(), PB(), AF.Exp, scale=-1.0)
                act(k, Lsp(), Lsp(), AF.Ln, bias=1.0)
                yield
                for d in range(2):
                    k.op("dve", lambda g, d=d: g.tensor_tensor_scan(
                        out=cum.t[:, d * 512:(d + 1) * 512], data0=rmask.t[:, :], data1=Lsp.t[:, d * 512:(d + 1) * 512],
                        initial=0.0, op0=ALU.mult, op1=ALU.add), [cum(d * 512, (d + 1) * 512)], [rmask(), Lsp()])
                tt(k, "dve", tmp(0, 512), cum(512, 1024), Lsp(512, 1024), ALU.subtract)
                for hd in range(4):
                    ts(k, "dve", tmp(512 + hd * 128, 512 + (hd + 1) * 128), tmp(hd * 128, (hd + 1) * 128), -1.0,
                       cum(512 + hd * 128 + 127, 512 + hd * 128 + 128), ALU.mult, ALU.add)
                yield
                srcs = [cum(0, 512), tmp(512, 1024)]
                for d in range(2):
                    act(k, Ep[d](), srcs[d], AF.Exp, scale=-1.0 / TAU)
                    act(k, Em[d](), srcs[d], AF.Exp, scale=1.0 / TAU)
                for d in range(2):
                    stt(k, qe[d](), qs, QSCALE, Ep[d](), ALU.mult, ALU.mult)
                    tt(k, "dve", ke[d](), ks_, Em[d](), ALU.mult)
                    for hd in range(4):
                        col = hd * 128 + (127 if d == 0 else 0)
                        ts(k, "dve", kt[d](hd * 128, (hd + 1) * 128), ke[d](hd * 128, (hd + 1) * 128),
                           Ep[d](col, col + 1), None, ALU.mult)
                    yield
                cp(k, "pool", elb().r("p (h o) -> p h o", h=4), Ep[1]().r("p (h t) -> p h t", h=4)[:, :, 0:1])
                k.dma("sp", self.scr("elb", t), elb(), sembuf=elb.bufs[0])
                k.dma("sp", self.scr("qeb", t), qe[1](), sembuf=qe[1].bufs[0])
                PT1 = PT(1024, 2048)
                trs(k, [(PT.t[:, 1024 + (d * 4 + hd) * 128:1024 + (d * 4 + hd + 1) * 128],
                         kt[d].t[:, hd * 128:(hd + 1) * 128], c["ident"].t[:, :]) for d in range(2) for hd in range(4)],
                    [PT1], [kt[0](), kt[1](), c["ident"]()])
                cp(k, "act", kttm[0](), PT(1024, 1536))
                cp(k, "act", kttm[1](), PT(1536, 2048))
                k.dma("sp", self.scr("ktb", t), kttm[1](), sembuf=kttm[1].bufs[0])
                yield
                items = []
                for d in range(2):
                    for hd in range(4):
                        items.append((PB.t[:, (d * 4 + hd) * 128:(d * 4 + hd + 1) * 128],
                                      ke[d].t[:, hd * 128:(hd + 1) * 128], qe[d].t[:, hd * 128:(hd + 1) * 128], True, True))
                mms(k, items, [PB()], [ke[0](), ke[1](), qe[0](), qe[1]()])
                for d, mk in ((0, c["LE"]), (1, c["GT"])):
                    tt(k, "dve", att(d * 512, (d + 1) * 512).r("p (h t) -> p h t", h=4),
                       PB(d * 512, (d + 1) * 512).r("p (h t) -> p h t", h=4), mk().bc(1, [128, 4, 128]), ALU.mult)
                yield
                for half in range(2):
                    items = []
                    for hd in (2 * half, 2 * half + 1):
                        o = PC.t[:, hd * 256:(hd + 1) * 256]
                        vv = vtm.t[:, hd * 256:(hd + 1) * 256]
                        items.append((o, att.t[:, hd * 128:(hd + 1) * 128], vv, True, False))
                        items.append((o, att.t[:, 512 + hd * 128:512 + (hd + 1) * 128], vv, False, False))
                        items.append((o, qe[0].t[:, hd * 128:(hd + 1) * 128], Sbf.t[:, hd * 256:(hd + 1) * 256], False, True))
                    mms(k, items, [PC(half * 512, (half + 1) * 512)], [att(), vtm(), qe[0](), Sbf()])
                cp(k, "act", opart(), PC())
                k.dma("sp", self.scr("op", t), opart(), sembuf=opart.bufs[0])
                yield
                for half in range(2):
                    items = []
                    for hd in (2 * half, 2 * half + 1):
                        items.append((PC.t[:, hd * 256:(hd + 1) * 256], kttm[0].t[:, hd * 128:(hd + 1) * 128],
                                      vtm.t[:, hd * 256:(hd + 1) * 256], True, True))
                    mms(k, items, [PC(half * 512, (half + 1) * 512)], [kttm[0](), vtm()])
                for hd in range(4):
                    stt(k, S32(hd * 256, (hd + 1) * 256), S32(hd * 256, (hd + 1) * 256),
                        Ep[0](hd * 128 + 127, hd * 128 + 128), PC(hd * 256, (hd + 1) * 256), ALU.mult, ALU.add)
                cp(k, "act", Sbf(), S32())

            self.front_load(fe, si, layer, 0)
            if NT > 1:
                self.front_load(fe, si, layer, 1)
            self.front(fe, si, layer, 0, uT[0], 0)
            run_gen(stageA(0))
            for t in range(NT):
                merge(stageB(t), stageA(t + 1) if t + 1 < NT else None, pattern="ABABABABABABBB")
            k.barrier()

    def phase_f2(self, si, L, layer):
        k = self.k
        NT = L // 128
        w = self.w
        with ExitStack() as ph:
            c = self.consts(ph, {"masks"})
            W = k.sb(ph, "W2", [128, 8 * 3104], BF16)
            self.load_w(W, w["w_in"][layer][:, 3104:6208], D, 3104)
            PT = k.ps(ph, "PT", [128, 2048], BF16, seg=1024)
            PA = k.ps(ph, "PA", [128, 1024], F32, seg=512)
            PB = k.ps(ph, "PB", [128, 1024], F32, seg=512)
            PC = k.ps(ph, "PC", [128, 1024], F32, seg=512)
            cwr = k.sb(ph, "cwr", [80, 128], F32)
            k.dma("sp", cwr(), self.wv(w["conv_w"][layer].rearrange("j (b p) -> (j b) p", p=128)))
            trs(k, [(PC.t[:, 0:80], cwr.t[0:80, :], c["identf"].t[0:80, 0:80])], [PC(0, 512)], [cwr(), c["identf"]()])
            cw = k.sb(ph, "cw", [128, 80], F32)
            cp(k, "act", cw(), PC(0, 80))
            diag = k.sb(ph, "diag", [128, 80 * 128], BF16)
            for i in range(80):
                ts(k, "dve", diag(i * 128, (i + 1) * 128), c["identf"](), cw(i, i + 1), None, ALU.mult)
            cbr = k.sb(ph, "cbr", [1, 2048], BF16)
            k.dma("pool", cbr(0, 1024), self.wv(w["conv_b"][layer:layer + 1, 0:1024]))
            k.dma("pool", cbr(1024, 2048), self.wv(w["conv_b"][layer:layer + 1, 1024:2048]))
            lexp = k.sb(ph, "lexp", [128, 2048], BF16)
            sel = k.sb(ph, "sel", [32, 32 * 128], BF16)
            mset(k, "pool", lexp(), 1.0)
            for hf in range(2):
                k.op("pool", lambda g, hf=hf: g.affine_select(
                    out=sel.t[:, hf * 2048:(hf + 1) * 2048].rearrange("p (a b) -> p a b", a=16),
                    in_=lexp.t[0:32, :].rearrange("p (a b) -> p a b", a=16),
                    pattern=[[-1, 16], [0, 128]], compare_op=ALU.is_equal, fill=0.0, base=-16 * hf,
                    channel_multiplier=1), [sel()], [lexp()])
            dtb = self.load_bcast(ph, "dtb", w["dt_bias"][layer].rearrange("a b -> (a b)"), 32)
            Abc = self.load_bcast(ph, "Abc", w["a_log"][layer].rearrange("a b -> (a b)"), 32)
            act(k, Abc(), Abc(), AF.Exp)
            ts(k, "dve", Abc(), Abc(), -1.0, None, ALU.mult)
            Dbc = self.load_bcast(ph, "Dbc", w["d_skip"][layer], 16)
            flg = self.load_bcast(ph, "flg", self.flg_in, 1)
            fe = self.front_bufs(ph, c, layer, PT)
            uT = [k.sb(ph, f"uT{i}", [128, 8 * 132], BF16) for i in range(2)]
            tmpA = k.sb(ph, "tmpA", [128, 1024], F32, seg=512)
            tmp = k.sb(ph, "tmpf", [128, 1024], F32)
            sz = k.sb(ph, "sz", [128, 1024], BF16)
            raw = k.sb(ph, "raw", [128, 16 * 132], BF16, seg=3 * 132)
            xc_ = [k.sb(ph, f"xc{i}", [128, 1024], BF16) for i in range(2)]
            bcf_ = [k.sb(ph, f"bcf{i}", [128, 1024], BF16) for i in range(2)]
            draw_ = [k.sb(ph, f"draw{i}", [128, 32], F32) for i in range(2)]
            sm = k.sb(ph, "sm", [128, 32], F32)
            dts_ = [k.sb(ph, f"dts{i}", [128, 32 * 8], F32, seg=32) for i in range(2)]
            dtaf_ = [k.sb(ph, f"dtaf{i}", [128, 64], F32) for i in range(2)]
            for i in range(2):
                mset(k, "pool", dtaf_[i](), 0.0)
            etot_ = [k.sb(ph, f"etot{i}", [128, 32], F32) for i in range(2)]
            cumfm = k.sb(ph, "cumfm", [32, 128], F32)
            chi_ = [k.sb(ph, f"chi{i}", [32, 128], BF16) for i in range(2)]
            clo_ = [k.sb(ph, f"clo{i}", [32, 128], BF16) for i in range(2)]
            xdt = [k.sb(ph, f"xdt{d}", [128, 1024], BF16) for d in range(2)]
            xtl = [k.sb(ph, f"xtl{d}", [128, 1024], BF16) for d in range(2)]
            yD = k.sb(ph, "yD", [128, 1024], F32)
            btm = k.sb(ph, "btm", [128, 512], BF16)
            cbm = [k.sb(ph, f"cbm{d}", [128, 512], BF16) for d in range(2)]
            seg = k.sb(ph, "seg", [128, 2048], F32)
            WT = [k.sb(ph, f"WT{d}", [128, 2048], BF16) for d in range(2)]
            h32 = k.sb(ph, "h32", [128, 1024], F32)
            hbf = k.sb(ph, "hbf", [128, 1024], BF16)
            mset(k, "pool", h32(), 0.0)
            mset(k, "pool", hbf(), 0.0)
            for i in range(2):
                mset(k, "pool", uT[i](), 0.0)

            self.front_load(fe, si, layer, 0)
            if NT > 1:
                self.front_load(fe, si, layer, 1)
            self.front(fe, si, layer, 0, uT[0], 2)
            if NT > 1:
                self.front(fe, si, layer, 1, uT[1], 2)
                if NT > 2:
                    self.front_load(fe, si, layer, 2)

            def stageA(t):
                u = uT[t % 2]
                un = uT[(t + 1) % 2]
                u3 = u().r("p (a b) -> p a b", a=8)
                un3 = un().r("p (a b) -> p a b", a=8)
                dts = dts_[t % 2]
                _, d_dt, d_dta, d_cum, d_tot, d_ecum, d_tail, d_dtt = [dts(i * 32, (i + 1) * 32) for i in range(8)]
                dtaf, etot, chi, clo = dtaf_[t % 2], etot_[t % 2], chi_[t % 2], clo_[t % 2]
                if t + 1 < NT:
                    if (t + 1) % BLK == 0:
                        ts(k, "pool", u3[:, :, 130:132], un3[:, :, 2:4], flg(0, 1), None, ALU.mult)
                        ts(k, "pool", un3[:, :, 0:2], u3[:, :, 128:130], flg(0, 1), None, ALU.mult)
                    else:
                        cp(k, "pool", u3[:, :, 130:132], un3[:, :, 2:4])
                        cp(k, "pool", un3[:, :, 0:2], u3[:, :, 128:130])
                else:
                    k.op("pool", lambda g: g.memset(u3.ap[:, :, 130:132], 0.0), [u()], [])
                yield
                ui = lambda kc: u.t[:, kc * 132 + 2:kc * 132 + 130]
                ue = lambda kc: u.t[:, kc * 132:(kc + 1) * 132]
                xc, bcf, d_raw = xc_[t % 2], bcf_[t % 2], draw_[t % 2]
                groups = [(0, 1, 2), (3, 4, 5), (6, 7, 8), (9, 10, 11), (12, 13, 14), (15,)]
                for gi, blks in enumerate(groups):
                    base = (gi % 2) * 512
                    items = []
                    for bi, blk in enumerate(blks):
                        o = PA.t[:, base + bi * 132:base + (bi + 1) * 132]
                        for kc in range(8):
                            items.append((o, W.t[:, kc * 3104 + 1024 + blk * 128:kc * 3104 + 1024 + (blk + 1) * 128],
                                          ue(kc), kc == 0, kc == 7))
                    if gi == 5:
                        for kc in range(8):
                            items.append((PA.t[:, base + 132:base + 164], ui(kc), W.t[:, kc * 3104 + 3072:kc * 3104 + 3104],
                                          kc == 0, kc == 7))
                    mms(k, items, [PA(base, base + 512)], [W(), u()])
                    if gi == 5:
                        tt(k, "dve", d_raw(), PA(base + 132, base + 164), dtb(), ALU.add)
                    cp(k, "act", raw(blks[0] * 132, (blks[-1] + 1) * 132), PA(base, base + len(blks) * 132))
                    if gi % 2 == 1:
                        yield
                for half in range(2):
                    P = PA(half * 512, (half + 1) * 512)
                    mms(k, [(P.ap, ui(kc), W.t[:, kc * 3104 + half * 512:kc * 3104 + (half + 1) * 512], kc == 0, kc == 7)
                            for kc in range(8)], [P], [W(), u()])
                    th = tmpA(half * 512, (half + 1) * 512)
                    act(k, th, P, AF.Exp, scale=-1.0)
                    act(k, th, th, AF.Ln, bias=1.0)
                    act(k, th, th, AF.Exp, scale=-1.0)
                    tt(k, "dve", sz(half * 512, (half + 1) * 512), P, th, ALU.mult)
                k.dma("sp", self.scr("sz", t), sz(), sembuf=sz.bufs[0])
                yield
                for r in range(4):
                    P = PA((r % 2) * 512, (r % 2 + 1) * 512)
                    dst = xc if r < 2 else bcf
                    items = []
                    for b in range(4):
                        blk = r * 4 + b
                        o = PA.t[:, (r % 2) * 512 + b * 128:(r % 2) * 512 + (b + 1) * 128]
                        for j in range(5):
                            items.append((o, diag.t[:, (j * 16 + blk) * 128:(j * 16 + blk + 1) * 128],
                                          raw.t[:, blk * 132 + j:blk * 132 + j + 128], j == 0, False))
                        items.append((o, cbr.t[0:1, blk * 128:(blk + 1) * 128], c["onesr"].t[0:1, :], False, True))
                    mms(k, items, [P], [diag(), raw(), cbr(), c["onesr"]()])
                    th = tmpA((r % 2) * 512, (r % 2 + 1) * 512)
                    act(k, th, P, AF.Exp, scale=-1.0)
                    act(k, th, th, AF.Ln, bias=1.0)
                    act(k, th, th, AF.Exp, scale=-1.0)
                    tt(k, "dve", dst((r % 2) * 512, (r % 2 + 1) * 512), P, th, ALU.mult)
                    if r % 2 == 1:
                        yield
                k.dma("sp", self.scr("c", t), bcf(512, 1024), sembuf=bcf.bufs[0])
                act(k, d_dt, d_raw(), AF.Exp)
                act(k, d_dt, d_dt, AF.Ln, bias=1.0)
                tt(k, "dve", d_dta, d_dt, Abc(), ALU.mult)
                cp(k, "dve", dtaf(0, 16), d_dta[:, 0:16])
                cp(k, "dve", dtaf(48, 64), d_dta[:, 16:32])
                mms(k, [(PA.t[:, 0:16], c["LE"].t[:, :], dts.t[:, 64:80], True, True),
                        (PA.t[:, 16:32], c["GE"].t[:, :], dts.t[:, 80:96], True, True),
                        (PA.t[:, 32:64], c["ones"].t[:, :], dts.t[:, 64:96], True, True),
                        (PA.t[0:32, 64:192], dtaf.t[:, 0:32], c["LE"].t[:, :], True, False),
                        (PA.t[0:32, 64:192], dtaf.t[:, 32:64], c["GE"].t[:, :], False, True)],
                    [PA(0, 512)], [c["LE"](), c["GE"](), c["ones"](), d_dta, dtaf()])
                cp(k, "act", d_cum, PA(0, 32))
                cp(k, "act", d_tot, PA(32, 64))
                cp(k, "act", cumfm(), PA(64, 192, 0, 32))
                cp(k, "act", chi(), cumfm())
                tt(k, "dve", clo(), cumfm(), chi(), ALU.subtract)
                act(k, d_ecum, d_cum, AF.Exp)
                tt(k, "dve", d_tail, d_tot, d_cum, ALU.subtract)
                act(k, d_tail, d_tail, AF.Exp)
                tt(k, "dve", d_dtt, d_dt, d_tail, ALU.mult)
                act(k, etot(), d_tot, AF.Exp)
                cp(k, "pool", sm(0, 16), d_ecum[:, 16:32])
                cp(k, "pool", sm(16, 32), etot(16, 32))
                k.dma("sp", self.scr("sm", t), sm(), sembuf=sm.bufs[0])
                yield
                if t + 2 < NT:
                    self.front(fe, si, layer, t + 2, u, 2)
                    if t + 3 < NT:
                        self.front_load(fe, si, layer, t + 3)

            def stageB(t):
                xc, bcf, d_raw = xc_[t % 2], bcf_[t % 2], draw_[t % 2]
                dts = dts_[t % 2]
                _, d_dt, d_dta, d_cum, d_tot, d_ecum, d_tail, d_dtt = [dts(i * 32, (i + 1) * 32) for i in range(8)]
                dtaf, etot, chi, clo = dtaf_[t % 2], etot_[t % 2], chi_[t % 2], clo_[t % 2]
                if t % BLK == 0 and t > 0:
                    ts(k, "dve", h32(), h32(), flg(0, 1), None, ALU.mult)
                    cp(k, "act", hbf(), h32())
                PT1 = PT(1024, 2048)
                trs(k, [(PT.t[:, 1024 + b * 128:1024 + (b + 1) * 128], xc.t[:, b * 128:(b + 1) * 128], c["ident"].t[:, :])
                        for b in range(8)], [PT1], [xc(), c["ident"]()])
                x3 = PT1.r("p (h d) -> p h d", h=16)
                for d in range(2):
                    tt(k, "dve", xdt[d]().r("p (h d) -> p h d", h=16), x3,
                       d_dt[:, d * 16:(d + 1) * 16].bc(2, [128, 16, 64]), ALU.mult)
                    tt(k, "dve", xtl[d]().r("p (h d) -> p h d", h=16), x3,
                       d_dtt[:, d * 16:(d + 1) * 16].bc(2, [128, 16, 64]), ALU.mult)
                    yield
                tt(k, "dve", yD().r("p (h d) -> p h d", h=16), x3, Dbc().bc(2, [128, 16, 64]), ALU.mult)
                k.dma("sp", self.scr("xtb", t), xtl[1](), sembuf=xtl[1].bufs[0])
                trs(k, [(PT.t[:, 1024 + g * 128:1024 + (g + 1) * 128], bcf.t[:, g * 128:(g + 1) * 128], c["ident"].t[:, :])
                        for g in range(4)], [PT1], [bcf(), c["ident"]()])
                cp(k, "act", btm(), PT(1024, 1536))
                k.dma("sp", self.scr("btm", t), btm(), sembuf=btm.bufs[0])
                mms(k, [(PC.t[:, 512 + g * 128:512 + (g + 1) * 128], bcf.t[:, g * 128:(g + 1) * 128],
                         bcf.t[:, (4 + g) * 128:(5 + g) * 128], True, True) for g in range(4)],
                    [PC(512, 1024)], [bcf()])
                for d, mk in ((0, c["LE"]), (1, c["GT"])):
                    tt(k, "dve", cbm[d]().r("p (g t) -> p g t", g=4), PC(512, 1024).r("p (g t) -> p g t", g=4),
                       mk().bc(1, [128, 4, 128]), ALU.mult)
                yield
                for d in range(2):
                    for r in range(4):
                        base = (r % 2) * 512
                        items = []
                        for hh in range(4):
                            h = r * 4 + hh
                            o = PB.t[:, base + hh * 128:base + (hh + 1) * 128]
                            s_ = sel.t[0:32, (d * 16 + h) * 128:(d * 16 + h + 1) * 128]
                            items.append((o, s_, chi.t[0:32, :], True, False))
                            items.append((o, s_, clo.t[0:32, :], False, True))
                        mms(k, items, [PB(base, base + 512)], [sel(), chi(), clo()])
                        for hh in range(4):
                            h = r * 4 + hh
                            ts(k, "dve", seg(h * 128, (h + 1) * 128), PB(base + hh * 128, base + (hh + 1) * 128),
                               d_cum[:, d * 16 + h:d * 16 + h + 1], 0.0, ALU.subtract, ALU.min)
                        if r % 2 == 1:
                            yield
                    act(k, lexp(), seg(), AF.Exp)
                    for g in range(4):
                        tt(k, "dve", WT[d](g * 512, (g + 1) * 512).r("p (h t) -> p h t", h=4),
                           lexp(g * 512, (g + 1) * 512).r("p (h t) -> p h t", h=4),
                           cbm[d](g * 128, (g + 1) * 128).bc(1, [128, 4, 128]), ALU.mult)
                    yield
                for half in range(2):
                    items = []
                    for h in range(half * 8, half * 8 + 8):
                        o = PC.t[:, h * 64:(h + 1) * 64]
                        items.append((o, WT[0].t[:, h * 128:(h + 1) * 128], xdt[0].t[:, h * 64:(h + 1) * 64], True, False))
                        items.append((o, WT[1].t[:, h * 128:(h + 1) * 128], xdt[1].t[:, h * 64:(h + 1) * 64], False, True))
                    mms(k, items, [PC(half * 512, (half + 1) * 512)], [WT[0](), WT[1](), xdt[0](), xdt[1]()])
                for half in range(2):
                    items = []
                    for g in (2 * half, 2 * half + 1):
                        items.append((PB.t[:, g * 256:(g + 1) * 256], bcf.t[:, (4 + g) * 128:(5 + g) * 128],
                                      hbf.t[:, g * 256:(g + 1) * 256], True, True))
                    mms(k, items, [PB(half * 512, (half + 1) * 512)], [bcf(), hbf()])
                tt(k, "dve", yD(), PC(), yD(), ALU.add)
                tt(k, "dve", tmp().r("p (h d) -> p h d", h=16), PB().r("p (h d) -> p h d", h=16),
                   d_ecum[:, 0:16].bc(2, [128, 16, 64]), ALU.mult)
                tt(k, "dve", yD(), yD(), tmp(), ALU.add)
                k.dma("sp", self.scr("yp", t), yD(), sembuf=yD.bufs[0])
                yield
                for half in range(2):
                    items = []
                    for g in (2 * half, 2 * half + 1):
                        items.append((PC.t[:, g * 256:(g + 1) * 256], btm.t[:, g * 128:(g + 1) * 128],
                                      xtl[0].t[:, g * 256:(g + 1) * 256], True, True))
                    mms(k, items, [PC(half * 512, (half + 1) * 512)], [btm(), xtl[0]()])
                tt(k, "dve", h32().r("p (h d) -> p h d", h=16), h32().r("p (h d) -> p h d", h=16),
                   etot(0, 16).bc(2, [128, 16, 64]), ALU.mult)
                tt(k, "dve", h32(), h32(), PC(), ALU.add)
                cp(k, "act", hbf(), h32())

            run_gen(stageA(0))
            for t in range(NT):
                merge(stageB(t), stageA(t + 1) if t + 1 < NT else None)
            k.barrier()

    def phase_b(self, si, L, layer):
        k = self.k
        NT = L // 128
        w = self.w
        with ExitStack() as ph:
            c = self.consts(ph, set())
            Wo = k.sb(ph, "Wo", [128, 16 * 1024], BF16)
            self.load_w(Wo, w["w_out"][layer], 2048, 1024)
            Wg = k.sb(ph, "Wpg", [128, 8 * 1024], BF16)
            self.load_w(Wg, w["w_ple_gate"][layer], 1024, 1024)
            Wp = k.sb(ph, "Wpp", [128, 2 * 1024], BF16)
            self.load_w(Wp, w["w_ple_proj"][layer], 256, 1024)
            gssd = self.load_bcast(ph, "gssd", w["ssd_norm_g"][layer], D)
            gple = self.load_bcast(ph, "gple", w["ple_norm_g"][layer], D)
            gfin = self.load_bcast(ph, "gfin", w["final_norm_g"], D) if layer == 1 else None
            eps = k.sb(ph, "eps", [128, 1], F32)
            mset(k, "pool", eps(), EPS)
            flg = self.load_bcast(ph, "flg", self.flg_in, 1)
            PT = k.ps(ph, "PT", [128, 2048], BF16, seg=1024)
            PA = k.ps(ph, "PA", [128, 1024], F32, seg=512)
            PB = k.ps(ph, "PB", [128, 1024], F32, seg=512)
            PC = k.ps(ph, "PC", [128, 1024], F32, seg=512)
            namesA = [("qeb", 512, BF16), ("ktb", 512, BF16), ("v", 1024, BF16), ("op", 1024, F32), ("elb", 4, F32),
                      ("c", 512, BF16), ("btm", 512, BF16), ("xtb", 1024, BF16), ("yp", 1024, F32), ("sm", 32, F32)]
            namesB = [("sg", 1024, BF16), ("sz", 1024, BF16)]
            ldA = [{n: k.sb(ph, f"l_{n}{i}", [128, wd], dt) for n, wd, dt in namesA} for i in range(2)]
            ldB = [{n: k.sb(ph, f"l_{n}{i}", [128, wd], dt) for n, wd, dt in namesB} for i in range(2)]
            for i in range(2):
                ldB[i]["h"] = k.sb(ph, f"l_h{i}", [128, D], F32)
                ldB[i]["p"] = k.sb(ph, f"l_p{i}", [128, 256], F32)
            Sb32 = k.sb(ph, "Sb32", [128, 1024], F32)
            Sbbf = k.sb(ph, "Sbbf", [128, 1024], BF16)
            hb32 = k.sb(ph, "hb32", [128, 1024], F32)
            hbbf = k.sb(ph, "hbbf", [128, 1024], BF16)
            for x_ in (Sb32, Sbbf, hb32, hbbf):
                mset(k, "pool", x_(), 0.0)
            o_ = [k.sb(ph, f"o{i}", [128, 1024], F32) for i in range(2)]
            y_ = [k.sb(ph, f"y{i}", [128, 1024], F32) for i in range(2)]
            tmpA = k.sb(ph, "tmpA", [128, 1024], F32)
            junk = k.sb(ph, "junkb", [128, 1024], BF16)
            st = k.sb(ph, "stat", [128, 32], F32, seg=4)
            mix = k.sb(ph, "mix", [128, 2048], BF16, seg=1024)
            mixT = k.sb(ph, "mixT", [128, 2048], BF16, seg=1024)
            hmid = k.sb(ph, "hmid", [128, 1024], F32)
            hmbf = k.sb(ph, "hmbf", [128, 1024], BF16)
            hmT = k.sb(ph, "hmT", [128, 1024], BF16)
            pbf = k.sb(ph, "pbf", [128, 256], BF16)
            pT = k.sb(ph, "pT", [128, 256], BF16)
            eg = k.sb(ph, "eg", [128, 1024], F32, seg=512)
            en = k.sb(ph, "en", [128, 1024], F32)
            hnew = [k.sb(ph, f"hnew{i}", [128, 1024], F32) for i in range(2)]
            yo = [k.sb(ph, f"yo{i}", [128, 1024], F32) for i in range(2)] if layer == 1 else None

            def loadsA(t):
                for n, wd, dt in namesA:
                    k.dma("sp", ldA[t % 2][n](), self.scr(n, t))

            def loadsB(t):
                b = ldB[t % 2]
                for n, wd, dt in namesB:
                    k.dma("sp", b[n](), self.scr(n, t))
                k.dma("sp", b["h"](), self.hsrc(si, layer, t))
                k.dma("sp", b["p"](), self.wv(self.pin[si][layer, t * 128:(t + 1) * 128, :]))

            def rstd(dst, src, n):
                act(k, dst, src, AF.Ln, scale=1.0 / n, bias=eps(0, 1))
                act(k, dst, dst, AF.Exp, scale=-0.5)

            def stageA(t):
                b = ldA[t % 2]
                o, y = o_[t % 2], y_[t % 2]
                if t % BLK == BLK - 1 and t < NT - 1:
                    ts(k, "dve", Sb32(), Sb32(), flg(0, 1), None, ALU.mult)
                    cp(k, "act", Sbbf(), Sb32())
                    ts(k, "dve", hb32(), hb32(), flg(0, 1), None, ALU.mult)
                    cp(k, "act", hbbf(), hb32())
                for half in range(2):
                    mms(k, [(PA.t[:, hd * 256:(hd + 1) * 256], b["qeb"].t[:, hd * 128:(hd + 1) * 128],
                             Sbbf.t[:, hd * 256:(hd + 1) * 256], True, True) for hd in (2 * half, 2 * half + 1)],
                        [PA(half * 512, (half + 1) * 512)], [b["qeb"](), Sbbf()])
                tt(k, "dve", o(), PA(), b["op"](), ALU.add)
                yield
                for half in range(2):
                    mms(k, [(PA.t[:, hd * 256:(hd + 1) * 256], b["ktb"].t[:, hd * 128:(hd + 1) * 128],
                             b["v"].t[:, hd * 256:(hd + 1) * 256], True, True) for hd in (2 * half, 2 * half + 1)],
                        [PA(half * 512, (half + 1) * 512)], [b["ktb"](), b["v"]()])
                for hd in range(4):
                    stt(k, Sb32(hd * 256, (hd + 1) * 256), Sb32(hd * 256, (hd + 1) * 256), b["elb"](hd, hd + 1),
                        PA(hd * 256, (hd + 1) * 256), ALU.mult, ALU.add)
                cp(k, "act", Sbbf(), Sb32())
                yield
                for half in range(2):
                    mms(k, [(PA.t[:, g * 256:(g + 1) * 256], b["c"].t[:, g * 128:(g + 1) * 128],
                             hbbf.t[:, g * 256:(g + 1) * 256], True, True) for g in (2 * half, 2 * half + 1)],
                        [PA(half * 512, (half + 1) * 512)], [b["c"](), hbbf()])
                tt(k, "dve", tmpA().r("p (h d) -> p h d", h=16), PA().r("p (h d) -> p h d", h=16),
                   b["sm"](0, 16).bc(2, [128, 16, 64]), ALU.mult)
                tt(k, "dve", y(), tmpA(), b["yp"](), ALU.add)
                yield
                for half in range(2):
                    mms(k, [(PA.t[:, g * 256:(g + 1) * 256], b["btm"].t[:, g * 128:(g + 1) * 128],
                             b["xtb"].t[:, g * 256:(g + 1) * 256], True, True) for g in (2 * half, 2 * half + 1)],
                        [PA(half * 512, (half + 1) * 512)], [b["btm"](), b["xtb"]()])
                tt(k, "dve", hb32().r("p (h d) -> p h d", h=16), hb32().r("p (h d) -> p h d", h=16),
                   b["sm"](16, 32).bc(2, [128, 16, 64]), ALU.mult)
                tt(k, "dve", hb32(), hb32(), PA(), ALU.add)
                cp(k, "act", hbbf(), hb32())

            def stageB(t, it):
                b = ldB[t % 2]
                o, y = o_[t % 2], y_[t % 2]
                for hd in range(4):
                    act(k, junk(0, 256), o(hd * 256, (hd + 1) * 256), AF.Square, accum=st(hd, hd + 1))
                rstd(st(4, 8), st(0, 4), 256)
                for hd in range(4):
                    stt(k, mix(hd * 256, (hd + 1) * 256), o(hd * 256, (hd + 1) * 256), st(4 + hd, 5 + hd),
                        b["sg"](hd * 256, (hd + 1) * 256), ALU.mult, ALU.mult)
                yield
                tt(k, "dve", y(), y(), b["sz"](), ALU.mult)
                act(k, junk(), y(), AF.Square, accum=st(8, 9))
                rstd(st(12, 13), st(8, 9), 1024)
                stt(k, mix(1024, 2048), y(), st(12, 13), gssd(), ALU.mult, ALU.mult)
                yield
                for half in range(2):
                    trs(k, [(PT.t[:, cc * 128:(cc + 1) * 128], mix.t[:, cc * 128:(cc + 1) * 128], c["ident"].t[:, :])
                            for cc in range(half * 8, half * 8 + 8)],
                        [PT(half * 1024, (half + 1) * 1024)], [mix(half * 1024, (half + 1) * 1024), c["ident"]()])
                    cp(k, "act" if half == 0 else "dve", mixT(half * 1024, (half + 1) * 1024),
                       PT(half * 1024, (half + 1) * 1024))
                yield
                for half in range(2):
                    mms(k, [(PB.t[:, half * 512:(half + 1) * 512], mixT.t[:, cc * 128:(cc + 1) * 128],
                             Wo.t[:, cc * 1024 + half * 512:cc * 1024 + (half + 1) * 512], cc == 0, cc == 15)
                            for cc in range(16)], [PB(half * 512, (half + 1) * 512)], [mixT(), Wo()])
                tt(k, "dve", hmid(), PB(), b["h"](), ALU.add)
                cp(k, "act", hmbf(), hmid())
                cp(k, "pool", pbf(), b["p"]())
                yield
                trs(k, [(PT.t[:, cc * 128:(cc + 1) * 128], hmbf.t[:, cc * 128:(cc + 1) * 128], c["ident"].t[:, :])
                        for cc in range(8)], [PT(0, 1024)], [hmbf(), c["ident"]()])
                cp(k, "act", hmT(), PT(0, 1024))
                trs(k, [(PT.t[:, 1024 + cc * 128:1024 + (cc + 1) * 128], pbf.t[:, cc * 128:(cc + 1) * 128], c["ident"].t[:, :])
                        for cc in range(2)], [PT(1024, 2048)], [pbf(), c["ident"]()])
                cp(k, "dve", pT(), PT(1024, 1280))
                yield
                for half in range(2):
                    mms(k, [(PC.t[:, half * 512:(half + 1) * 512], hmT.t[:, cc * 128:(cc + 1) * 128],
                             Wg.t[:, cc * 1024 + half * 512:cc * 1024 + (half + 1) * 512], cc == 0, cc == 7)
                            for cc in range(8)], [PC(half * 512, (half + 1) * 512)], [hmT(), Wg()])
                    eh = eg(half * 512, (half + 1) * 512)
                    act(k, eh, PC(half * 512, (half + 1) * 512), AF.Exp, scale=-1.0)
                    act(k, eh, eh, AF.Ln, bias=1.0)
                    act(k, eh, eh, AF.Exp, scale=-1.0)
                for half in range(2):
                    mms(k, [(PB.t[:, half * 512:(half + 1) * 512], pT.t[:, cc * 128:(cc + 1) * 128],
                             Wp.t[:, cc * 1024 + half * 512:cc * 1024 + (half + 1) * 512], cc == 0, cc == 1)
                            for cc in range(2)], [PB(half * 512, (half + 1) * 512)], [pT(), Wp()])
                act(k, junk(), PB(), AF.Square, accum=st(16, 17))
                rstd(st(20, 21), st(16, 17), 1024)
                stt(k, en(), PB(), st(20, 21), gple(), ALU.mult, ALU.mult)
                yield
                tt(k, "dve", en(), en(), eg(), ALU.mult)
                hn = hnew[it % 2]
                tt(k, "dve", hn(), hmid(), en(), ALU.add)
                if layer == 0:
                    k.dma("sp", self.scr("hs", t), hn(), sembuf=hn.bufs[0])
                else:
                    act(k, junk(), hn(), AF.Square, accum=st(24, 25))
                    rstd(st(28, 29), st(24, 25), 1024)
                    yy = yo[it % 2]
                    stt(k, yy(), hn(), st(28, 29), gfin(), ALU.mult, ALU.mult)
                    k.dma("sp", V(self.yout[si][t * 128:(t + 1) * 128, :], [Buf("yout")]), yy(), sembuf=yy.bufs[0])

            loadsA(NT - 1)
            loadsB(NT - 1)
            if NT > 1:
                loadsA(NT - 2)
            run_gen(stageA(NT - 1))
            for it, t in enumerate(range(NT - 1, -1, -1)):
                if t - 1 >= 0:
                    loadsB(t - 1)
                if t - 2 >= 0:
                    loadsA(t - 2)
                merge(stageB(t, it), stageA(t - 1) if t - 1 >= 0 else None, pattern="BBABABABBAB")
            k.barrier()


def run_gen(g):
    for _ in g:
        pass


def merge(gb, ga, pattern=None):
    if pattern is not None and ga is not None:
        live = {"B": gb, "A": ga}
        for ch in pattern:
            g = live.get(ch)
            if g is None:
                continue
            try:
                next(g)
            except StopIteration:
                live[ch] = None
        gens = [g for g in (live["B"], live["A"]) if g is not None]
    else:
        gens = [g for g in (gb, ga) if g is not None]
    while gens:
        for g in list(gens):
            try:
                next(g)
            except StopIteration:
                gens.remove(g)


_CACHE = {}


def get_prog(seqs):
    key = tuple(seqs)
    if key not in _CACHE:
        _CACHE[key] = Prog(list(seqs))
    return _CACHE[key]


WNAMES = ["norm_g", "w_in", "w_gla_gate", "b_gla_gate", "gla_onorm_g", "conv_w", "conv_b", "dt_bias", "a_log",
          "d_skip", "ssd_norm_g", "w_out", "w_ple_gate", "w_ple_proj", "ple_norm_g", "final_norm_g"]


def run(seqs, per_core, weights, flags=None):
    prog = get_prog(seqs)
    if flags is None:
        flags = [1.0] * len(per_core)
    in_maps = []
    for core in per_core:
        m = {n: np.ascontiguousarray(weights[n], dtype=np.float32) for n in WNAMES}
        for i, (x, p) in enumerate(core):
            m[f"x{i}"] = np.ascontiguousarray(x, dtype=np.float32)
            m[f"p{i}"] = np.ascontiguousarray(p, dtype=np.float32)
        m["flg"] = np.asarray([flags[len(in_maps)]], dtype=np.float32)
        in_maps.append(m)
    res = run_bass_kernel_spmd(prog.nc, in_maps, core_ids=list(range(len(per_core))))
    return [[r[f"y{i}"] for i in range(len(seqs))] for r in res.results]


def kernel(x_prompt, x_sample, p_prompt, p_sample, **weights):
    x_prompt = np.asarray(x_prompt); x_sample = np.asarray(x_sample)
    p_prompt = np.asarray(p_prompt); p_sample = np.asarray(p_sample)
    Lp, Ls = x_prompt.shape[1], x_sample.shape[1]
    nb = x_sample.shape[0]
    assert nb * Ls == Lp and Ls == BLK * 128
    zx = np.zeros((Lp, D), np.float32)
    zp = np.zeros((2, Lp, 256), np.float32)
    xs = x_sample.reshape(nb * Ls, D)
    ps = p_sample.reshape(2, nb * Ls, 256)
    per_core, flags = [], []
    for cidx in range(8):
        if cidx == 0:
            per_core.append([(x_prompt[0], p_prompt[:, 0])]); flags.append(1.0)
        elif cidx == 4:
            per_core.append([(x_prompt[1], p_prompt[:, 1])]); flags.append(1.0)
        elif cidx == 2:
            per_core.append([(xs, ps)]); flags.append(0.0)
        else:
            per_core.append([(zx, zp)]); flags.append(0.0)
    outs = run([Lp], per_core, weights, flags)
    y_prompt = np.stack([outs[0][0], outs[4][0]], axis=0).astype(np.float32)
    y_sample = outs[2][0].reshape(nb, Ls, D).astype(np.float32)
    return (y_prompt, y_sample)
```

```python
import numpy as np
from contextlib import ExitStack
import concourse.bass as bass
import concourse.mybir as mybir
from concourse.bass_utils import run_bass_kernel_spmd

F32 = mybir.dt.float32
BF16 = mybir.dt.bfloat16
ALU = mybir.AluOpType
AF = mybir.ActivationFunctionType

D = 1024
NIN = 6208
EPS = 1e-6
TAU = 16.0
QSCALE = 128 ** -0.5
DEBUG = False
PHASES = "12b"
BLK = 16
STOP = 0
VAR = 0


class Buf:
    __slots__ = ("name", "lw", "rd", "dsem", "excl")

    def __init__(self, name):
        self.name = name
        self.excl = False
        self.lw = None
        self.rd = {}
        self.dsem = None


class DSem:
    def __init__(self, name, h):
        self.name, self.h, self.cnt = name, h, 0


class V:
    def __init__(self, ap, bufs):
        self.ap, self.bufs = ap, tuple(bufs)

    def r(self, pat, **kw):
        return V(self.ap.rearrange(pat, **kw), self.bufs)

    def __getitem__(self, key):
        return V(self.ap[key], self.bufs)

    def bc(self, axis, shape):
        return V(self.ap.unsqueeze(axis).broadcast_to(list(shape)), self.bufs)


class Tile:
    def __init__(self, t, name, ncols, seg):
        self.t, self.name = t, name
        self.ncols = ncols
        self.seg = seg or ncols
        n = (ncols + self.seg - 1) // self.seg
        self.bufs = [Buf(f"{name}.{i}") for i in range(n)]

    def __call__(self, c0=None, c1=None, p0=None, p1=None):
        ncols = self.ncols
        a = 0 if c0 is None else c0
        b = ncols if c1 is None else c1
        bufs = self.bufs[a // self.seg:(b - 1) // self.seg + 1]
        if p0 is None:
            ap = self.t[:, a:b]
        else:
            ap = self.t[p0:p1, a:b]
        return V(ap, bufs)


class K:
    def __init__(self, nc, st):
        self.nc, self.st = nc, st
        self.E = {"pe": nc.tensor, "act": nc.scalar, "dve": nc.vector, "pool": nc.gpsimd, "sp": nc.sync}
        self.sem = {k: st.enter_context(nc.semaphore("s_" + k)) for k in self.E}
        self.cnt = {k: 0 for k in self.E}
        self.waited = {k: {} for k in self.E}
        self.dpool = [DSem(f"d{i}", st.enter_context(nc.semaphore(f"d{i}"))) for i in range(56)]
        self.dfree = list(self.dpool)
        self.swpool = [DSem(f"w{i}", st.enter_context(nc.semaphore(f"w{i}"))) for i in range(6)]
        self.swfree = list(self.swpool)
        self.ring = [DSem(f"r{i}", st.enter_context(nc.semaphore(f"r{i}"))) for i in range(4)]
        self.uid = 0
        self.dummy = self.sb(st, "dummy", [128, 8], F32)

    def sb(self, ph, name, shape, dt, seg=None):
        self.uid += 1
        t = ph.enter_context(self.nc.sbuf_tensor(f"{name}_{self.uid}", list(shape), dt))
        ncols = int(np.prod(shape[1:]))
        return Tile(t, name, ncols, seg)

    def ps(self, ph, name, shape, dt, seg=None):
        self.uid += 1
        t = ph.enter_context(self.nc.psum_tensor(f"{name}_{self.uid}", list(shape), dt))
        tl = Tile(t, name, shape[1], seg)
        for b in tl.bufs:
            b.excl = True
        return tl

    def _sync(self, e, reads, writes):
        deps = {}

        def add(ev):
            if ev is not None:
                (sn, h), v = ev
                if deps.get(sn, (None, 0))[1] < v:
                    deps[sn] = (h, v)

        for b in reads:
            add(b.lw)
        for b in writes:
            add(b.lw)
            for ev in b.rd.values():
                add(ev)
        w = self.waited[e]
        for sn, (h, v) in deps.items():
            if e == "pe" and sn == "s_pe":
                continue
            if w.get(sn, 0) < v:
                self.E[e].wait_ge(h, v)
                w[sn] = v

    def _mark(self, ev, reads, writes):
        for b in writes:
            b.lw = ev
            b.rd = {}
        for b in reads:
            b.rd[ev[0][0]] = ev

    def op(self, e, fn, outs, ins):
        writes = [b for v in outs for b in v.bufs]
        reads = [b for v in ins for b in v.bufs]
        writes += [b for b in reads if b.excl]
        reads = [b for b in reads if not b.excl]
        self._sync(e, reads, writes)
        r = fn(self.E[e])
        last = r[-1] if isinstance(r, (list, tuple)) else r
        self.cnt[e] += 1
        last.then_inc(self.sem[e], 1)
        ev = (("s_" + e, self.sem[e]), self.cnt[e])
        self._mark(ev, reads, writes)

    def dma(self, q, out, in_, sembuf=None):
        writes = list(out.bufs)
        reads = list(in_.bufs)
        self._sync(q, reads, writes)
        sb_ = sembuf if sembuf is not None else out.bufs[0]
        if sb_.dsem is None:
            sb_.dsem = self.swfree.pop() if q == "pool" else self.dfree.pop()
            if DEBUG:
                print("DSEM", sb_.name, sb_.dsem.name, q)
        ds = sb_.dsem
        ds.cnt += 16
        self.E[q].dma_start(out=out.ap, in_=in_.ap).then_inc(ds.h, 16)
        ev = ((ds.name, ds.h), ds.cnt)
        self._mark(ev, reads, writes)

    def dma_multi(self, q, items, outv):
        writes = list(outv.bufs)
        self._sync(q, [], writes)
        w = self.waited[q]
        for idx, (o, i) in enumerate(items):
            ds = self.ring[idx % len(self.ring)]
            if w.get(ds.name, 0) < ds.cnt:
                self.E[q].wait_ge(ds.h, ds.cnt)
                w[ds.name] = ds.cnt
            ds.cnt += 16
            self.E[q].dma_start(out=o, in_=i).then_inc(ds.h, 16)
        for ds in self.ring:
            if w.get(ds.name, 0) < ds.cnt:
                self.E[q].wait_ge(ds.h, ds.cnt)
                w[ds.name] = ds.cnt
        self.cnt[q] += 1
        self.E[q].memset(self.dummy.t[:, :], 0.0).then_inc(self.sem[q], 1)
        self._mark((("s_" + q, self.sem[q]), self.cnt[q]), [], writes)

    def barrier(self, release=()):
        for e in self.E:
            w = self.waited[e]
            for e2 in self.E:
                if e2 != e and w.get("s_" + e2, 0) < self.cnt[e2]:
                    self.E[e].wait_ge(self.sem[e2], self.cnt[e2])
                    w["s_" + e2] = self.cnt[e2]
            for ds in self.dpool + self.ring + self.swpool:
                if ds.cnt and w.get(ds.name, 0) < ds.cnt:
                    self.E[e].wait_ge(ds.h, ds.cnt)
                    w[ds.name] = ds.cnt
        self.dfree = list(self.dpool)
        self.swfree = list(self.swpool)


def tt(k, e, out, a, b, op):
    k.op(e, lambda g: g.tensor_tensor(out=out.ap, in0=a.ap, in1=b.ap, op=op), [out], [a, b])


def ts(k, e, out, a, s1, s2, op0, op1=None):
    ins = [a] + [s for s in (s1, s2) if isinstance(s, V)]
    a1 = s1.ap if isinstance(s1, V) else s1
    a2 = s2.ap if isinstance(s2, V) else s2
    if op1 is None:
        k.op(e, lambda g: g.tensor_scalar(out=out.ap, in0=a.ap, scalar1=a1, scalar2=None, op0=op0), [out], ins)
    else:
        k.op(e, lambda g: g.tensor_scalar(out=out.ap, in0=a.ap, scalar1=a1, scalar2=a2, op0=op0, op1=op1), [out], ins)


def stt(k, out, a, s, b, op0, op1):
    ins = [a, b] + ([s] if isinstance(s, V) else [])
    sa = s.ap if isinstance(s, V) else s
    k.op("dve", lambda g: g.scalar_tensor_tensor(out=out.ap, in0=a.ap, scalar=sa, in1=b.ap, op0=op0, op1=op1),
         [out], ins)


def act(k, out, a, func, scale=1.0, bias=0.0, accum=None):
    ins = [a] + ([scale] if isinstance(scale, V) else []) + ([bias] if isinstance(bias, V) else [])
    sc = scale.ap if isinstance(scale, V) else scale
    bi = bias.ap if isinstance(bias, V) else bias
    outs = [out] + ([accum] if accum is not None else [])
    if accum is None:
        k.op("act", lambda g: g.activation(out=out.ap, in_=a.ap, func=func, bias=bi, scale=sc), outs, ins)
    else:
        k.op("act", lambda g: g.activation(out=out.ap, in_=a.ap, func=func, bias=bi, scale=sc,
                                           accum_out=accum.ap), outs, ins)


def recip(k, out, a):
    k.op("dve", lambda g: g.reciprocal(out=out.ap, in_=a.ap), [out], [a])


def cp(k, e, out, a):
    if e == "act":
        k.op("act", lambda g: g.copy(out=out.ap, in_=a.ap), [out], [a])
    else:
        k.op(e, lambda g: g.tensor_copy(out=out.ap, in_=a.ap), [out], [a])


def mset(k, e, out, val):
    k.op(e, lambda g: g.memset(out.ap, val), [out], [])


def mms(k, items, outs, ins):
    def fn(pe):
        r = None
        for (o, l, rr, s0, s1) in items:
            r = pe.matmul(o, l, rr, start=s0, stop=s1)
        return r
    k.op("pe", fn, outs, ins)


def trs(k, items, outs, ins):
    def fn(pe):
        r = None
        for (o, i, idn) in items:
            r = pe.transpose(o, i, idn)
        return r
    k.op("pe", fn, outs, ins)


class Prog:
    def __init__(self, seqs):
        self.seqs = seqs
        self.nc = nc = bass.Bass("TRN2", target_bir_lowering=False)
        dt_in = lambda n, s: nc.dram_tensor(n, list(s), F32, kind="ExternalInput").ap()
        self.xin, self.pin, self.yout = [], [], []
        for i, L in enumerate(seqs):
            self.xin.append(dt_in(f"x{i}", [L, D]))
            self.pin.append(dt_in(f"p{i}", [2, L, 256]))
            self.yout.append(nc.dram_tensor(f"y{i}", [L, D], F32, kind="ExternalOutput").ap())
        self.flg_in = dt_in("flg", [1])
        self.w = {
            "norm_g": dt_in("norm_g", [2, D]), "w_in": dt_in("w_in", [2, D, NIN]),
            "w_gla_gate": dt_in("w_gla_gate", [2, 2, 16, 512]), "b_gla_gate": dt_in("b_gla_gate", [2, 2, 512]),
            "gla_onorm_g": dt_in("gla_onorm_g", [2, 256]), "conv_w": dt_in("conv_w", [2, 5, 2048]),
            "conv_b": dt_in("conv_b", [2, 2048]), "dt_bias": dt_in("dt_bias", [2, 2, 16]),
            "a_log": dt_in("a_log", [2, 2, 16]), "d_skip": dt_in("d_skip", [2, 16]),
            "ssd_norm_g": dt_in("ssd_norm_g", [2, D]), "w_out": dt_in("w_out", [2, 2048, D]),
            "w_ple_gate": dt_in("w_ple_gate", [2, D, D]), "w_ple_proj": dt_in("w_ple_proj", [2, 256, D]),
            "ple_norm_g": dt_in("ple_norm_g", [2, D]), "final_norm_g": dt_in("final_norm_g", [D]),
        }
        T = max(seqs) // 128
        self.Tmax = T
        scr = lambda n, s, d: nc.dram_tensor(n, list(s), d, kind="Internal").ap()
        self.s = {
            "hs": scr("s_hs", [T, 128, D], F32),
            "qeb": scr("s_qeb", [T, 128, 512], BF16), "ktb": scr("s_ktb", [T, 128, 512], BF16),
            "v": scr("s_v", [T, 128, 1024], BF16), "sg": scr("s_sg", [T, 128, 1024], BF16),
            "op": scr("s_op", [T, 128, 1024], F32), "elb": scr("s_elb", [T, 128, 4], F32),
            "c": scr("s_c", [T, 128, 512], BF16), "btm": scr("s_btm", [T, 128, 512], BF16),
            "xtb": scr("s_xtb", [T, 128, 1024], BF16), "sz": scr("s_sz", [T, 128, 1024], BF16),
            "yp": scr("s_yp", [T, 128, 1024], F32), "sm": scr("s_sm", [T, 128, 32], F32),
        }
        self.sbuf_ = {n: [Buf(f"dram_{n}_{t}") for t in range(T)] for n in self.s}
        self.wbuf = Buf("weights")

        with ExitStack() as st:
            self.k = K(nc, st)
            for si, L in enumerate(seqs):
                for layer in range(2):
                    if "1" in PHASES:
                        self.phase_f1(si, L, layer)
                    if "2" in PHASES:
                        self.phase_f2(si, L, layer)
                    if "b" in PHASES:
                        self.phase_b(si, L, layer)
            self.k.barrier()

    def scr(self, name, t):
        return V(self.s[name][t], [self.sbuf_[name][t]])

    def wv(self, ap):
        return V(ap, [self.wbuf])

    def hsrc(self, si, layer, t):
        if layer == 0:
            return V(self.xin[si][t * 128:(t + 1) * 128, :], [self.wbuf])
        return self.scr("hs", t)

    def consts(self, ph, need):
        k = self.k
        c = {}
        ones = k.sb(ph, "ones", [128, 128], F32)
        mset(k, "pool", ones(), 1.0)
        c["ones"] = ones

        def affine(name, pattern, cm, cmp):
            t = k.sb(ph, name, [128, 128], F32)
            k.op("pool", lambda g: g.affine_select(out=t.t[:, :], in_=ones.t[:, :], pattern=pattern, compare_op=cmp,
                                                   fill=0.0, base=0, channel_multiplier=cm), [t()], [ones()])
            return t

        identf = affine("identf", [[-1, 128]], 1, ALU.is_equal)
        c["identf"] = identf
        ident = k.sb(ph, "ident", [128, 128], BF16)
        cp(k, "pool", ident(), identf())
        c["ident"] = ident
        if "masks" in need:
            c["LE"] = affine("mLE", [[1, 128]], -1, ALU.is_ge)
            c["GT"] = affine("mGT", [[-1, 128]], 1, ALU.is_gt)
            c["GE"] = affine("mGE", [[-1, 128]], 1, ALU.is_ge)
        onesr = k.sb(ph, "onesr", [1, 128], BF16)
        mset(k, "pool", onesr(), 1.0)
        c["onesr"] = onesr
        return c

    def load_bcast(self, ph, name, ap1d, n):
        t = self.k.sb(ph, name, [128, n], F32)
        self.k.dma("sp", t(), self.wv(ap1d.partition_broadcast(128)))
        return t

    def load_w(self, dst, src_ap, rows, ncols):
        k = self.k
        items = []
        for kc in range(rows // 128):
            for c0 in range(0, ncols, 1024):
                c1 = min(ncols, c0 + 1024)
                items.append((dst.t[:, kc * ncols + c0:kc * ncols + c1], src_ap[kc * 128:(kc + 1) * 128, c0:c1]))
        k.dma_multi("pool", items, dst())

    def front_load(self, fe, si, layer, t):
        self.k.dma("sp", fe["x"][t % 2](), self.hsrc(si, layer, t))

    def front(self, fe, si, layer, t, uT, off):
        k = self.k
        x = fe["x"][t % 2]
        ss = fe["ss"][t % 2]
        act(k, fe["junk"](), x(), AF.Square, accum=ss(0, 1))
        act(k, ss(1, 2), ss(0, 1), AF.Ln, scale=1.0 / D, bias=fe["eps"](0, 1))
        act(k, ss(2, 3), ss(1, 2), AF.Exp, scale=-0.5)
        u = fe["u"]
        stt(k, u(), x(), ss(2, 3), fe["g"](), ALU.mult, ALU.mult)
        pT = fe["pT"]
        idn = fe["ident"]
        trs(k, [(pT(kc * 128, (kc + 1) * 128).ap, u(kc * 128, (kc + 1) * 128).ap, idn().ap) for kc in range(8)],
            [pT(0, 1024)], [u(), idn()])
        cp(k, "act", uT().r("p (a b) -> p a b", a=8)[:, :, off:off + 128],
           pT(0, 1024).r("p (a b) -> p a b", a=8))

    def front_bufs(self, ph, c, layer, pT):
        k = self.k
        fe = {"x": [k.sb(ph, f"x{i}", [128, D], F32) for i in range(2)],
              "ss": [k.sb(ph, f"ss{i}", [128, 4], F32) for i in range(2)],
              "junk": k.sb(ph, "junk", [128, D], BF16), "u": k.sb(ph, "u", [128, D], BF16),
              "g": self.load_bcast(ph, "normg", self.w["norm_g"][layer], D),
              "pT": pT, "ident": c["ident"], "uTw": None}
        eps = k.sb(ph, "eps", [128, 1], F32)
        mset(k, "pool", eps(), EPS)
        fe["eps"] = eps
        return fe

    def phase_f1(self, si, L, layer):
        k = self.k
        NT = L // 128
        w = self.w
        with ExitStack() as ph:
            c = self.consts(ph, {"masks"})
            W = k.sb(ph, "W1", [128, 8 * 3104], BF16)
            self.load_w(W, w["w_in"][layer][:, 0:3104], D, 3104)
            Wv = lambda kc, c0, c1: W(kc * 3104 + c0, kc * 3104 + c1)
            wg = k.sb(ph, "wg", [32, 1024], BF16)
            mset(k, "pool", wg(), 0.0)
            k.dma("pool", wg(0, 512, 0, 16), self.wv(w["w_gla_gate"][layer, 0]))
            k.dma("pool", wg(512, 1024, 16, 32), self.wv(w["w_gla_gate"][layer, 1]))
            bgr = k.sb(ph, "bgr", [1, 1024], BF16)
            k.dma("pool", bgr(), self.wv(w["b_gla_gate"][layer].rearrange("(o a) b -> o (a b)", o=1)))
            gon = self.load_bcast(ph, "gon", w["gla_onorm_g"][layer], 256)
            flg = self.load_bcast(ph, "flg", self.flg_in, 1)
            PT = k.ps(ph, "PT", [128, 2048], BF16, seg=1024)
            PA = k.ps(ph, "PA", [128, 1024], F32, seg=512)
            PB = k.ps(ph, "PB", [128, 1024], F32, seg=512)
            PC = k.ps(ph, "PC", [128, 1024], F32, seg=512)
            fe = self.front_bufs(ph, c, layer, PT)
            uT = [k.sb(ph, f"uT{i}", [128, 8 * 128], BF16) for i in range(2)]
            qk = [k.sb(ph, f"qk{i}", [128, 1024], F32, seg=512) for i in range(2)]
            vtm_ = [k.sb(ph, f"vtm{i}", [128, 1024], BF16, seg=512) for i in range(2)]
            lrT_ = [k.sb(ph, f"lrT{i}", [32, 128], BF16) for i in range(2)]
            sg = k.sb(ph, "sg", [128, 1024], BF16)
            tmpA = k.sb(ph, "tmpA", [128, 1024], F32, seg=512)
            tmp = k.sb(ph, "tmpf", [128, 1024], F32)
            Lsp = k.sb(ph, "Lsp", [128, 1024], F32)
            cum = k.sb(ph, "cum", [128, 1024], F32, seg=512)
            rmask = k.sb(ph, "rmask", [128, 512], F32)
            mset(k, "pool", rmask(), 1.0)
            for hd in range(4):
                mset(k, "pool", rmask(hd * 128, hd * 128 + 1), 0.0)
            Ep = [k.sb(ph, f"Ep{d}", [128, 512], F32) for d in range(2)]
            Em = [k.sb(ph, f"Em{d}", [128, 512], F32) for d in range(2)]
            qe = [k.sb(ph, f"qe{d}", [128, 512], BF16) for d in range(2)]
            ke = [k.sb(ph, f"ke{d}", [128, 512], BF16) for d in range(2)]
            kt = [k.sb(ph, f"kt{d}", [128, 512], BF16) for d in range(2)]
            kttm = [k.sb(ph, f"kttm{d}", [128, 512], BF16) for d in range(2)]
            elb = k.sb(ph, "elb", [128, 4], F32)
            att = k.sb(ph, "att", [128, 1024], BF16, seg=512)
            opart = k.sb(ph, "opart", [128, 1024], F32)
            S32 = k.sb(ph, "S32", [128, 1024], F32)
            Sbf = k.sb(ph, "Sbf", [128, 1024], BF16)
            mset(k, "pool", S32(), 0.0)
            mset(k, "pool", Sbf(), 0.0)

            def stageA(t):
                u = uT[t % 2]
                if t + 1 < NT:
                    self.front_load(fe, si, layer, t + 1)
                self.front(fe, si, layer, t, u, 0)
                yield
                u3 = lambda kc: u(kc * 128, (kc + 1) * 128)
                for qi, P in ((0, PA(0, 512)), (1, PA(512, 1024))):
                    items = []
                    for hd in range(4):
                        for kc in range(8):
                            items.append((P.ap[:, hd * 128:(hd + 1) * 128],
                                          Wv(kc, qi * 512 + hd * 128, qi * 512 + (hd + 1) * 128).ap,
                                          u3(kc).ap, kc == 0, kc == 7))
                    mms(k, items, [P], [W(), u()])
                    cp(k, "act", qk[t % 2](qi * 512, (qi + 1) * 512), P)
                    yield
                mms(k, [(PA(0, 128, 0, 32).ap, Wv(kc, 3072, 3104).ap, u3(kc).ap, kc == 0, kc == 7) for kc in range(8)],
                    [PA(0, 512)], [W(), u()])
                cp(k, "act", lrT_[t % 2](), PA(0, 128, 0, 32))
                vtm = vtm_[t % 2]
                for half in range(2):
                    P = PA((1 - half) * 512, (2 - half) * 512)
                    mms(k, [(P.ap, u3(kc).ap, Wv(kc, 1024 + half * 512, 1024 + (half + 1) * 512).ap, kc == 0, kc == 7)
                            for kc in range(8)], [P], [W(), u()])
                    cp(k, "act", vtm(half * 512, (half + 1) * 512), P)
                k.dma("sp", self.scr("v", t), vtm(), sembuf=vtm.bufs[0])
                yield
                for half in range(2):
                    P = PA((1 - half) * 512, (2 - half) * 512)
                    mms(k, [(P.ap, u3(kc).ap, Wv(kc, 2048 + half * 512, 2048 + (half + 1) * 512).ap, kc == 0, kc == 7)
                            for kc in range(8)], [P], [W(), u()])
                    th = tmpA(half * 512, (half + 1) * 512)
                    act(k, th, P, AF.Exp, scale=-1.0)
                    act(k, th, th, AF.Ln, bias=1.0)
                    act(k, th, th, AF.Exp, scale=-1.0)
                    tt(k, "dve", th, P, th, ALU.mult)
                yield
                tt(k, "dve", sg().r("p (h d) -> p h d", h=4), tmpA().r("p (h d) -> p h d", h=4),
                   gon().bc(1, [128, 4, 256]), ALU.mult)
                k.dma("sp", self.scr("sg", t), sg(), sembuf=sg.bufs[0])

            def stageB(t):
                vtm = vtm_[t % 2]
                lrT = lrT_[t % 2]
                if t % BLK == 0 and t > 0:
                    ts(k, "dve", S32(), S32(), flg(0, 1), None, ALU.mult)
                    cp(k, "act", Sbf(), S32())
                qs = qk[t % 2](0, 512)
                ks_ = qk[t % 2](512, 1024)
                items = []
                for d in range(2):
                    for hd in range(4):
                        o = PB.t[:, (d * 4 + hd) * 128:(d * 4 + hd + 1) * 128]
                        items.append((o, wg.t[0:32, d * 512 + hd * 128:d * 512 + (hd + 1) * 128], lrT.t[0:32, :], True, False))
                        items.append((o, bgr.t[0:1, d * 512 + hd * 128:d * 512 + (hd + 1) * 128], c["onesr"].t[0:1, :], False, True))
                mms(k, items, [PB()], [wg(), lrT(), bgr(), c["onesr"]()])
                act(k, Lsp(), PB(), AF.Exp, scale=-1.0)
                act(k, Lsp(), Lsp(), AF.Ln, bias=1.0)
                yield
                for d in range(2):
                    k.op("dve", lambda g, d=d: g.tensor_tensor_scan(
                        out=cum.t[:, d * 512:(d + 1) * 512], data0=rmask.t[:, :], data1=Lsp.t[:, d * 512:(d + 1) * 512],
                        initial=0.0, op0=ALU.mult, op1=ALU.add), [cum(d * 512, (d + 1) * 512)], [rmask(), Lsp()])
                tt(k, "dve", tmp(0, 512), cum(512, 1024), Lsp(512, 1024), ALU.subtract)
                for hd in range(4):
                    ts(k, "dve", tmp(512 + hd * 128, 512 + (hd + 1) * 128), tmp(hd * 128, (hd + 1) * 128), -1.0,
                       cum(512 + hd * 128 + 127, 512 + hd * 128 + 128), ALU.mult, ALU.add)
                yield
                srcs = [cum(0, 512), tmp(512, 1024)]
                for d in range(2):
                    act(k, Ep[d](), srcs[d], AF.Exp, scale=-1.0 / TAU)
                    act(k, Em[d](), srcs[d], AF.Exp, scale=1.0 / TAU)
                for d in range(2):
                    stt(k, qe[d](), qs, QSCALE, Ep[d](), ALU.mult, ALU.mult)
                    tt(k, "dve", ke[d](), ks_, Em[d](), ALU.mult)
                    for hd in range(4):
                        col = hd * 128 + (127 if d == 0 else 0)
                        ts(k, "dve", kt[d](hd * 128, (hd + 1) * 128), ke[d](hd * 128, (hd + 1) * 128),
                           Ep[d](col, col + 1), None, ALU.mult)
                    yield
                cp(k, "pool", elb().r("p (h o) -> p h o", h=4), Ep[1]().r("p (h t) -> p h t", h=4)[:, :, 0:1])
                k.dma("sp", self.scr("elb", t), elb(), sembuf=elb.bufs[0])
                k.dma("sp", self.scr("qeb", t), qe[1](), sembuf=qe[1].bufs[0])
                PT1 = PT(1024, 2048)
                trs(k, [(PT.t[:, 1024 + (d * 4 + hd) * 128:1024 + (d * 4 + hd + 1) * 128],
                         kt[d].t[:, hd * 128:(hd + 1) * 128], c["ident"].t[:, :]) for d in range(2) for hd in range(4)],
                    [PT1], [kt[0](), kt[1](), c["ident"]()])
                cp(k, "act", kttm[0](), PT(1024, 1536))
                cp(k, "act", kttm[1](), PT(1536, 2048))
                k.dma("sp", self.scr("ktb", t), kttm[1](), sembuf=kttm[1].bufs[0])
                yield
                items = []
                for d in range(2):
                    for hd in range(4):
                        items.append((PB.t[:, (d * 4 + hd) * 128:(d * 4 + hd + 1) * 128],
                                      ke[d].t[:, hd * 128:(hd + 1) * 128], qe[d].t[:, hd * 128:(hd + 1) * 128], True, True))
                mms(k, items, [PB()], [ke[0](), ke[1](), qe[0](), qe[1]()])
                for d, mk in ((0, c["LE"]), (1, c["GT"])):
                    tt(k, "dve", att(d * 512, (d + 1) * 512).r("p (h t) -> p h t", h=4),
                       PB(d * 512, (d + 1) * 512).r("p (h t) -> p h t", h=4), mk().bc(1, [128, 4, 128]), ALU.mult)
                yield
                for half in range(2):
                    items = []
                    for hd in (2 * half, 2 * half + 1):
                        o = PC.t[:, hd * 256:(hd + 1) * 256]
                        vv = vtm.t[:, hd * 256:(hd + 1) * 256]
                        items.append((o, att.t[:, hd * 128:(hd + 1) * 128], vv, True, False))
                        items.append((o, att.t[:, 512 + hd * 128:512 + (hd + 1) * 128], vv, False, False))
                        items.append((o, qe[0].t[:, hd * 128:(hd + 1) * 128], Sbf.t[:, hd * 256:(hd + 1) * 256], False, True))
                    mms(k, items, [PC(half * 512, (half + 1) * 512)], [att(), vtm(), qe[0](), Sbf()])
                cp(k, "act", opart(), PC())
                k.dma("sp", self.scr("op", t), opart(), sembuf=opart.bufs[0])
                yield
                for half in range(2):
                    items = []
                    for hd in (2 * half, 2 * half + 1):
                        items.append((PC.t[:, hd * 256:(hd + 1) * 256], kttm[0].t[:, hd * 128:(hd + 1) * 128],
                                      vtm.t[:, hd * 256:(hd + 1) * 256], True, True))
                    mms(k, items, [PC(half * 512, (half + 1) * 512)], [kttm[0](), vtm()])
                for hd in range(4):
                    stt(k, S32(hd * 256, (hd + 1) * 256), S32(hd * 256, (hd + 1) * 256),
                        Ep[0](hd * 128 + 127, hd * 128 + 128), PC(hd * 256, (hd + 1) * 256), ALU.mult, ALU.add)
                cp(k, "act", Sbf(), S32())

            self.front_load(fe, si, layer, 0)
            run_gen(stageA(0))
            for t in range(NT):
                merge(stageB(t), stageA(t + 1) if t + 1 < NT else None)
            k.barrier()

    def phase_f2(self, si, L, layer):
        k = self.k
        NT = L // 128
        w = self.w
        with ExitStack() as ph:
            c = self.consts(ph, {"masks"})
            W = k.sb(ph, "W2", [128, 8 * 3104], BF16)
            self.load_w(W, w["w_in"][layer][:, 3104:6208], D, 3104)
            PT = k.ps(ph, "PT", [128, 2048], BF16, seg=1024)
            PA = k.ps(ph, "PA", [128, 1024], F32, seg=512)
            PB = k.ps(ph, "PB", [128, 1024], F32, seg=512)
            PC = k.ps(ph, "PC", [128, 1024], F32, seg=512)
            cwr = k.sb(ph, "cwr", [80, 128], F32)
            k.dma("sp", cwr(), self.wv(w["conv_w"][layer].rearrange("j (b p) -> (j b) p", p=128)))
            trs(k, [(PC.t[:, 0:80], cwr.t[0:80, :], c["identf"].t[0:80, 0:80])], [PC(0, 512)], [cwr(), c["identf"]()])
            cw = k.sb(ph, "cw", [128, 80], F32)
            cp(k, "act", cw(), PC(0, 80))
            diag = k.sb(ph, "diag", [128, 80 * 128], BF16)
            for i in range(80):
                ts(k, "dve", diag(i * 128, (i + 1) * 128), c["identf"](), cw(i, i + 1), None, ALU.mult)
            cbr = k.sb(ph, "cbr", [1, 2048], BF16)
            k.dma("pool", cbr(0, 1024), self.wv(w["conv_b"][layer:layer + 1, 0:1024]))
            k.dma("pool", cbr(1024, 2048), self.wv(w["conv_b"][layer:layer + 1, 1024:2048]))
            lexp = k.sb(ph, "lexp", [128, 2048], BF16)
            sel = k.sb(ph, "sel", [32, 32 * 128], BF16)
            mset(k, "pool", lexp(), 1.0)
            for hf in range(2):
                k.op("pool", lambda g, hf=hf: g.affine_select(
                    out=sel.t[:, hf * 2048:(hf + 1) * 2048].rearrange("p (a b) -> p a b", a=16),
                    in_=lexp.t[0:32, :].rearrange("p (a b) -> p a b", a=16),
                    pattern=[[-1, 16], [0, 128]], compare_op=ALU.is_equal, fill=0.0, base=-16 * hf,
                    channel_multiplier=1), [sel()], [lexp()])
            dtb = self.load_bcast(ph, "dtb", w["dt_bias"][layer].rearrange("a b -> (a b)"), 32)
            Abc = self.load_bcast(ph, "Abc", w["a_log"][layer].rearrange("a b -> (a b)"), 32)
            act(k, Abc(), Abc(), AF.Exp)
            ts(k, "dve", Abc(), Abc(), -1.0, None, ALU.mult)
            Dbc = self.load_bcast(ph, "Dbc", w["d_skip"][layer], 16)
            flg = self.load_bcast(ph, "flg", self.flg_in, 1)
            fe = self.front_bufs(ph, c, layer, PT)
            uT = [k.sb(ph, f"uT{i}", [128, 8 * 132], BF16) for i in range(2)]
            tmpA = k.sb(ph, "tmpA", [128, 1024], F32, seg=512)
            tmp = k.sb(ph, "tmpf", [128, 1024], F32)
            sz = k.sb(ph, "sz", [128, 1024], BF16)
            raw = k.sb(ph, "raw", [128, 16 * 132], BF16, seg=3 * 132)
            xc_ = [k.sb(ph, f"xc{i}", [128, 1024], BF16) for i in range(2)]
            bcf_ = [k.sb(ph, f"bcf{i}", [128, 1024], BF16) for i in range(2)]
            draw_ = [k.sb(ph, f"draw{i}", [128, 32], F32) for i in range(2)]
            sm = k.sb(ph, "sm", [128, 32], F32)
            dts_ = [k.sb(ph, f"dts{i}", [128, 32 * 8], F32, seg=32) for i in range(2)]
            dtaf_ = [k.sb(ph, f"dtaf{i}", [128, 64], F32) for i in range(2)]
            for i in range(2):
                mset(k, "pool", dtaf_[i](), 0.0)
            etot_ = [k.sb(ph, f"etot{i}", [128, 32], F32) for i in range(2)]
            cumfm = k.sb(ph, "cumfm", [32, 128], F32)
            chi_ = [k.sb(ph, f"chi{i}", [32, 128], BF16) for i in range(2)]
            clo_ = [k.sb(ph, f"clo{i}", [32, 128], BF16) for i in range(2)]
            xdt = [k.sb(ph, f"xdt{d}", [128, 1024], BF16) for d in range(2)]
            xtl = [k.sb(ph, f"xtl{d}", [128, 1024], BF16) for d in range(2)]
            yD = k.sb(ph, "yD", [128, 1024], F32)
            btm = k.sb(ph, "btm", [128, 512], BF16)
            cbm = [k.sb(ph, f"cbm{d}", [128, 512], BF16) for d in range(2)]
            seg = k.sb(ph, "seg", [128, 2048], F32)
            WT = [k.sb(ph, f"WT{d}", [128, 2048], BF16) for d in range(2)]
            h32 = k.sb(ph, "h32", [128, 1024], F32)
            hbf = k.sb(ph, "hbf", [128, 1024], BF16)
            mset(k, "pool", h32(), 0.0)
            mset(k, "pool", hbf(), 0.0)
            for i in range(2):
                mset(k, "pool", uT[i](), 0.0)

            self.front_load(fe, si, layer, 0)
            if NT > 1:
                self.front_load(fe, si, layer, 1)
            self.front(fe, si, layer, 0, uT[0], 2)
            if NT > 1:
                self.front(fe, si, layer, 1, uT[1], 2)
                if NT > 2:
                    self.front_load(fe, si, layer, 2)

            def stageA(t):
                u = uT[t % 2]
                un = uT[(t + 1) % 2]
                u3 = u().r("p (a b) -> p a b", a=8)
                un3 = un().r("p (a b) -> p a b", a=8)
                dts = dts_[t % 2]
                _, d_dt, d_dta, d_cum, d_tot, d_ecum, d_tail, d_dtt = [dts(i * 32, (i + 1) * 32) for i in range(8)]
                dtaf, etot, chi, clo = dtaf_[t % 2], etot_[t % 2], chi_[t % 2], clo_[t % 2]
                if t + 1 < NT:
                    if (t + 1) % BLK == 0:
                        ts(k, "pool", u3[:, :, 130:132], un3[:, :, 2:4], flg(0, 1), None, ALU.mult)
                        ts(k, "pool", un3[:, :, 0:2], u3[:, :, 128:130], flg(0, 1), None, ALU.mult)
                    else:
                        cp(k, "pool", u3[:, :, 130:132], un3[:, :, 2:4])
                        cp(k, "pool", un3[:, :, 0:2], u3[:, :, 128:130])
                else:
                    k.op("pool", lambda g: g.memset(u3.ap[:, :, 130:132], 0.0), [u()], [])
                yield
                ui = lambda kc: u.t[:, kc * 132 + 2:kc * 132 + 130]
                ue = lambda kc: u.t[:, kc * 132:(kc + 1) * 132]
                xc, bcf, d_raw = xc_[t % 2], bcf_[t % 2], draw_[t % 2]
                groups = [(0, 1, 2), (3, 4, 5), (6, 7, 8), (9, 10, 11), (12, 13, 14), (15,)]
                for gi, blks in enumerate(groups):
                    base = (gi % 2) * 512
                    items = []
                    for bi, blk in enumerate(blks):
                        o = PA.t[:, base + bi * 132:base + (bi + 1) * 132]
                        for kc in range(8):
                            items.append((o, W.t[:, kc * 3104 + 1024 + blk * 128:kc * 3104 + 1024 + (blk + 1) * 128],
                                          ue(kc), kc == 0, kc == 7))
                    if gi == 5:
                        for kc in range(8):
                            items.append((PA.t[:, base + 132:base + 164], ui(kc), W.t[:, kc * 3104 + 3072:kc * 3104 + 3104],
                                          kc == 0, kc == 7))
                    mms(k, items, [PA(base, base + 512)], [W(), u()])
                    if gi == 5:
                        tt(k, "dve", d_raw(), PA(base + 132, base + 164), dtb(), ALU.add)
                    cp(k, "act", raw(blks[0] * 132, (blks[-1] + 1) * 132), PA(base, base + len(blks) * 132))
                    if gi % 2 == 1:
                        yield
                for half in range(2):
                    P = PA(half * 512, (half + 1) * 512)
                    mms(k, [(P.ap, ui(kc), W.t[:, kc * 3104 + half * 512:kc * 3104 + (half + 1) * 512], kc == 0, kc == 7)
                            for kc in range(8)], [P], [W(), u()])
                    th = tmpA(half * 512, (half + 1) * 512)
                    act(k, th, P, AF.Exp, scale=-1.0)
                    act(k, th, th, AF.Ln, bias=1.0)
                    act(k, th, th, AF.Exp, scale=-1.0)
                    tt(k, "dve", sz(half * 512, (half + 1) * 512), P, th, ALU.mult)
                k.dma("sp", self.scr("sz", t), sz(), sembuf=sz.bufs[0])
                yield
                for r in range(4):
                    P = PA((r % 2) * 512, (r % 2 + 1) * 512)
                    dst = xc if r < 2 else bcf
                    items = []
                    for b in range(4):
                        blk = r * 4 + b
                        o = PA.t[:, (r % 2) * 512 + b * 128:(r % 2) * 512 + (b + 1) * 128]
                        for j in range(5):
                            items.append((o, diag.t[:, (j * 16 + blk) * 128:(j * 16 + blk + 1) * 128],
                                          raw.t[:, blk * 132 + j:blk * 132 + j + 128], j == 0, False))
                        items.append((o, cbr.t[0:1, blk * 128:(blk + 1) * 128], c["onesr"].t[0:1, :], False, True))
                    mms(k, items, [P], [diag(), raw(), cbr(), c["onesr"]()])
                    th = tmpA((r % 2) * 512, (r % 2 + 1) * 512)
                    act(k, th, P, AF.Exp, scale=-1.0)
                    act(k, th, th, AF.Ln, bias=1.0)
                    act(k, th, th, AF.Exp, scale=-1.0)
                    tt(k, "dve", dst((r % 2) * 512, (r % 2 + 1) * 512), P, th, ALU.mult)
                    if r % 2 == 1:
                        yield
                k.dma("sp", self.scr("c", t), bcf(512, 1024), sembuf=bcf.bufs[0])
                act(k, d_dt, d_raw(), AF.Exp)
                act(k, d_dt, d_dt, AF.Ln, bias=1.0)
                tt(k, "dve", d_dta, d_dt, Abc(), ALU.mult)
                cp(k, "dve", dtaf(0, 16), d_dta[:, 0:16])
                cp(k, "dve", dtaf(48, 64), d_dta[:, 16:32])
                mms(k, [(PA.t[:, 0:16], c["LE"].t[:, :], dts.t[:, 64:80], True, True),
                        (PA.t[:, 16:32], c["GE"].t[:, :], dts.t[:, 80:96], True, True),
                        (PA.t[:, 32:64], c["ones"].t[:, :], dts.t[:, 64:96], True, True),
                        (PA.t[0:32, 64:192], dtaf.t[:, 0:32], c["LE"].t[:, :], True, False),
                        (PA.t[0:32, 64:192], dtaf.t[:, 32:64], c["GE"].t[:, :], False, True)],
                    [PA(0, 512)], [c["LE"](), c["GE"](), c["ones"](), d_dta, dtaf()])
                cp(k, "act", d_cum, PA(0, 32))
                cp(k, "act", d_tot, PA(32, 64))
                cp(k, "act", cumfm(), PA(64, 192, 0, 32))
                cp(k, "act", chi(), cumfm())
                tt(k, "dve", clo(), cumfm(), chi(), ALU.subtract)
                act(k, d_ecum, d_cum, AF.Exp)
                tt(k, "dve", d_tail, d_tot, d_cum, ALU.subtract)
                act(k, d_tail, d_tail, AF.Exp)
                tt(k, "dve", d_dtt, d_dt, d_tail, ALU.mult)
                act(k, etot(), d_tot, AF.Exp)
                cp(k, "pool", sm(0, 16), d_ecum[:, 16:32])
                cp(k, "pool", sm(16, 32), etot(16, 32))
                k.dma("sp", self.scr("sm", t), sm(), sembuf=sm.bufs[0])
                yield
                if t + 2 < NT:
                    self.front(fe, si, layer, t + 2, u, 2)
                    if t + 3 < NT:
                        self.front_load(fe, si, layer, t + 3)

            def stageB(t):
                xc, bcf, d_raw = xc_[t % 2], bcf_[t % 2], draw_[t % 2]
                dts = dts_[t % 2]
                _, d_dt, d_dta, d_cum, d_tot, d_ecum, d_tail, d_dtt = [dts(i * 32, (i + 1) * 32) for i in range(8)]
                dtaf, etot, chi, clo = dtaf_[t % 2], etot_[t % 2], chi_[t % 2], clo_[t % 2]
                if t % BLK == 0 and t > 0:
                    ts(k, "dve", h32(), h32(), flg(0, 1), None, ALU.mult)
                    cp(k, "act", hbf(), h32())
                PT1 = PT(1024, 2048)
                trs(k, [(PT.t[:, 1024 + b * 128:1024 + (b + 1) * 128], xc.t[:, b * 128:(b + 1) * 128], c["ident"].t[:, :])
                        for b in range(8)], [PT1], [xc(), c["ident"]()])
                x3 = PT1.r("p (h d) -> p h d", h=16)
                for d in range(2):
                    tt(k, "dve", xdt[d]().r("p (h d) -> p h d", h=16), x3,
                       d_dt[:, d * 16:(d + 1) * 16].bc(2, [128, 16, 64]), ALU.mult)
                    tt(k, "dve", xtl[d]().r("p (h d) -> p h d", h=16), x3,
                       d_dtt[:, d * 16:(d + 1) * 16].bc(2, [128, 16, 64]), ALU.mult)
                    yield
                tt(k, "dve", yD().r("p (h d) -> p h d", h=16), x3, Dbc().bc(2, [128, 16, 64]), ALU.mult)
                k.dma("sp", self.scr("xtb", t), xtl[1](), sembuf=xtl[1].bufs[0])
                trs(k, [(PT.t[:, 1024 + g * 128:1024 + (g + 1) * 128], bcf.t[:, g * 128:(g + 1) * 128], c["ident"].t[:, :])
                        for g in range(4)], [PT1], [bcf(), c["ident"]()])
                cp(k, "act", btm(), PT(1024, 1536))
                k.dma("sp", self.scr("btm", t), btm(), sembuf=btm.bufs[0])
                mms(k, [(PC.t[:, 512 + g * 128:512 + (g + 1) * 128], bcf.t[:, g * 128:(g + 1) * 128],
                         bcf.t[:, (4 + g) * 128:(5 + g) * 128], True, True) for g in range(4)],
                    [PC(512, 1024)], [bcf()])
                for d, mk in ((0, c["LE"]), (1, c["GT"])):
                    tt(k, "dve", cbm[d]().r("p (g t) -> p g t", g=4), PC(512, 1024).r("p (g t) -> p g t", g=4),
                       mk().bc(1, [128, 4, 128]), ALU.mult)
                yield
                for d in range(2):
                    for r in range(4):
                        base = (r % 2) * 512
                        items = []
                        for hh in range(4):
                            h = r * 4 + hh
                            o = PB.t[:, base + hh * 128:base + (hh + 1) * 128]
                            s_ = sel.t[0:32, (d * 16 + h) * 128:(d * 16 + h + 1) * 128]
                            items.append((o, s_, chi.t[0:32, :], True, False))
                            items.append((o, s_, clo.t[0:32, :], False, True))
                        mms(k, items, [PB(base, base + 512)], [sel(), chi(), clo()])
                        for hh in range(4):
                            h = r * 4 + hh
                            ts(k, "dve", seg(h * 128, (h + 1) * 128), PB(base + hh * 128, base + (hh + 1) * 128),
                               d_cum[:, d * 16 + h:d * 16 + h + 1], 0.0, ALU.subtract, ALU.min)
                        if r % 2 == 1:
                            yield
                    act(k, lexp(), seg(), AF.Exp)
                    for g in range(4):
                        tt(k, "dve", WT[d](g * 512, (g + 1) * 512).r("p (h t) -> p h t", h=4),
                           lexp(g * 512, (g + 1) * 512).r("p (h t) -> p h t", h=4),
                           cbm[d](g * 128, (g + 1) * 128).bc(1, [128, 4, 128]), ALU.mult)
                    yield
                for half in range(2):
                    items = []
                    for h in range(half * 8, half * 8 + 8):
                        o = PC.t[:, h * 64:(h + 1) * 64]
                        items.append((o, WT[0].t[:, h * 128:(h + 1) * 128], xdt[0].t[:, h * 64:(h + 1) * 64], True, False))
                        items.append((o, WT[1].t[:, h * 128:(h + 1) * 128], xdt[1].t[:, h * 64:(h + 1) * 64], False, True))
                    mms(k, items, [PC(half * 512, (half + 1) * 512)], [WT[0](), WT[1](), xdt[0](), xdt[1]()])
                for half in range(2):
                    items = []
                    for g in (2 * half, 2 * half + 1):
                        items.append((PB.t[:, g * 256:(g + 1) * 256], bcf.t[:, (4 + g) * 128:(5 + g) * 128],
                                      hbf.t[:, g * 256:(g + 1) * 256], True, True))
                    mms(k, items, [PB(half * 512, (half + 1) * 512)], [bcf(), hbf()])
                tt(k, "dve", yD(), PC(), yD(), ALU.add)
                tt(k, "dve", tmp().r("p (h d) -> p h d", h=16), PB().r("p (h d) -> p h d", h=16),
                   d_ecum[:, 0:16].bc(2, [128, 16, 64]), ALU.mult)
                tt(k, "dve", yD(), yD(), tmp(), ALU.add)
                k.dma("sp", self.scr("yp", t), yD(), sembuf=yD.bufs[0])
                yield
                for half in range(2):
                    items = []
                    for g in (2 * half, 2 * half + 1):
                        items.append((PC.t[:, g * 256:(g + 1) * 256], btm.t[:, g * 128:(g + 1) * 128],
                                      xtl[0].t[:, g * 256:(g + 1) * 256], True, True))
                    mms(k, items, [PC(half * 512, (half + 1) * 512)], [btm(), xtl[0]()])
                tt(k, "dve", h32().r("p (h d) -> p h d", h=16), h32().r("p (h d) -> p h d", h=16),
                   etot(0, 16).bc(2, [128, 16, 64]), ALU.mult)
                tt(k, "dve", h32(), h32(), PC(), ALU.add)
                cp(k, "act", hbf(), h32())

            run_gen(stageA(0))
            for t in range(NT):
                merge(stageB(t), stageA(t + 1) if t + 1 < NT else None, pattern="BABABBABABBABABABAB")
            k.barrier()

    def phase_b(self, si, L, layer):
        k = self.k
        NT = L // 128
        w = self.w
        with ExitStack() as ph:
            c = self.consts(ph, set())
            Wo = k.sb(ph, "Wo", [128, 16 * 1024], BF16)
            self.load_w(Wo, w["w_out"][layer], 2048, 1024)
            Wg = k.sb(ph, "Wpg", [128, 8 * 1024], BF16)
            self.load_w(Wg, w["w_ple_gate"][layer], 1024, 1024)
            Wp = k.sb(ph, "Wpp", [128, 2 * 1024], BF16)
            self.load_w(Wp, w["w_ple_proj"][layer], 256, 1024)
            gssd = self.load_bcast(ph, "gssd", w["ssd_norm_g"][layer], D)
            gple = self.load_bcast(ph, "gple", w["ple_norm_g"][layer], D)
            gfin = self.load_bcast(ph, "gfin", w["final_norm_g"], D) if layer == 1 else None
            eps = k.sb(ph, "eps", [128, 1], F32)
            mset(k, "pool", eps(), EPS)
            flg = self.load_bcast(ph, "flg", self.flg_in, 1)
            PT = k.ps(ph, "PT", [128, 2048], BF16, seg=1024)
            PA = k.ps(ph, "PA", [128, 1024], F32, seg=512)
            PB = k.ps(ph, "PB", [128, 1024], F32, seg=512)
            PC = k.ps(ph, "PC", [128, 1024], F32, seg=512)
            namesA = [("qeb", 512, BF16), ("ktb", 512, BF16), ("v", 1024, BF16), ("op", 1024, F32), ("elb", 4, F32),
                      ("c", 512, BF16), ("btm", 512, BF16), ("xtb", 1024, BF16), ("yp", 1024, F32), ("sm", 32, F32)]
            namesB = [("sg", 1024, BF16), ("sz", 1024, BF16)]
            ldA = [{n: k.sb(ph, f"l_{n}{i}", [128, wd], dt) for n, wd, dt in namesA} for i in range(2)]
            ldB = [{n: k.sb(ph, f"l_{n}{i}", [128, wd], dt) for n, wd, dt in namesB} for i in range(2)]
            for i in range(2):
                ldB[i]["h"] = k.sb(ph, f"l_h{i}", [128, D], F32)
                ldB[i]["p"] = k.sb(ph, f"l_p{i}", [128, 256], F32)
            Sb32 = k.sb(ph, "Sb32", [128, 1024], F32)
            Sbbf = k.sb(ph, "Sbbf", [128, 1024], BF16)
            hb32 = k.sb(ph, "hb32", [128, 1024], F32)
            hbbf = k.sb(ph, "hbbf", [128, 1024], BF16)
            for x_ in (Sb32, Sbbf, hb32, hbbf):
                mset(k, "pool", x_(), 0.0)
            o_ = [k.sb(ph, f"o{i}", [128, 1024], F32) for i in range(2)]
            y_ = [k.sb(ph, f"y{i}", [128, 1024], F32) for i in range(2)]
            tmpA = k.sb(ph, "tmpA", [128, 1024], F32)
            junk = k.sb(ph, "junkb", [128, 1024], BF16)
            st = k.sb(ph, "stat", [128, 32], F32, seg=4)
            mix = k.sb(ph, "mix", [128, 2048], BF16, seg=1024)
            mixT = k.sb(ph, "mixT", [128, 2048], BF16, seg=1024)
            hmid = k.sb(ph, "hmid", [128, 1024], F32)
            hmbf = k.sb(ph, "hmbf", [128, 1024], BF16)
            hmT = k.sb(ph, "hmT", [128, 1024], BF16)
            pbf = k.sb(ph, "pbf", [128, 256], BF16)
            pT = k.sb(ph, "pT", [128, 256], BF16)
            eg = k.sb(ph, "eg", [128, 1024], F32, seg=512)
            en = k.sb(ph, "en", [128, 1024], F32)
            hnew = [k.sb(ph, f"hnew{i}", [128, 1024], F32) for i in range(2)]
            yo = [k.sb(ph, f"yo{i}", [128, 1024], F32) for i in range(2)] if layer == 1 else None

            def loadsA(t):
                for n, wd, dt in namesA:
                    k.dma("sp", ldA[t % 2][n](), self.scr(n, t))

            def loadsB(t):
                b = ldB[t % 2]
                for n, wd, dt in namesB:
                    k.dma("sp", b[n](), self.scr(n, t))
                k.dma("sp", b["h"](), self.hsrc(si, layer, t))
                k.dma("sp", b["p"](), self.wv(self.pin[si][layer, t * 128:(t + 1) * 128, :]))

            def rstd(dst, src, n):
                act(k, dst, src, AF.Ln, scale=1.0 / n, bias=eps(0, 1))
                act(k, dst, dst, AF.Exp, scale=-0.5)

            def stageA(t):
                b = ldA[t % 2]
                o, y = o_[t % 2], y_[t % 2]
                if t % BLK == BLK - 1 and t < NT - 1:
                    ts(k, "dve", Sb32(), Sb32(), flg(0, 1), None, ALU.mult)
                    cp(k, "act", Sbbf(), Sb32())
                    ts(k, "dve", hb32(), hb32(), flg(0, 1), None, ALU.mult)
                    cp(k, "act", hbbf(), hb32())
                for half in range(2):
                    mms(k, [(PA.t[:, hd * 256:(hd + 1) * 256], b["qeb"].t[:, hd * 128:(hd + 1) * 128],
                             Sbbf.t[:, hd * 256:(hd + 1) * 256], True, True) for hd in (2 * half, 2 * half + 1)],
                        [PA(half * 512, (half + 1) * 512)], [b["qeb"](), Sbbf()])
                tt(k, "dve", o(), PA(), b["op"](), ALU.add)
                yield
                for half in range(2):
                    mms(k, [(PA.t[:, hd * 256:(hd + 1) * 256], b["ktb"].t[:, hd * 128:(hd + 1) * 128],
                             b["v"].t[:, hd * 256:(hd + 1) * 256], True, True) for hd in (2 * half, 2 * half + 1)],
                        [PA(half * 512, (half + 1) * 512)], [b["ktb"](), b["v"]()])
                for hd in range(4):
                    stt(k, Sb32(hd * 256, (hd + 1) * 256), Sb32(hd * 256, (hd + 1) * 256), b["elb"](hd, hd + 1),
                        PA(hd * 256, (hd + 1) * 256), ALU.mult, ALU.add)
                cp(k, "act", Sbbf(), Sb32())
                yield
                for half in range(2):
                    mms(k, [(PA.t[:, g * 256:(g + 1) * 256], b["c"].t[:, g * 128:(g + 1) * 128],
                             hbbf.t[:, g * 256:(g + 1) * 256], True, True) for g in (2 * half, 2 * half + 1)],
                        [PA(half * 512, (half + 1) * 512)], [b["c"](), hbbf()])
                tt(k, "dve", tmpA().r("p (h d) -> p h d", h=16), PA().r("p (h d) -> p h d", h=16),
                   b["sm"](0, 16).bc(2, [128, 16, 64]), ALU.mult)
                tt(k, "dve", y(), tmpA(), b["yp"](), ALU.add)
                yield
                for half in range(2):
                    mms(k, [(PA.t[:, g * 256:(g + 1) * 256], b["btm"].t[:, g * 128:(g + 1) * 128],
                             b["xtb"].t[:, g * 256:(g + 1) * 256], True, True) for g in (2 * half, 2 * half + 1)],
                        [PA(half * 512, (half + 1) * 512)], [b["btm"](), b["xtb"]()])
                tt(k, "dve", hb32().r("p (h d) -> p h d", h=16), hb32().r("p (h d) -> p h d", h=16),
                   b["sm"](16, 32).bc(2, [128, 16, 64]), ALU.mult)
                tt(k, "dve", hb32(), hb32(), PA(), ALU.add)
                cp(k, "act", hbbf(), hb32())

            def stageB(t, it):
                b = ldB[t % 2]
                o, y = o_[t % 2], y_[t % 2]
                for hd in range(4):
                    act(k, junk(0, 256), o(hd * 256, (hd + 1) * 256), AF.Square, accum=st(hd, hd + 1))
                rstd(st(4, 8), st(0, 4), 256)
                for hd in range(4):
                    stt(k, mix(hd * 256, (hd + 1) * 256), o(hd * 256, (hd + 1) * 256), st(4 + hd, 5 + hd),
                        b["sg"](hd * 256, (hd + 1) * 256), ALU.mult, ALU.mult)
                yield
                tt(k, "dve", y(), y(), b["sz"](), ALU.mult)
                act(k, junk(), y(), AF.Square, accum=st(8, 9))
                rstd(st(12, 13), st(8, 9), 1024)
                stt(k, mix(1024, 2048), y(), st(12, 13), gssd(), ALU.mult, ALU.mult)
                yield
                for half in range(2):
                    trs(k, [(PT.t[:, cc * 128:(cc + 1) * 128], mix.t[:, cc * 128:(cc + 1) * 128], c["ident"].t[:, :])
                            for cc in range(half * 8, half * 8 + 8)],
                        [PT(half * 1024, (half + 1) * 1024)], [mix(half * 1024, (half + 1) * 1024), c["ident"]()])
                    cp(k, "act" if half == 0 else "dve", mixT(half * 1024, (half + 1) * 1024),
                       PT(half * 1024, (half + 1) * 1024))
                yield
                for half in range(2):
                    mms(k, [(PB.t[:, half * 512:(half + 1) * 512], mixT.t[:, cc * 128:(cc + 1) * 128],
                             Wo.t[:, cc * 1024 + half * 512:cc * 1024 + (half + 1) * 512], cc == 0, cc == 15)
                            for cc in range(16)], [PB(half * 512, (half + 1) * 512)], [mixT(), Wo()])
                tt(k, "dve", hmid(), PB(), b["h"](), ALU.add)
                cp(k, "act", hmbf(), hmid())
                cp(k, "pool", pbf(), b["p"]())
                yield
                trs(k, [(PT.t[:, cc * 128:(cc + 1) * 128], hmbf.t[:, cc * 128:(cc + 1) * 128], c["ident"].t[:, :])
                        for cc in range(8)], [PT(0, 1024)], [hmbf(), c["ident"]()])
                cp(k, "act", hmT(), PT(0, 1024))
                trs(k, [(PT.t[:, 1024 + cc * 128:1024 + (cc + 1) * 128], pbf.t[:, cc * 128:(cc + 1) * 128], c["ident"].t[:, :])
                        for cc in range(2)], [PT(1024, 2048)], [pbf(), c["ident"]()])
                cp(k, "dve", pT(), PT(1024, 1280))
                yield
                for half in range(2):
                    mms(k, [(PC.t[:, half * 512:(half + 1) * 512], hmT.t[:, cc * 128:(cc + 1) * 128],
                             Wg.t[:, cc * 1024 + half * 512:cc * 1024 + (half + 1) * 512], cc == 0, cc == 7)
                            for cc in range(8)], [PC(half * 512, (half + 1) * 512)], [hmT(), Wg()])
                    eh = eg(half * 512, (half + 1) * 512)
                    act(k, eh, PC(half * 512, (half + 1) * 512), AF.Exp, scale=-1.0)
                    act(k, eh, eh, AF.Ln, bias=1.0)
                    act(k, eh, eh, AF.Exp, scale=-1.0)
                for half in range(2):
                    mms(k, [(PB.t[:, half * 512:(half + 1) * 512], pT.t[:, cc * 128:(cc + 1) * 128],
                             Wp.t[:, cc * 1024 + half * 512:cc * 1024 + (half + 1) * 512], cc == 0, cc == 1)
                            for cc in range(2)], [PB(half * 512, (half + 1) * 512)], [pT(), Wp()])
                act(k, junk(), PB(), AF.Square, accum=st(16, 17))
                rstd(st(20, 21), st(16, 17), 1024)
                stt(k, en(), PB(), st(20, 21), gple(), ALU.mult, ALU.mult)
                yield
                tt(k, "dve", en(), en(), eg(), ALU.mult)
                hn = hnew[it % 2]
                tt(k, "dve", hn(), hmid(), en(), ALU.add)
                if layer == 0:
                    k.dma("sp", self.scr("hs", t), hn(), sembuf=hn.bufs[0])
                else:
                    act(k, junk(), hn(), AF.Square, accum=st(24, 25))
                    rstd(st(28, 29), st(24, 25), 1024)
                    yy = yo[it % 2]
                    stt(k, yy(), hn(), st(28, 29), gfin(), ALU.mult, ALU.mult)
                    k.dma("sp", V(self.yout[si][t * 128:(t + 1) * 128, :], [Buf("yout")]), yy(), sembuf=yy.bufs[0])

            loadsA(NT - 1)
            loadsB(NT - 1)
            if NT > 1:
                loadsA(NT - 2)
            run_gen(stageA(NT - 1))
            for it, t in enumerate(range(NT - 1, -1, -1)):
                if t - 1 >= 0:
                    loadsB(t - 1)
                if t - 2 >= 0:
                    loadsA(t - 2)
                merge(stageB(t, it), stageA(t - 1) if t - 1 >= 0 else None, pattern="BBABABABBAB")
            k.barrier()


def run_gen(g):
    for _ in g:
        pass


def merge(gb, ga, pattern=None):
    if pattern is not None and ga is not None:
        live = {"B": gb, "A": ga}
        for ch in pattern:
            g = live.get(ch)
            if g is None:
                continue
            try:
                next(g)
            except StopIteration:
                live[ch] = None
        gens = [g for g in (live["B"], live["A"]) if g is not None]
    else:
        gens = [g for g in (gb, ga) if g is not None]
    while gens:
        for g in list(gens):
            try:
                next(g)
            except StopIteration:
                gens.remove(g)


_CACHE = {}


def get_prog(seqs):
    key = tuple(seqs)
    if key not in _CACHE:
        _CACHE[key] = Prog(list(seqs))
    return _CACHE[key]


WNAMES = ["norm_g", "w_in", "w_gla_gate", "b_gla_gate", "gla_onorm_g", "conv_w", "conv_b", "dt_bias", "a_log",
          "d_skip", "ssd_norm_g", "w_out", "w_ple_gate", "w_ple_proj", "ple_norm_g", "final_norm_g"]


def run(seqs, per_core, weights, flags=None):
    prog = get_prog(seqs)
    if flags is None:
        flags = [1.0] * len(per_core)
    in_maps = []
    for core in per_core:
        m = {n: np.ascontiguousarray(weights[n], dtype=np.float32) for n in WNAMES}
        for i, (x, p) in enumerate(core):
            m[f"x{i}"] = np.ascontiguousarray(x, dtype=np.float32)
            m[f"p{i}"] = np.ascontiguousarray(p, dtype=np.float32)
        m["flg"] = np.asarray([flags[len(in_maps)]], dtype=np.float32)
        in_maps.append(m)
    res = run_bass_kernel_spmd(prog.nc, in_maps, core_ids=list(range(len(per_core))))
    return [[r[f"y{i}"] for i in range(len(seqs))] for r in res.results]


def kernel(x_prompt, x_sample, p_prompt, p_sample, **weights):
    x_prompt = np.asarray(x_prompt); x_sample = np.asarray(x_sample)
    p_prompt = np.asarray(p_prompt); p_sample = np.asarray(p_sample)
    Lp, Ls = x_prompt.shape[1], x_sample.shape[1]
    nb = x_sample.shape[0]
    assert nb * Ls == Lp and Ls == BLK * 128
    zx = np.zeros((Lp, D), np.float32)
    zp = np.zeros((2, Lp, 256), np.float32)
    xs = x_sample.reshape(nb * Ls, D)
    ps = p_sample.reshape(2, nb * Ls, 256)
    per_core, flags = [], []
    for cidx in range(8):
        if cidx == 0:
            per_core.append([(x_prompt[0], p_prompt[:, 0])]); flags.append(1.0)
        elif cidx == 4:
            per_core.append([(x_prompt[1], p_prompt[:, 1])]); flags.append(1.0)
        elif cidx == 2:
            per_core.append([(xs, ps)]); flags.append(0.0)
        else:
            per_core.append([(zx, zp)]); flags.append(0.0)
    outs = run([Lp], per_core, weights, flags)
    y_prompt = np.stack([outs[0][0], outs[4][0]], axis=0).astype(np.float32)
    y_sample = outs[2][0].reshape(nb, Ls, D).astype(np.float32)
    return (y_prompt, y_sample)
```
